# Optimizing a Trainium2 kernel written in Bass

```python
import math
import jax, jax.numpy as jnp
from jax import lax
import numpy as np

D_MODEL = 2048
BATCH = 1
SEQ = 8192
DEPTH = 1

MIX_WIDTH = D_MODEL
SSM_WIDTH = MIX_WIDTH // 2
SSM_GROUP = 16
SSM_GROUPS = SSM_WIDTH // SSM_GROUP
SSM_STATE = 64
SGU_WIDTH = MIX_WIDTH - SSM_WIDTH
SGU_HEADS = 8
SGU_HEAD_DIM = SGU_WIDTH // SGU_HEADS
SGU_CHUNK = 128
IN_WIDTH = SSM_WIDTH + 2 * SGU_WIDTH
D_FF = 4 * D_MODEL
EPS = 1e-6
DT_MIN = 1e-3
DT_MAX = 1e-1

kernel_name = "hymba_style_s5_sgu_hybrid_layer"


def rmsnorm(x, g):
    xf = x.astype(jnp.float32)
    xf = xf * lax.rsqrt(jnp.mean(xf * xf, axis=-1, keepdims=True) + EPS)
    return (xf * g.astype(jnp.float32)).astype(x.dtype)


def layernorm(x, g, b):
    xf = x.astype(jnp.float32)
    mu = jnp.mean(xf, axis=-1, keepdims=True)
    var = jnp.mean(jnp.square(xf - mu), axis=-1, keepdims=True)
    y = (xf - mu) * lax.rsqrt(var + EPS) * g.astype(jnp.float32) + b.astype(jnp.float32)
    return y.astype(x.dtype)


def _scan_combine(left, right):
    a_l, b_l = left
    a_r, b_r = right
    return a_l * a_r, a_r * b_l + b_r


def s5_mixer(u, a_re, a_im, b_re, b_im, c_re, c_im, d, log_dt, glu_w, glu_b):
    dtype = u.dtype
    bsz, seq, _ = u.shape
    uf = u.astype(jnp.float32).reshape(bsz, seq, SSM_GROUPS, SSM_GROUP)
    lam = lax.complex(a_re.astype(jnp.float32), a_im.astype(jnp.float32))
    dt = jnp.exp(log_dt.astype(jnp.float32))[:, None]
    a_bar = jnp.exp(lam * dt)
    b_mat = lax.complex(b_re.astype(jnp.float32), b_im.astype(jnp.float32))
    b_bar = ((a_bar - 1.0) / lam)[..., None] * b_mat
    bu = jnp.einsum('bsgh,gph->bsgp', uf.astype(jnp.complex64), b_bar)
    a_seq = jnp.broadcast_to(a_bar, bu.shape)
    _, states = lax.associative_scan(_scan_combine, (a_seq, bu), axis=1)
    c_mat = lax.complex(c_re.astype(jnp.float32), c_im.astype(jnp.float32))
    y = jnp.einsum('bsgp,ghp->bsgh', states, c_mat).real + d.astype(jnp.float32) * uf
    y = jax.nn.gelu(y.reshape(bsz, seq, SSM_WIDTH))
    gate = jax.nn.sigmoid(y @ glu_w.astype(jnp.float32) + glu_b.astype(jnp.float32))
    return (y * gate).astype(dtype)


def sgu_mixer(z, ln_g, ln_b, w_s, b_s):
    bsz, seq, _ = z.shape
    u, v = jnp.split(jax.nn.gelu(z), 2, axis=-1)
    v = layernorm(v, ln_g, ln_b)
    v = v.reshape(bsz, seq // SGU_CHUNK, SGU_CHUNK, SGU_HEADS, SGU_HEAD_DIM)
    causal = jnp.tril(jnp.ones((SGU_CHUNK, SGU_CHUNK), dtype=bool))
    w = jnp.where(causal[None], w_s, jnp.zeros_like(w_s))
    mixed = jnp.einsum('hts,bcshd->bcthd', w, v) + b_s.T[:, :, None]
    return u * mixed.reshape(bsz, seq, SGU_WIDTH)


def setup_inputs(seed: int = 0) -> dict:
    key = jax.random.key(seed)
    ks = jax.random.split(key, 24)
    L = DEPTH
    G, P, H = SSM_GROUPS, SSM_STATE, SSM_GROUP
    f32 = jnp.float32
    nrm = lambda k, shape: jax.random.normal(k, shape, f32)
    x = nrm(ks[0], (BATCH, SEQ, D_MODEL))
    norm_mix_g = 1.0 + 0.02 * nrm(ks[1], (L, D_MODEL))
    w_in = nrm(ks[2], (L, D_MODEL, IN_WIDTH)) * D_MODEL ** -0.5
    n = jnp.arange(P, dtype=f32)
    ssm_a_re = -0.5 + 0.01 * nrm(ks[3], (L, G, P))
    ssm_a_im = math.pi * n + 0.01 * nrm(ks[4], (L, G, P))
    ssm_b_re = nrm(ks[5], (L, G, P, H)) * (2.0 * H) ** -0.5
    ssm_b_im = nrm(ks[6], (L, G, P, H)) * (2.0 * H) ** -0.5
    ssm_c_re = nrm(ks[7], (L, G, H, P)) * (2.0 * P) ** -0.5
    ssm_c_im = nrm(ks[8], (L, G, H, P)) * (2.0 * P) ** -0.5
    ssm_d = nrm(ks[9], (L, G, H))
    ssm_log_dt = jax.random.uniform(ks[10], (L, G), f32, math.log(DT_MIN), math.log(DT_MAX))
    ssm_glu_w = nrm(ks[11], (L, SSM_WIDTH, SSM_WIDTH)) * SSM_WIDTH ** -0.5
    ssm_glu_b = 0.01 * nrm(ks[12], (L, SSM_WIDTH))
    sgu_ln_g = 1.0 + 0.02 * nrm(ks[13], (L, SGU_WIDTH))
    sgu_ln_b = 0.01 * nrm(ks[14], (L, SGU_WIDTH))
    sgu_w = nrm(ks[15], (L, SGU_HEADS, SGU_CHUNK, SGU_CHUNK)) * 0.5 * SGU_CHUNK ** -0.5
    sgu_b = 1.0 + 0.01 * nrm(ks[16], (L, SGU_HEADS, SGU_CHUNK))
    out_norm_ssm_g = 1.0 + 0.02 * nrm(ks[17], (L, SSM_WIDTH))
    out_norm_sgu_g = 1.0 + 0.02 * nrm(ks[18], (L, SGU_WIDTH))
    w_out = nrm(ks[19], (L, MIX_WIDTH, D_MODEL)) * MIX_WIDTH ** -0.5
    norm_mlp_g = 1.0 + 0.02 * nrm(ks[20], (L, D_MODEL))
    w_up = nrm(ks[21], (L, D_MODEL, D_FF)) * D_MODEL ** -0.5
    w_down = nrm(ks[22], (L, D_FF, D_MODEL)) * D_FF ** -0.5
    norm_final_g = 1.0 + 0.02 * nrm(ks[23], (D_MODEL,))
    return {"x": x, "norm_mix_g": norm_mix_g, "w_in": w_in,
            "ssm_a_re": ssm_a_re, "ssm_a_im": ssm_a_im,
            "ssm_b_re": ssm_b_re, "ssm_b_im": ssm_b_im,
            "ssm_c_re": ssm_c_re, "ssm_c_im": ssm_c_im,
            "ssm_d": ssm_d, "ssm_log_dt": ssm_log_dt,
            "ssm_glu_w": ssm_glu_w, "ssm_glu_b": ssm_glu_b,
            "sgu_ln_g": sgu_ln_g, "sgu_ln_b": sgu_ln_b,
            "sgu_w": sgu_w, "sgu_b": sgu_b,
            "out_norm_ssm_g": out_norm_ssm_g, "out_norm_sgu_g": out_norm_sgu_g,
            "w_out": w_out, "norm_mlp_g": norm_mlp_g,
            "w_up": w_up, "w_down": w_down, "norm_final_g": norm_final_g}


def reference(x, norm_mix_g, w_in, ssm_a_re, ssm_a_im, ssm_b_re, ssm_b_im,
              ssm_c_re, ssm_c_im, ssm_d, ssm_log_dt, ssm_glu_w, ssm_glu_b,
              sgu_ln_g, sgu_ln_b, sgu_w, sgu_b, out_norm_ssm_g, out_norm_sgu_g,
              w_out, norm_mlp_g, w_up, w_down, norm_final_g):
    for l in range(DEPTH):
        h = rmsnorm(x, norm_mix_g[l])
        z = h @ w_in[l]
        z_ssm = z[..., :SSM_WIDTH]
        z_sgu = z[..., SSM_WIDTH:]
        y_ssm = s5_mixer(z_ssm, ssm_a_re[l], ssm_a_im[l], ssm_b_re[l], ssm_b_im[l],
                         ssm_c_re[l], ssm_c_im[l], ssm_d[l], ssm_log_dt[l],
                         ssm_glu_w[l], ssm_glu_b[l])
        y_sgu = sgu_mixer(z_sgu, sgu_ln_g[l], sgu_ln_b[l], sgu_w[l], sgu_b[l])
        mixed = jnp.concatenate([rmsnorm(y_ssm, out_norm_ssm_g[l]),
                                 rmsnorm(y_sgu, out_norm_sgu_g[l])], axis=-1)
        x = x + mixed @ w_out[l]
        h = rmsnorm(x, norm_mlp_g[l])
        x = x + jnp.square(jax.nn.relu(h @ w_up[l])) @ w_down[l]
    return rmsnorm(x, norm_final_g)
```

```python
import math
from contextlib import ExitStack
import numpy as np
import concourse.bass as bass
import concourse.mybir as mybir
from concourse.bass_utils import run_bass_kernel_spmd

F32 = mybir.dt.float32
BF16 = mybir.dt.bfloat16
I32 = mybir.dt.int32
AF = mybir.ActivationFunctionType
ALU = mybir.AluOpType
AX = mybir.AxisListType

NC = 8
T = 1024
D = 2048
KD = 16
DFF = 8192
EPS = 1e-6
TWO_PI = 2.0 * math.pi


class Tok:
    __slots__ = ("key", "sem", "val", "eng")

    def __init__(self, key, sem, val, eng):
        self.key, self.sem, self.val, self.eng = key, sem, val, eng


class Buf:
    def __init__(self, name="", excl=False):
        self.name = name
        self.excl = excl
        self.w = {}
        self.r = {}
        self.dsem = None


class _Rec:
    def __getattr__(self, name):
        def f(*a, **k):
            self.call = (name, a, k)
            return self
        return f


class Prog:
    ENGS = ["pe", "act", "dve", "pool", "sp"]

    def __init__(self, nc, stack):
        self.nc, self.stack = nc, stack
        self.ops = {e: [] for e in self.ENGS}
        self.esem = {e: stack.enter_context(nc.semaphore("es_" + e)) for e in self.ENGS}
        self.ecnt = {e: 0 for e in self.ENGS}
        self.waited = {}
        self.dcnt = {}
        self.dsems = {}
        self.nd = 0

    def barrier(self):
        toks = [Tok(f, self.esem[f], self.ecnt[f], f) for f in self.ENGS if self.ecnt[f] > 0]
        toks += [Tok(k, self.dsems[k], v, "dma") for k, v in self.dcnt.items()]
        for e in self.ENGS:
            for t in toks:
                if t.key != e:
                    self._wait(e, t)

    def _wait(self, eng, tok):
        k = (eng, tok.key)
        if self.waited.get(k, 0) >= tok.val:
            return
        self.waited[k] = tok.val
        sem, val = tok.sem, tok.val
        self.ops[eng].append(lambda e: e.wait_ge(sem, val))

    def alias(self, new, olds):
        for o in olds:
            for d in (o.w, o.r):
                for k, t in d.items():
                    if k not in new.w or new.w[k].val < t.val:
                        new.w[k] = t

    def op(self, eng, fn, reads=(), writes=(), dma=False):
        rec = _Rec()
        fn(rec)
        name_, a_, k_ = rec.call
        fn = lambda e: getattr(e, name_)(*a_, **k_)
        deps = []
        for b in reads:
            deps += list(b.w.values())
            if b.excl:
                deps += [t for t in b.r.values() if t.eng != eng or dma]
        for b in writes:
            deps += [t for t in list(b.w.values()) + list(b.r.values()) if t.eng != eng or dma]
        for t in deps:
            if eng == "pe" and t.eng == "pe" and not dma:
                continue
            self._wait(eng, t)
        if dma:
            tgt = writes[0] if writes else reads[0]
            if tgt.dsem is None:
                self.nd += 1
                tgt.dsem = ("d%d" % self.nd, self.stack.enter_context(self.nc.semaphore("ds%d" % self.nd)))
            key, sem = tgt.dsem
            self.dsems[key] = sem
            self.dcnt[key] = self.dcnt.get(key, 0) + 16
            tok = Tok(key, sem, self.dcnt[key], "dma")
            self.ops[eng].append(lambda e: fn(e).then_inc(sem, 16))
        else:
            self.ecnt[eng] += 1
            sem = self.esem[eng]
            tok = Tok(eng, sem, self.ecnt[eng], eng)
            self.ops[eng].append(lambda e: fn(e).then_inc(sem, 1))
        for b in reads:
            b.r[tok.key] = tok
        for b in writes:
            b.w[tok.key] = tok
            b.r = {}
        return tok

    def final_wait(self, eng, bufs):
        for b in bufs:
            for t in list(b.w.values()) + list(b.r.values()):
                self._wait(eng, t)

    def emit(self, block):
        m = {"pe": block.tensor, "act": block.scalar, "dve": block.vector, "pool": block.gpsimd, "sp": block.sync}
        for en in self.ENGS:
            ops = self.ops[en]

            def body(e, ops=ops):
                for f in ops:
                    f(e)

            m[en](body)


def build(mode, stage=None):
    nc = bass.Bass("TRN2", target_bir_lowering=False)
    full = True
    dbg_done = [False]

    def din(name, shape, dt=F32):
        return nc.dram_tensor(name, list(shape), dt, kind="ExternalInput").ap()

    xT_d = din("xT", [D, T])
    w_in_d = din("w_in", [D, 3072])
    cst_d = din("cst", [128, 640])
    pB_d = din("pB", [128, 3, 512])
    BT_d = din("BT", [128, 2, 512])
    pE_d = din("pE", [3, 4096])
    CT_d = din("CT", [128, 2, 512])
    cols_d = din("cols", [128, 96])
    if full:
        w_out_d = din("w_out", [D, D])
        w_up_d = din("w_up", [D, DFF])
        w_dn_d = din("w_down", [DFF, D])
        glu_d = din("glu_w", [1024, 1024])
        rows_d = din("rows", [2, 1024])
        sguw_d = din("sguwT", [128, 8, 128])
        sgub_d = din("sgub", [1, 1024])
        xTp_d = din("xTp", [8, D, T])
        pT_d = din("pT", [128, 3, 512])
        BTl_d = din("BTl", [128, 2, 512])
        out_d = nc.dram_tensor("yT", [D, T], F32, kind="ExternalOutput").ap()
    else:
        out_d = nc.dram_tensor("send", [128, 64], F32, kind="ExternalOutput").ap()

    stack = ExitStack()
    with stack:
        P = Prog(nc, stack)

        def tap(name, ap, shape, dt, bufs):
            if stage != name:
                return False
            dd = nc.dram_tensor("dbg", list(shape), dt, kind="ExternalOutput").ap()
            P.barrier()
            tok = P.op("sp", lambda e: e.dma_start(out=dd, in_=ap), reads=list(bufs), dma=True)
            P._wait("sp", tok)
            with nc.Block() as block:
                P.emit(block)
            return True

        ARENA_KB = 192
        arena = stack.enter_context(nc.sbuf_tensor("arena", [128, ARENA_KB * 512], BF16))

        def A(off_kb, nelem, dt=BF16):
            esz = 2 if dt == BF16 else 4
            a = arena[:, int(off_kb * 512): int(off_kb * 512) + nelem * esz // 2]
            return a if dt == BF16 else a.bitcast(dt)

        def sb(name, shape, dt):
            return stack.enter_context(nc.sbuf_tensor("t_" + name, list(shape), dt))

        PA = stack.enter_context(nc.psum_tensor("PA", [128, 2048], F32))
        PB = stack.enter_context(nc.psum_tensor("PB", [128, 2048], F32))
        bPA = [Buf("PA%d" % i, True) for i in range(4)]
        bPB = [Buf("PB%d" % i, True) for i in range(4)]
        banks = [(PA[:, 512 * i:512 * i + 512], bPA[i]) for i in range(4)] + \
                [(PB[:, 512 * i:512 * i + 512], bPB[i]) for i in range(4)]
        bank_rr = [0]

        def next_bank():
            b = banks[bank_rr[0] % 8]
            bank_rr[0] += 1
            return b

        cst = sb("cst", [128, 640], F32); b_cst = Buf("cst")
        ident = cst[:, 0:128]
        tri_f = cst[:, 128:256]
        iota_c = cst[:, 256:257]
        maskc = cst[:, 320:328]
        colsv = sb("colsv", [128, 96], F32); b_cols = Buf("cols")
        g_mix = colsv[:, 0:16]; g_mlp = colsv[:, 16:32]; g_fin = colsv[:, 32:48]
        g_oss = colsv[:, 48:56]; g_osg = colsv[:, 56:64]; glu_b = colsv[:, 64:72]; Dcol = colsv[:, 72:80]
        ones_bf = sb("ones_bf", [128, 128], BF16); b_ones = Buf("ones")
        tri_bf = sb("tri_bf", [128, 128], BF16); b_tri = Buf("tri")
        negiota = sb("negiota", [128, 1], F32); b_ni = Buf("ni")
        b_rstd = Buf("rstd"); b_rstd2 = Buf("rstd2"); b_sq = Buf("sq")
        a1 = sb("a1", [128, 2, 32], F32)
        a127 = sb("a127", [128, 2, 32], F32)
        a128 = sb("a128", [128, 2, 32], F32)
        b_a = Buf("apow")
        b_hl = Buf("hl"); b_hp = Buf("hp"); b_sm = Buf("sm")
        yl = sb("yl", [128, 2, 32], F32); b_yl = Buf("yl")
        H0 = sb("H0", [128, 2, 32], F32); b_H0 = Buf("H0")

        Em = A(0, 8192); EpT = A(16, 8192); Bbd = A(32, 8192); Cbd = A(48, 8192)
        Wb = A(96, 8192); XT = A(112, 8192)
        rstd = A(88, 1024, F32); sq = A(92, 1024); rstd2 = A(64, 1024, F32)
        hl = A(184, 9 * 64, F32).rearrange("p (c r a) -> p c r a", c=9, r=2)
        hp = A(186.25, 8 * 64, F32).rearrange("p (c r a) -> p c r a", c=8, r=2)
        sm = sb("sm", [128, 6, 2, 32], F32)
        hrun = sb("hrun", [128, 2, 2, 32], F32)
        b_Em, b_EpT, b_Bbd, b_Cbd, b_Wb, b_XT = [Buf(n) for n in ("Em", "EpT", "Bbd", "Cbd", "Wb", "XT")]
        hn = A(96, 16 * 1024).rearrange("p (k t) -> p k t", k=16); b_hn = [Buf("hn%d" % k) for k in range(16)]
        zs = A(128, 8192).rearrange("p (k t) -> p k t", k=8); b_zs = [Buf("zs%d" % k) for k in range(8)]
        zu = A(144, 8192).rearrange("p (k t) -> p k t", k=8); b_zu = [Buf("zu%d" % k) for k in range(8)]
        zv = A(160, 8192).rearrange("p (c d) -> p c d", c=8); b_zv = [Buf("zv%d" % k) for k in range(8)]
        ys = zs; b_ys = b_zs
        wsl = [A(o_, 2048).rearrange("p (k c) -> p k c", k=16) for o_ in (176, 180, 76)]
        b_wsl = [Buf("wsl%d" % i) for i in range(3)]
        xT0 = A(128, 16 * 1024, F32).rearrange("p (k t) -> p k t", k=16); b_x0 = [Buf("x0_%d" % k) for k in range(16)]

        P.op("sp", lambda e: e.dma_start(out=cst[:], in_=cst_d[:, :]), writes=[b_cst], dma=True)
        P.op("sp", lambda e: e.dma_start(out=colsv[:], in_=cols_d[:, :]), writes=[b_cols], dma=True)
        P.op("dve", lambda e: e.memset(ones_bf[:], 1.0), writes=[b_ones])
        P.op("dve", lambda e: e.tensor_copy(out=tri_bf[:], in_=tri_f), reads=[b_cst], writes=[b_tri])
        P.op("dve", lambda e: e.tensor_scalar(out=negiota[:], in0=iota_c, scalar1=-1.0, scalar2=None, op0=ALU.mult),
             reads=[b_cst], writes=[b_ni])

        def rmsnorm(src, b_src, gcols, dst, b_dst, nk, rs, b_rs, width):
            for k in range(nk):
                P.op("act", lambda e, k=k: e.activation(out=sq, in_=src[:, k, :], func=AF.Square),
                     reads=[b_src[k]], writes=[b_sq])
                for h in range(2):
                    P.op("pe", lambda e, k=k, h=h: e.matmul(PA[:, 512 * h:512 * h + 512], lhsT=ones_bf[:],
                                                          rhs=sq[:, 512 * h:512 * h + 512], start=(k == 0), stop=(k == nk - 1)),
                         reads=[b_sq, b_ones], writes=[bPA[h]])
            P.op("act", lambda e: e.activation(out=rs, in_=PA[:, 0:1024], func=AF.Sqrt, scale=1.0 / width, bias=eps_c[:]),
                 reads=[bPA[0], bPA[1], b_eps], writes=[b_rs])
            P.op("dve", lambda e: e.reciprocal(out=rs, in_=rs), reads=[b_rs], writes=[b_rs])
            for k in range(nk):
                P.op("dve", lambda e, k=k: e.scalar_tensor_tensor(out=dst[:, k, :], in0=src[:, k, :], scalar=gcols[:, k:k + 1],
                                                                  in1=rs, op0=ALU.mult, op1=ALU.mult),
                     reads=[b_src[k], b_rs, b_cols], writes=[b_dst[k]])

        eps_c = sb("eps_c", [128, 1], F32); b_eps = Buf("eps")
        P.op("dve", lambda e: e.memset(eps_c[:], EPS), writes=[b_eps])
        tg = [A(o_, 4096, F32) for o_ in (128, 144, 160, 176, 64, 80)]
        b_tg = [Buf("tg%d" % i) for i in range(6)]
        lr, th, ang, kf, sn, cs = tg
        b_lr, b_th, b_ang, b_kf, b_sn, b_cs = b_tg
        P.op("sp", lambda e: e.dma_start(out=lr, in_=pE_d[0:1, :].partition_broadcast(128).rearrange("p o f -> p (o f)")),
             writes=[b_lr], dma=True)
        P.op("sp", lambda e: e.dma_start(out=th, in_=pE_d[1:2, :].partition_broadcast(128).rearrange("p o f -> p (o f)")),
             writes=[b_th], dma=True)
        P.op("sp", lambda e: e.dma_start(out=ang, in_=pE_d[2:3, :].partition_broadcast(128).rearrange("p o f -> p (o f)")),
             writes=[b_ang], dma=True)
        P.op("act", lambda e: e.activation(out=ang, in_=ang, func=AF.Exp), reads=[b_ang], writes=[b_ang])
        if tap("tg_load", ang, [128, 4096], F32, [b_ang]):
            return nc
        P.op("dve", lambda e: e.tensor_tensor(out=lr, in0=lr, in1=ang, op=ALU.mult), reads=[b_lr, b_ang], writes=[b_lr])
        P.op("dve", lambda e: e.tensor_tensor(out=th, in0=th, in1=ang, op=ALU.mult), reads=[b_th, b_ang], writes=[b_th])
        P.op("dve", lambda e: e.tensor_scalar(out=ang, in0=th, scalar1=iota_c, scalar2=None, op0=ALU.mult),
             reads=[b_th, b_cst], writes=[b_ang])
        kint = sn.bitcast(I32)
        P.op("dve", lambda e: e.tensor_scalar(out=kint, in0=ang, scalar1=1.0 / TWO_PI, scalar2=None, op0=ALU.mult),
             reads=[b_ang], writes=[b_sn])
        P.op("dve", lambda e: e.tensor_copy(out=kf, in_=kint), reads=[b_sn], writes=[b_kf])
        P.op("dve", lambda e: e.scalar_tensor_tensor(out=ang, in0=kf, scalar=-TWO_PI, in1=ang, op0=ALU.mult, op1=ALU.add),
             reads=[b_kf, b_ang], writes=[b_ang])
        P.op("dve", lambda e: e.tensor_scalar(out=ang, in0=ang, scalar1=math.pi, scalar2=-math.pi, op0=ALU.min, op1=ALU.max),
             reads=[b_ang], writes=[b_ang])
        if tap("tg_ang", ang, [128, 4096], F32, [b_ang]):
            return nc
        P.op("act", lambda e: e.activation(out=sn, in_=ang, func=AF.Sin), reads=[b_ang], writes=[b_sn])
        P.op("act", lambda e: e.activation(out=cs, in_=ang, func=AF.Sin, scale=0.5), reads=[b_ang], writes=[b_cs])
        P.op("dve", lambda e: e.tensor_tensor(out=cs, in0=cs, in1=cs, op=ALU.mult), reads=[b_cs], writes=[b_cs])
        P.op("dve", lambda e: e.tensor_scalar(out=cs, in0=cs, scalar1=-2.0, scalar2=1.0, op0=ALU.mult, op1=ALU.add),
             reads=[b_cs], writes=[b_cs])
        if tap("tg_cs", cs, [128, 4096], F32, [b_cs]):
            return nc
        P.op("act", lambda e: e.activation(out=kf, in_=lr, func=AF.Exp, scale=iota_c), reads=[b_lr, b_cst], writes=[b_kf])
        if tap("tg_mag", kf, [128, 4096], F32, [b_kf]):
            return nc
        P.op("act", lambda e: e.activation(out=ang, in_=lr, func=AF.Exp, scale=negiota[:]), reads=[b_lr, b_ni, b_sn, b_cs],
             writes=[b_ang])
        Em4 = Em.rearrange("p (a r q) -> p a r q", a=32, r=2)
        v3 = lambda ap: ap.rearrange("p (a q) -> p a q", a=32)
        P.op("dve", lambda e: e.tensor_tensor(out=Em4[:, :, 0, :], in0=v3(ang), in1=v3(cs), op=ALU.mult),
             reads=[b_ang, b_cs], writes=[b_Em])
        P.op("dve", lambda e: e.scalar_tensor_tensor(out=Em4[:, :, 1, :], in0=v3(ang), scalar=-1.0, in1=v3(sn),
                                                      op0=ALU.mult, op1=ALU.mult), reads=[b_ang, b_sn], writes=[b_Em])
        P.op("dve", lambda e: e.tensor_tensor(out=th, in0=kf, in1=cs, op=ALU.mult), reads=[b_kf, b_cs, b_Em], writes=[b_th])
        P.op("dve", lambda e: e.tensor_tensor(out=ang, in0=kf, in1=sn, op=ALU.mult), reads=[b_kf, b_sn, b_Em], writes=[b_ang])
        if tap("tg_ep", ang, [128, 4096], F32, [b_ang]):
            return nc
        EpT4 = EpT.rearrange("p (a r t) -> p a r t", a=32, r=2)
        for ri, (src, bsrc) in enumerate(((th, b_th), (ang, b_ang))):
            for grp in range(2):
                for pj in range(16):
                    pair = grp * 16 + pj
                    P.op("pe", lambda e, pair=pair, pj=pj, src=src: e.transpose(
                        out=PA[:, pj * 128:(pj + 1) * 128], in_=src[:, pair * 128:(pair + 1) * 128], identity=ident),
                        reads=[bsrc, b_cst], writes=bPA)
                pav = PA.rearrange("p (a t) -> p a t", a=16)
                sl = slice(grp * 16, grp * 16 + 16)
                P.op("act", lambda e, sl=sl, ri=ri, pav=pav: e.activation(out=EpT4[:, sl, ri, :], in_=pav, func=AF.Copy),
                     reads=bPA, writes=[b_EpT])
                if stage == "EpT_noA":
                    continue
                P.op("dve", lambda e, sl=sl, ri=ri, pav=pav: e.tensor_copy(out=a1[:, ri, sl], in_=pav[:, :, 1]),
                     reads=bPA, writes=[b_a])
                P.op("dve", lambda e, sl=sl, ri=ri, pav=pav: e.tensor_copy(out=a127[:, ri, sl], in_=pav[:, :, 127]),
                     reads=bPA, writes=[b_a])
        if tap("EpT_noA", EpT, [128, 8192], BF16, [b_EpT]):
            return nc
        if tap("a1tap", a127[:, :, :], [128, 2, 32], F32, [b_a]):
            return nc
        a1r, a1i = a1[:, 0, :], a1[:, 1, :]
        a127r, a127i = a127[:, 0, :], a127[:, 1, :]
        a128r, a128i = a128[:, 0, :], a128[:, 1, :]

        def cmul(eng, outr, outi, ar, ai, br, bi, t1, t2, reads, writes):
            bt = Buf("cm")
            P.op(eng, lambda e: e.tensor_tensor(out=t1, in0=ai, in1=bi, op=ALU.mult), reads=reads, writes=[bt])
            P.op(eng, lambda e: e.tensor_tensor(out=t2, in0=ai, in1=br, op=ALU.mult), reads=reads, writes=[bt])
            P.op(eng, lambda e: e.tensor_tensor(out=outr, in0=ar, in1=br, op=ALU.mult), reads=reads + [bt], writes=writes)
            P.op(eng, lambda e: e.tensor_tensor(out=outi, in0=ar, in1=bi, op=ALU.mult), reads=reads + [bt], writes=writes)
            P.op(eng, lambda e: e.tensor_tensor(out=outr, in0=outr, in1=t1, op=ALU.subtract), reads=writes + [bt], writes=writes)
            P.op(eng, lambda e: e.tensor_tensor(out=outi, in0=outi, in1=t2, op=ALU.add), reads=writes + [bt], writes=writes)

        cmul("dve", a128r, a128i, a127r, a127i, a1r, a1i, sm[:, 0, 0, :], sm[:, 0, 1, :], [b_a], [b_a, b_sm])
        if tap("Em", Em, [128, 8192], BF16, [b_Em]):
            return nc
        if tap("EpT", EpT, [128, 8192], BF16, [b_EpT]):
            return nc
        if tap("a128", a128[:, :, :], [128, 2, 32], F32, [b_a]):
            return nc

        def bbar_calc(p_dram, B_dram):
            P.barrier()
            pB = A(128, 3 * 512, F32).rearrange("p (a f) -> p a f", a=3)
            BT = A(136, 2 * 512, F32).rearrange("p (a f) -> p a f", a=2)
            qq = A(144, 8 * 512, F32).rearrange("p (a f) -> p a f", a=8)
            b_pB = Buf("pB")
            P.op("sp", lambda e: e.dma_start(out=pB, in_=p_dram[:, :, :]), writes=[b_pB], dma=True)
            P.op("sp", lambda e: e.dma_start(out=BT, in_=B_dram[:, :, :]), writes=[b_pB], dma=True)
            are, aim, ldt = pB[:, 0, :], pB[:, 1, :], pB[:, 2, :]
            dtq, lrq, thq, magq, snq, csq, t1q, t2q = [qq[:, i, :] for i in range(8)]
            R, Wr = [b_pB], [b_pB]
            o = lambda fn: P.op("dve", fn, reads=R, writes=Wr)
            oa = lambda fn: P.op("act", fn, reads=R, writes=Wr)
            oa(lambda e: e.activation(out=dtq, in_=ldt, func=AF.Exp))
            o(lambda e: e.tensor_tensor(out=lrq, in0=are, in1=dtq, op=ALU.mult))
            o(lambda e: e.tensor_tensor(out=thq, in0=aim, in1=dtq, op=ALU.mult))
            oa(lambda e: e.activation(out=magq, in_=lrq, func=AF.Exp))
            kq = t1q.bitcast(I32)
            o(lambda e: e.tensor_scalar(out=kq, in0=thq, scalar1=1.0 / TWO_PI, scalar2=None, op0=ALU.mult))
            o(lambda e: e.tensor_copy(out=t2q, in_=kq))
            o(lambda e: e.scalar_tensor_tensor(out=thq, in0=t2q, scalar=-TWO_PI, in1=thq, op0=ALU.mult, op1=ALU.add))
            o(lambda e: e.tensor_scalar(out=thq, in0=thq, scalar1=math.pi, scalar2=-math.pi, op0=ALU.min, op1=ALU.max))
            oa(lambda e: e.activation(out=snq, in_=thq, func=AF.Sin))
            oa(lambda e: e.activation(out=csq, in_=thq, func=AF.Sin, scale=0.5))
            o(lambda e: e.tensor_tensor(out=csq, in0=csq, in1=csq, op=ALU.mult))
            o(lambda e: e.tensor_scalar(out=csq, in0=csq, scalar1=-2.0, scalar2=1.0, op0=ALU.mult, op1=ALU.add))
            o(lambda e: e.tensor_tensor(out=csq, in0=csq, in1=magq, op=ALU.mult))
            o(lambda e: e.tensor_scalar(out=csq, in0=csq, scalar1=-1.0, scalar2=None, op0=ALU.add))
            o(lambda e: e.tensor_tensor(out=snq, in0=snq, in1=magq, op=ALU.mult))
            o(lambda e: e.tensor_tensor(out=t1q, in0=are, in1=are, op=ALU.mult))
            o(lambda e: e.tensor_tensor(out=t2q, in0=aim, in1=aim, op=ALU.mult))
            o(lambda e: e.tensor_tensor(out=magq, in0=t1q, in1=t2q, op=ALU.add))
            o(lambda e: e.reciprocal(out=magq, in_=magq))
            o(lambda e: e.tensor_tensor(out=t1q, in0=csq, in1=are, op=ALU.mult))
            o(lambda e: e.tensor_tensor(out=t2q, in0=snq, in1=aim, op=ALU.mult))
            o(lambda e: e.tensor_tensor(out=lrq, in0=t1q, in1=t2q, op=ALU.add))
            o(lambda e: e.tensor_tensor(out=lrq, in0=lrq, in1=magq, op=ALU.mult))
            o(lambda e: e.tensor_tensor(out=t1q, in0=snq, in1=are, op=ALU.mult))
            o(lambda e: e.tensor_tensor(out=t2q, in0=csq, in1=aim, op=ALU.mult))
            o(lambda e: e.tensor_tensor(out=thq, in0=t1q, in1=t2q, op=ALU.subtract))
            o(lambda e: e.tensor_tensor(out=thq, in0=thq, in1=magq, op=ALU.mult))
            Br, Bi = BT[:, 0, :], BT[:, 1, :]
            o(lambda e: e.tensor_tensor(out=t1q, in0=lrq, in1=Br, op=ALU.mult))
            o(lambda e: e.tensor_tensor(out=t2q, in0=thq, in1=Bi, op=ALU.mult))
            o(lambda e: e.tensor_tensor(out=csq, in0=t1q, in1=t2q, op=ALU.subtract))
            o(lambda e: e.tensor_tensor(out=t1q, in0=lrq, in1=Bi, op=ALU.mult))
            o(lambda e: e.tensor_tensor(out=t2q, in0=thq, in1=Br, op=ALU.mult))
            o(lambda e: e.tensor_tensor(out=snq, in0=t1q, in1=t2q, op=ALU.add))

            return csq, snq, b_pB

        def gen_BC():
            csq, snq, b_pB = bbar_calc(pB_d, BT_d)
            Bbd6 = Bbd.rearrange("p (b a r g q) -> p b a r g q", b=8, a=4, r=2, g=2)
            for gl in range(8):
                pl, g2 = gl // 2, gl % 2
                for ri, src in enumerate((csq, snq)):
                    P.op("dve", lambda e, pl=pl, g2=g2, ri=ri, src=src, gl=gl: e.tensor_scalar(
                        out=Bbd6[:, :, pl, ri, g2, :], in0=src.rearrange("p (b q) -> p b q", b=8),
                        scalar1=maskc[:, gl:gl + 1], scalar2=None, op0=ALU.mult), reads=[b_pB, b_cst], writes=[b_Bbd])

            CT = A(160, 2 * 512, F32).rearrange("p (a f) -> p a f", a=2)
            b_CT = Buf("CT"); P.barrier()
            P.op("sp", lambda e: e.dma_start(out=CT, in_=CT_d[:, :, :]), writes=[b_CT], dma=True)
            P.op("pool", lambda e: e.memset(Cbd, 0.0), writes=[b_Cbd])
            Cbd5 = Cbd.rearrange("p (b a r c) -> p b a r c", b=8, a=4, r=2)
            CT5 = CT.rearrange("p r (b a h) -> p r b a h", b=8, a=4)
            for q4 in range(4):
                for ri in range(2):
                    for hf in range(2):
                        ps = slice(64 * hf, 64 * hf + 64)
                        c0 = 32 * q4 + 16 * hf
                        P.op("dve", lambda e, ps=ps, c0=c0, q4=q4, ri=ri: e.tensor_scalar(
                            out=Cbd5[ps, :, q4, ri, c0:c0 + 16], in0=CT5[ps, ri, :, q4, :],
                            scalar1=(1.0 if ri == 0 else -1.0), scalar2=None, op0=ALU.mult),
                            reads=[b_CT], writes=[b_Cbd])

            P.barrier()
        apw = sb("apw", [128, 4, 2, 32], F32); b_apw = Buf("apw")
        P.op("dve", lambda e: e.tensor_copy(out=apw[:, 0, :, :], in_=a128[:, :, :]), reads=[b_a], writes=[b_apw])
        for i in range(3):
            cmul("dve", apw[:, i + 1, 0, :], apw[:, i + 1, 1, :], apw[:, i, 0, :], apw[:, i, 1, :], apw[:, i, 0, :], apw[:, i, 1, :],
                 sm[:, 0, 0, :], sm[:, 0, 1, :], [b_apw], [b_apw, b_sm])
        a1024r, a1024i = apw[:, 3, 0, :], apw[:, 3, 1, :]
        P.op("dve", lambda e: e.memset(H0[:], 0.0), writes=[b_H0])

        csqT, snqT, b_pT = bbar_calc(pT_d, BTl_d)
        BbmR = A(64, 2048).rearrange("p (a r c) -> p a r c", a=32, r=2)
        BbmI = A(68, 2048).rearrange("p (a r c) -> p a r c", a=32, r=2)
        b_Bbm = Buf("Bbm")
        P.op("pool", lambda e: e.memset(A(64, 4096), 0.0), writes=[b_Bbm])
        for hf in range(2):
            ps_ = slice(64 * hf, 64 * hf + 64)
            cs_ = slice(16 * hf, 16 * hf + 16)
            br_ = csqT.rearrange("p (a h) -> p a h", a=32)
            bi_ = snqT.rearrange("p (a h) -> p a h", a=32)
            P.op("dve", lambda e: e.tensor_copy(out=BbmR[ps_, :, 0, cs_], in_=br_[ps_]), reads=[b_pT], writes=[b_Bbm])
            P.op("dve", lambda e: e.tensor_scalar(out=BbmR[ps_, :, 1, cs_], in0=bi_[ps_], scalar1=-1.0, scalar2=None, op0=ALU.mult),
                 reads=[b_pT], writes=[b_Bbm])
            P.op("dve", lambda e: e.tensor_copy(out=BbmI[ps_, :, 0, cs_], in_=bi_[ps_]), reads=[b_pT], writes=[b_Bbm])
            P.op("dve", lambda e: e.tensor_copy(out=BbmI[ps_, :, 1, cs_], in_=br_[ps_]), reads=[b_pT], writes=[b_Bbm])
        P.barrier()
        Wssm = A(32, 16 * 1024).rearrange("p (k c) -> p k c", k=16); b_Wssm = Buf("Wssm")
        w_in_v = w_in_d.rearrange("(k p) c -> p k c", p=128)
        for m in range(8):
            P.op("pool", lambda e, m=m: e.dma_start(out=Wssm[:, :, 128 * m:128 * m + 128], in_=w_in_v[:, :, 128 * m:128 * m + 128]),
                 writes=[b_Wssm], dma=True)
        for k in range(16):
            P.op("dve", lambda e, k=k: e.tensor_scalar(out=Wssm[:, k, :], in0=Wssm[:, k, :], scalar1=g_mix[:, k:k + 1], scalar2=None, op0=ALU.mult),
                 reads=[b_Wssm, b_cols], writes=[b_Wssm])
        AW = A(92, 512, F32).rearrange("p (c r a) -> p c r a", c=8, r=2); b_AW = Buf("AW")
        P.op("dve", lambda e: e.tensor_copy(out=AW[:, 7, :, :], in_=a127[:, :, :]), reads=[b_a], writes=[b_AW])
        for c in range(6, -1, -1):
            cmul("dve", AW[:, c, 0, :], AW[:, c, 1, :], a128r, a128i, AW[:, c + 1, 0, :], AW[:, c + 1, 1, :], sm[:, 0, 0, :], sm[:, 0, 1, :],
                 [b_a, b_AW], [b_AW, b_sm])
        xb = [A(128 + 32 * i, 16 * 1024).rearrange("p (k t) -> p k t", k=16) for i in range(2)]
        b_xb = [[Buf("xb%d_%d" % (i, k)) for k in range(16)] for i in range(2)]
        sqb = A(96, 16 * 1024).rearrange("p (k t) -> p k t", k=16); b_sqb = [Buf("sqb%d" % k) for k in range(16)]
        tVs = [A(84, 1024), A(86, 1024)]; b_tVs = [Buf("tV0"), Buf("tV1")]
        Vb = [A(80, 1024), A(82, 1024)]; b_Vb = [Buf("Vb0"), Buf("Vb1")]
        cmt = A(88, 4 * 256, F32).rearrange("p (j c a) -> p j c a", j=4, c=8); b_cmt = Buf("cmt")
        ylall = sb("ylall", [128, 8, 2, 32], F32); b_ylall = Buf("ylall")
        rc = sb("rc", [128, 8], F32); b_rc = Buf("rc")
        Em3 = Em.rearrange("p (j q) -> p j q", j=64)
        PBv4 = PB.rearrange("p (a r c) -> p a r c", a=32, r=2)
        ones_col = ones_bf[:, 0:1]

        def load_prev(sh):
            i = sh % 2
            for k in range(16):
                P.op("pool", lambda e, k=k: e.dma_start(out=xb[i][:, k, :], in_=xTp_d[sh][k * 128:(k + 1) * 128, :]),
                     writes=[b_xb[i][k]], dma=True)

        xh = A(128, 16 * 512, F32).rearrange("p (k t) -> p k t", k=16)
        ztq = A(72, 4096).rearrange("p (c f) -> p c f", c=4); b_ztq = [Buf("ztq%d" % i) for i in range(4)]
        PBq = [PB[:, 1024 * i:1024 * i + 1024].rearrange("p (a r c f) -> p a r c f", a=4, r=2, c=4) for i in range(2)]
        rtmp = sb("rtmp", [128, 2, 8, 4], F32); b_rtmp = [Buf("rtmp0"), Buf("rtmp1")]

        def vprime_half(sh, half):
            for tg8 in range(8):
                st = tg8 % 2
                bks = [bPB[2 * st], bPB[2 * st + 1]]
                for pl in range(4):
                    pr = 4 * tg8 + pl
                    for ri in range(2):
                        j = 2 * pr + ri
                        P.op("pe", lambda e: e.matmul(PBq[st][:, pl, ri, :, :], lhsT=Em3[:, j, :], rhs=ztq[:, :, 32 * pr:32 * pr + 32],
                                                      start=True, stop=True), reads=[b_Em] + b_ztq, writes=[bks[pl // 2]])
                vb, bvb = Vb[st], b_Vb[st]
                P.op("act", lambda e: e.activation(out=vb, in_=PB[:, 1024 * st:1024 * st + 1024], func=AF.Copy), reads=bks, writes=[bvb])
                vb4 = vb.rearrange("p (q c f) -> p q c f", q=8, c=4)
                for ri, Bm in enumerate((BbmR, BbmI)):
                    tv, btv = tVs[ri], b_tVs[ri]
                    tv4 = tv.rearrange("p (q c f) -> p q c f", q=8, c=4)
                    bm = Bm[:, 4 * tg8:4 * tg8 + 4, :, :].rearrange("p a r f -> p (a r) f").unsqueeze(2).broadcast_to([128, 8, 4, 32])
                    P.op("dve", lambda e: e.tensor_tensor(out=tv4, in0=vb4, in1=bm, op=ALU.mult), reads=[bvb, b_Bbm], writes=[btv])
                    P.op("dve", lambda e: e.tensor_reduce(out=rtmp[:, ri, :, :], in_=tv4, axis=AX.X, op=ALU.add),
                         reads=[btv], writes=[b_rtmp[ri]])
                    r5 = rtmp[:, ri, :, :].rearrange("p (a r) c -> p a r c", r=2)
                    P.op("dve", lambda e: e.tensor_tensor(out=ylall[:, 4 * half:4 * half + 4, ri, 4 * tg8:4 * tg8 + 4].rearrange("p c a -> p a c"),
                                                          in0=r5[:, :, 0, :], in1=r5[:, :, 1, :], op=ALU.add),
                         reads=[b_rtmp[ri]], writes=[b_ylall])

        load_prev(0)
        for sh in range(8):
            own = (sh == 7)
            xs, b_xs = xb[sh % 2], b_xb[sh % 2]
            if own:
                for k in range(16):
                    P.op("sp", lambda e, k=k: e.dma_start(out=xh[:, k, :], in_=xT_d[k * 128:(k + 1) * 128, 0:512]),
                         writes=[b_x0[k], b_xb[0][k]], dma=True)
            for k in range(16):
                P.op("act", lambda e, k=k: e.activation(out=sqb[:, k, :], in_=xs[:, k, :], func=AF.Square), reads=[b_xs[k]], writes=[b_sqb[k]])
            if sh + 1 < 8:
                load_prev(sh + 1)
            for c in range(8):
                for k in range(16):
                    P.op("pe", lambda e, k=k, c=c: e.matmul(PA[:, 1024 + c:1024 + c + 1], lhsT=sqb[:, k, 128 * c:128 * c + 128], rhs=ones_col,
                                                          start=(k == 0), stop=(k == 15)), reads=[b_sqb[k], b_ones], writes=[bPA[2]])
            P.op("act", lambda e: e.activation(out=rc[:], in_=PA[:, 1024:1032], func=AF.Sqrt, scale=1.0 / D, bias=eps_c[:]),
                 reads=[bPA[2], b_eps], writes=[b_rc])
            P.op("dve", lambda e: e.reciprocal(out=rc[:], in_=rc[:]), reads=[b_rc], writes=[b_rc])
            for half in range(2):
                for cc in range(4):
                    c = 4 * half + cc
                    for h in range(2):
                        for k in range(16):
                            P.op("pe", lambda e, k=k, h=h, c=c: e.matmul(
                                PA[:, 512 * h:512 * h + 512], lhsT=xs[:, k, 128 * c:128 * c + 128], rhs=Wssm[:, k, 512 * h:512 * h + 512],
                                start=(k == 0), stop=(k == 15)), reads=[b_xs[k], b_Wssm], writes=[bPA[h]])
                    P.op("act", lambda e, cc=cc, c=c: e.activation(out=ztq[:, cc, :], in_=PA[:, 0:1024], func=AF.Identity, scale=rc[:, c:c + 1]),
                         reads=[bPA[0], bPA[1], b_rc], writes=[b_ztq[cc]])
                vprime_half(sh, half)
            if own and tap("ylall", ylall[:], [128, 8, 2, 32], F32, [b_ylall]):
                return nc
            if own and tap("yl0", ylall[:, 0, :, :], [128, 2, 32], F32, [b_ylall]):
                return nc
            if own and tap("yl5", ylall[:, 5, :, :], [128, 2, 32], F32, [b_ylall]):
                return nc
            if own:
                P.barrier()
                P.op("dve", lambda e: e.memset(hl[:, 0, :, :], 0.0), writes=[b_hl])
                for c in range(8):
                    cmul("dve", sm[:, 2, 0, :], sm[:, 2, 1, :], a127r, a127i, ylall[:, c, 0, :], ylall[:, c, 1, :], sm[:, 0, 0, :], sm[:, 0, 1, :],
                         [b_a, b_ylall], [b_sm])
                    cmul("dve", sm[:, 3, 0, :], sm[:, 3, 1, :], a128r, a128i, hl[:, c, 0, :], hl[:, c, 1, :], sm[:, 1, 0, :], sm[:, 1, 1, :],
                         [b_a, b_hl], [b_sm])
                    P.op("dve", lambda e, c=c: e.tensor_tensor(out=hl[:, c + 1, :, :], in0=sm[:, 2, :, :], in1=sm[:, 3, :, :], op=ALU.add),
                         reads=[b_sm], writes=[b_hl])
            else:
                cmul("dve", cmt[:, 0, :, :], cmt[:, 1, :, :], AW[:, :, 0, :], AW[:, :, 1, :], ylall[:, :, 0, :], ylall[:, :, 1, :],
                     cmt[:, 2, :, :], cmt[:, 3, :, :], [b_AW, b_ylall], [b_cmt])
                P.op("dve", lambda e: e.tensor_reduce(out=sm[:, 4, 0, :], in_=cmt[:, 0, :, :].rearrange("p c a -> p a c"), axis=AX.X, op=ALU.add),
                     reads=[b_cmt], writes=[b_sm])
                P.op("dve", lambda e: e.tensor_reduce(out=sm[:, 4, 1, :], in_=cmt[:, 1, :, :].rearrange("p c a -> p a c"), axis=AX.X, op=ALU.add),
                     reads=[b_cmt], writes=[b_sm])
                cmul("dve", sm[:, 2, 0, :], sm[:, 2, 1, :], a1024r, a1024i, H0[:, 0, :], H0[:, 1, :], sm[:, 0, 0, :], sm[:, 0, 1, :],
                     [b_apw, b_H0], [b_sm])
                P.op("dve", lambda e: e.tensor_tensor(out=H0[:], in0=sm[:, 2, :, :], in1=sm[:, 4, :, :], op=ALU.add),
                     reads=[b_sm], writes=[b_H0])
        P.barrier()
        for hf in range(2):
            tsl = slice(512 * hf, 512 * hf + 512)
            if hf == 1:
                for k in range(16):
                    P.op("sp", lambda e, k=k, tsl=tsl: e.dma_start(out=xh[:, k, :], in_=xT_d[k * 128:(k + 1) * 128, tsl]), writes=[b_x0[k]], dma=True)
            for k in range(16):
                P.op("act", lambda e, k=k: e.activation(out=sq[:, 0:512], in_=xh[:, k, :], func=AF.Square), reads=[b_x0[k]], writes=[b_sq])
                P.op("pe", lambda e, k=k: e.matmul(PA[:, 0:512], lhsT=ones_bf[:], rhs=sq[:, 0:512], start=(k == 0), stop=(k == 15)),
                     reads=[b_sq, b_ones], writes=[bPA[0]])
            P.op("act", lambda e: e.activation(out=rstd[:, 0:512], in_=PA[:, 0:512], func=AF.Sqrt, scale=1.0 / D, bias=eps_c[:]),
                 reads=[bPA[0], b_eps], writes=[b_rstd])
            P.op("dve", lambda e: e.reciprocal(out=rstd[:, 0:512], in_=rstd[:, 0:512]), reads=[b_rstd], writes=[b_rstd])
            for k in range(16):
                P.op("dve", lambda e, k=k, tsl=tsl: e.scalar_tensor_tensor(out=hn[:, k, tsl], in0=xh[:, k, :], scalar=g_mix[:, k:k + 1],
                                                                          in1=rstd[:, 0:512], op0=ALU.mult, op1=ALU.mult),
                     reads=[b_x0[k], b_rstd, b_cols], writes=[b_hn[k]])
        if tap("hn", hn, [128, 16, 1024], BF16, b_hn):
            return nc
        if tap("hl1", hl[:, 1, :, :], [128, 2, 32], F32, [b_hl]):
            return nc
        if tap("hl8", hl[:, 8, :, :], [128, 2, 32], F32, [b_hl]):
            return nc
        P.barrier()
        gen_BC()
        P.barrier()
        slot_i = [0]

        def load_slab(dview, nk=16):
            i = slot_i[0] % 3
            slot_i[0] += 1
            P.op("pool", lambda e: e.dma_start(out=wsl[i][:, 0:nk, :], in_=dview), writes=[b_wsl[i]], dma=True)
            return wsl[i], b_wsl[i]

        nblk = 24

        for m in range(nblk):
            slab, b_slab = load_slab(w_in_v[:, :, 128 * m:128 * m + 128])
            if m < 16:
                for h in range(2):
                    pb, bb = next_bank()
                    for k in range(16):
                        P.op("pe", lambda e, pb=pb, k=k, h=h, slab=slab: e.matmul(
                            pb, lhsT=slab[:, k, :], rhs=hn[:, k, 512 * h:512 * h + 512], start=(k == 0), stop=(k == 15)),
                            reads=[b_slab, b_hn[k]], writes=[bb])
                    if m < 8:
                        P.op("act", lambda e, pb=pb, m=m, h=h: e.activation(out=zs[:, m, 512 * h:512 * h + 512], in_=pb, func=AF.Copy),
                             reads=[bb], writes=[b_zs[m]])
                    else:
                        P.op("act", lambda e, pb=pb, m=m, h=h: e.activation(out=zu[:, m - 8, 512 * h:512 * h + 512], in_=pb,
                                                                         func=AF.Gelu_apprx_tanh), reads=[bb], writes=[b_zu[m - 8]])
            else:
                mv = m - 16
                for cg in range(2):
                    pb, bb = next_bank()
                    for cj in range(4):
                        c = cg * 4 + cj
                        for k in range(16):
                            P.op("pe", lambda e, pb=pb, k=k, c=c, cj=cj, slab=slab: e.matmul(
                                pb[:, 128 * cj:128 * cj + 128], lhsT=hn[:, k, 128 * c:128 * c + 128], rhs=slab[:, k, :],
                                start=(k == 0), stop=(k == 15)), reads=[b_slab, b_hn[k]], writes=[bb])
                    P.op("act", lambda e, pb=pb, cg=cg, mv=mv: e.activation(
                        out=zv[:, 4 * cg:4 * cg + 4, 128 * mv:128 * mv + 128], in_=pb.rearrange("p (c d) -> p c d", c=4),
                        func=AF.Gelu_apprx_tanh), reads=[bb], writes=b_zv[4 * cg:4 * cg + 4])

        if tap("zs", zs, [128, 8, 1024], BF16, b_zs):
            return nc
        P.barrier()
        tmp4 = [A(64 + 4 * i, 1024, F32) for i in range(4)]
        b_tmp4 = [Buf("tmp4_%d" % i) for i in range(4)]
        Em5 = Em.rearrange("p (g a r q) -> p g a r q", g=4, a=8, r=2)
        Wb5 = Wb.rearrange("p (g a r q) -> p g a r q", g=4, a=8, r=2)
        EpT5 = EpT.rearrange("p (g a r t) -> p g a r t", g=4, a=8, r=2)
        XT5 = XT.rearrange("p (g a r t) -> p g a r t", g=4, a=8, r=2)
        XT3 = XT.rearrange("p (j t) -> p j t", j=64)
        Wb3 = Wb.rearrange("p (j q) -> p j q", j=64)
        PAv = PA.rearrange("p (a r q) -> p a r q", a=8, r=2)
        PBv = PB.rearrange("p (a r q) -> p a r q", a=8, r=2)
        v8 = lambda ap: ap.rearrange("p (a q) -> p a q", a=8)

        def step12(c):
            for g in range(4):
                PX, bPX, PXv = (PA, bPA, PAv) if g % 2 == 0 else (PB, bPB, PBv)
                for q in range(4):
                    blk, half = 2 * g + q // 2, q % 2
                    P.op("pe", lambda e, PX=PX, q=q, blk=blk, half=half: e.matmul(
                        PX[:, 512 * q:512 * q + 512], lhsT=zs[:, blk, 128 * c:128 * c + 128],
                        rhs=Bbd[:, blk * 1024 + half * 512: blk * 1024 + half * 512 + 512], start=True, stop=True),
                        reads=[b_zs[blk], b_Bbd], writes=[bPX[q]])
                Br_, Bi_ = PXv[:, :, 0, :], PXv[:, :, 1, :]
                Er_, Ei_ = Em5[:, g, :, 0, :], Em5[:, g, :, 1, :]
                t = [v8(x) for x in tmp4]
                P.op("dve", lambda e: e.tensor_tensor(out=t[0], in0=Br_, in1=Er_, op=ALU.mult), reads=bPX + [b_Em], writes=[b_tmp4[0]])
                P.op("dve", lambda e: e.tensor_tensor(out=t[1], in0=Bi_, in1=Ei_, op=ALU.mult), reads=bPX + [b_Em], writes=[b_tmp4[1]])
                P.op("dve", lambda e: e.tensor_tensor(out=t[2], in0=Bi_, in1=Er_, op=ALU.mult), reads=bPX + [b_Em], writes=[b_tmp4[2]])
                P.op("dve", lambda e: e.tensor_tensor(out=t[3], in0=Br_, in1=Ei_, op=ALU.mult), reads=bPX + [b_Em], writes=[b_tmp4[3]])
                P.op("pool", lambda e, g=g: e.tensor_tensor(out=Wb5[:, g, :, 0, :], in0=t[0], in1=t[1], op=ALU.subtract),
                     reads=[b_tmp4[0], b_tmp4[1]], writes=[b_Wb])
                P.op("pool", lambda e, g=g: e.tensor_tensor(out=Wb5[:, g, :, 1, :], in0=t[2], in1=t[3], op=ALU.add),
                     reads=[b_tmp4[2], b_tmp4[3]], writes=[b_Wb])

        if tap("H0", H0[:], [128, 2, 32], F32, [b_H0]):
            return nc
        hc = sb("hc", [128, 2, 32], F32); b_hc = Buf("hc")
        P.op("dve", lambda e: e.tensor_copy(out=hc[:], in_=H0[:]), reads=[b_H0], writes=[b_hc])
        for c in range(8):
            P.op("dve", lambda e, c=c: e.tensor_tensor(out=sm[:, 4, :, :], in0=hl[:, c, :, :], in1=hc[:], op=ALU.add),
                 reads=[b_hl, b_hc], writes=[b_sm])
            cmul("dve", hp[:, c, 0, :], hp[:, c, 1, :], a1r, a1i, sm[:, 4, 0, :], sm[:, 4, 1, :], sm[:, 0, 0, :], sm[:, 0, 1, :],
                 [b_a, b_sm], [b_hp, b_sm])
            if c < 7:
                cmul("dve", sm[:, 5, 0, :], sm[:, 5, 1, :], a128r, a128i, hc[:, 0, :], hc[:, 1, :], sm[:, 1, 0, :], sm[:, 1, 1, :],
                     [b_a, b_hc], [b_sm])
                P.op("dve", lambda e: e.tensor_copy(out=hc[:], in_=sm[:, 5, :, :]), reads=[b_sm], writes=[b_hc])

        ypre = sb("ypre", [128, 2, 128], F32); b_ypre = [Buf("ypre0"), Buf("ypre1")]
        W2 = [A(96 + 4 * i, 2048).rearrange("p (k a q) -> p k a q", k=4, a=4) for i in range(2)]; b_W2 = [Buf("W2_0"), Buf("W2_1")]
        X2 = [A(104 + 4 * i, 2048).rearrange("p (k a q) -> p k a q", k=4, a=4) for i in range(2)]; b_X2 = [Buf("X2_0"), Buf("X2_1")]
        Zs = [A(112 + 2 * i, 1024).rearrange("p (a r t) -> p a r t", a=4, r=2) for i in range(2)]; b_Zs = [Buf("Zs0"), Buf("Zs1")]
        Em6 = Em.rearrange("p (b a r q) -> p b a r q", b=8, a=4, r=2)
        EpT6 = EpT.rearrange("p (b a r t) -> p b a r t", b=8, a=4, r=2)
        def pb_views(i):
            st = i % 2
            PX, bPX = (PA, bPA) if st == 0 else (PB, bPB)
            return st, PX, bPX

        def stage0(i):
            c, blk = divmod(i, 8)
            st, PX, bPX = pb_views(i)
            PS1 = PX[:, 0:1024].rearrange("p (a r q) -> p a r q", a=4, r=2)
            for h in range(2):
                P.op("pe", lambda e: e.matmul(PX[:, 512 * h:512 * h + 512], lhsT=zs[:, blk, 128 * c:128 * c + 128],
                                              rhs=Bbd[:, blk * 1024 + h * 512: blk * 1024 + h * 512 + 512], start=True, stop=True),
                     reads=[b_zs[blk], b_Bbd], writes=[bPX[h]])
            Br_, Bi_ = PS1[:, :, 0, :], PS1[:, :, 1, :]
            Er_, Ei_ = Em6[:, blk, :, 0, :], Em6[:, blk, :, 1, :]
            w2, bw2 = W2[st], b_W2[st]
            rd = [bPX[0], bPX[1], b_Em]
            P.op("dve", lambda e: e.tensor_tensor(out=w2[:, 0], in0=Br_, in1=Er_, op=ALU.mult), reads=rd, writes=[bw2])
            P.op("dve", lambda e: e.scalar_tensor_tensor(out=w2[:, 1], in0=Bi_, scalar=-1.0, in1=Ei_, op0=ALU.mult, op1=ALU.mult), reads=rd, writes=[bw2])
            P.op("dve", lambda e: e.tensor_tensor(out=w2[:, 2], in0=Bi_, in1=Er_, op=ALU.mult), reads=rd, writes=[bw2])
            P.op("dve", lambda e: e.tensor_tensor(out=w2[:, 3], in0=Br_, in1=Ei_, op=ALU.mult), reads=rd, writes=[bw2])

        def stage1(i):
            c, blk = divmod(i, 8)
            st, PX, bPX = pb_views(i)
            w2, bw2 = W2[st], b_W2[st]
            for a in range(4):
                for ri in range(2):
                    jj = 2 * a + ri
                    for q2 in range(2):
                        P.op("pe", lambda e: e.matmul(PX[:, 1024 + 128 * jj:1024 + 128 * jj + 128], lhsT=w2[:, 2 * ri + q2, a, :], rhs=tri_bf[:],
                                                      start=(q2 == 0), stop=(q2 == 1)), reads=[bw2, b_tri], writes=[bPX[2 + jj // 4]])
            zsb, bz = Zs[st], b_Zs[st]
            for a in range(4):
                for ri in range(2):
                    jj = 2 * a + ri
                    pr = 4 * blk + a
                    P.op("act", lambda e: e.activation(out=zsb[:, a, ri, :], in_=PX[:, 1024 + 128 * jj:1024 + 128 * jj + 128], func=AF.Identity,
                                                       bias=hp[:, c, ri, pr:pr + 1]), reads=[bPX[2 + jj // 4], b_hp], writes=[bz])
            Zr, Zi = zsb[:, :, 0, :], zsb[:, :, 1, :]
            Fr_, Fi_ = EpT6[:, blk, :, 0, :], EpT6[:, blk, :, 1, :]
            x2, bx2 = X2[st], b_X2[st]
            rd2 = [bz, b_EpT]
            P.op("dve", lambda e: e.tensor_tensor(out=x2[:, 0], in0=Zr, in1=Fr_, op=ALU.mult), reads=rd2, writes=[bx2])
            P.op("dve", lambda e: e.scalar_tensor_tensor(out=x2[:, 1], in0=Zi, scalar=-1.0, in1=Fi_, op0=ALU.mult, op1=ALU.mult), reads=rd2, writes=[bx2])
            P.op("dve", lambda e: e.tensor_tensor(out=x2[:, 2], in0=Zi, in1=Fr_, op=ALU.mult), reads=rd2, writes=[bx2])
            P.op("dve", lambda e: e.tensor_tensor(out=x2[:, 3], in0=Zr, in1=Fi_, op=ALU.mult), reads=rd2, writes=[bx2])

        def stage2(i):
            c, blk = divmod(i, 8)
            st, PX, bPX = pb_views(i)
            x2, bx2 = X2[st], b_X2[st]
            PS3 = PX[:, 1024:1152]
            n_mm = 0
            for a in range(4):
                for k4 in range(4):
                    j = (4 * blk + a) * 2 + (k4 // 2)
                    P.op("pe", lambda e: e.matmul(PS3, lhsT=Cbd[:, j * 128:(j + 1) * 128], rhs=x2[:, k4, a, :], start=(n_mm == 0), stop=(n_mm == 15)),
                         reads=[b_Cbd, bx2], writes=[bPX[2]])
                    n_mm += 1
            P.op("dve", lambda e: e.scalar_tensor_tensor(out=ypre[:, st, :], in0=zs[:, blk, 128 * c:128 * c + 128], scalar=Dcol[:, blk:blk + 1], in1=PS3,
                                                         op0=ALU.mult, op1=ALU.add), reads=[b_zs[blk], bPX[2], b_cols], writes=[b_ypre[st]])
            P.op("act", lambda e: e.activation(out=ys[:, blk, 128 * c:128 * c + 128], in_=ypre[:, st, :], func=AF.Gelu_apprx_tanh),
                 reads=[b_ypre[st]], writes=[b_ys[blk]])

        NI = 64
        for step in range(NI + 2):
            if step < NI:
                stage0(step)
            if 0 <= step - 1 < NI:
                stage1(step - 1)
            if 0 <= step - 2 < NI:
                stage2(step - 2)

        if tap("ys", ys, [128, 8, 1024], BF16, b_ys):
            return nc
        mixed = hn; b_mixed = b_hn
        glu_v = glu_d.rearrange("(k p) c -> p k c", p=128)
        P.barrier()
        gate = A(94, 512); b_gate = Buf("gate")
        for m in range(8):
            slab, b_slab = load_slab(glu_v[:, :, 128 * m:128 * m + 128], 8)
            for h in range(2):
                pb, bb = next_bank()
                for k in range(8):
                    P.op("pe", lambda e, pb=pb, k=k, h=h, slab=slab: e.matmul(pb, lhsT=slab[:, k, :], rhs=ys[:, k, 512 * h:512 * h + 512],
                                                                           start=(k == 0), stop=(k == 7)), reads=[b_slab, b_ys[k]], writes=[bb])
                P.op("act", lambda e, pb=pb, m=m: e.activation(out=gate, in_=pb, func=AF.Sigmoid, bias=glu_b[:, m:m + 1]),
                     reads=[bb, b_cols], writes=[b_gate])
                P.op("dve", lambda e, m=m, h=h: e.tensor_tensor(out=mixed[:, m, 512 * h:512 * h + 512], in0=ys[:, m, 512 * h:512 * h + 512],
                                                              in1=gate, op=ALU.mult), reads=[b_gate, b_ys[m]], writes=[b_mixed[m]])

        if tap("mixA", mixed[:, 0:8, :], [128, 8, 1024], BF16, b_mixed):
            return nc
        lnrows = A(84, 2048).rearrange("p (a f) -> p a f", a=2); b_lnrows = Buf("lnrows")
        for a_ in range(2):
            P.op("pool", lambda e, a_=a_: e.dma_start(out=lnrows[:, a_, :], in_=rows_d[a_:a_ + 1, :].partition_broadcast(128).rearrange("p o f -> p (o f)")),
                 writes=[b_lnrows], dma=True)
        wmT = A(68, 1024).rearrange("p (a t) -> p a t", a=8); b_wmT = Buf("wmT")
        wtmp = A(64, 1024, F32).rearrange("p (a t) -> p a t", a=8)
        b_wtmp = Buf("wtmp")
        P.op("sp", lambda e: e.dma_start(out=wtmp, in_=sguw_d[:, :, :]), writes=[b_wtmp], dma=True)
        P.op("dve", lambda e: e.tensor_tensor(out=wmT, in0=wtmp, in1=tri_f.unsqueeze(1).broadcast_to([128, 8, 128]), op=ALU.mult),
             reads=[b_wtmp, b_cst], writes=[b_wmT])
        bsrow = A(70, 1024)[0:1, :]; b_bsrow = Buf("bsrow")
        P.op("pool", lambda e: e.dma_start(out=bsrow, in_=sgub_d[:, :]), writes=[b_bsrow], dma=True)
        stats = sb("stats", [128, 2, 6], F32); b_stats = Buf("stats")
        mv_ = sb("mv", [128, 2], F32); b_mv = Buf("mv")
        for c in range(8):
            for h in range(2):
                P.op("dve", lambda e, c=c, h=h: e.bn_stats(out=stats[:, h, :], in_=zv[:, c, 512 * h:512 * h + 512]),
                     reads=[b_zv[c]], writes=[b_stats])
            P.op("dve", lambda e: e.bn_aggr(out=mv_[:], in_=stats[:].rearrange("p a s -> p (a s)")), reads=[b_stats], writes=[b_mv])
            P.op("act", lambda e: e.activation(out=mv_[:, 1:2], in_=mv_[:, 1:2], func=AF.Sqrt, bias=eps_c[:]), reads=[b_mv, b_eps], writes=[b_mv])
            P.op("dve", lambda e: e.reciprocal(out=mv_[:, 1:2], in_=mv_[:, 1:2]), reads=[b_mv], writes=[b_mv])
            P.op("dve", lambda e, c=c: e.tensor_scalar(out=zv[:, c, :], in0=zv[:, c, :], scalar1=mv_[:, 0:1], scalar2=mv_[:, 1:2],
                                                      op0=ALU.subtract, op1=ALU.mult), reads=[b_mv, b_zv[c]], writes=[b_zv[c]])
            P.op("pool", lambda e, c=c: e.tensor_tensor(out=zv[:, c, :], in0=zv[:, c, :], in1=lnrows[:, 0, :], op=ALU.mult),
                 reads=[b_lnrows, b_zv[c]], writes=[b_zv[c]])
            P.op("pool", lambda e, c=c: e.tensor_tensor(out=zv[:, c, :], in0=zv[:, c, :], in1=lnrows[:, 1, :], op=ALU.add),
                 reads=[b_lnrows, b_zv[c]], writes=[b_zv[c]])
        for hd in range(8):
            for cg in range(2):
                pb, bb = next_bank()
                for cj in range(4):
                    c = cg * 4 + cj
                    P.op("pe", lambda e, pb=pb, cj=cj, c=c, hd=hd: e.matmul(pb[:, 128 * cj:128 * cj + 128], lhsT=zv[:, c, 128 * hd:128 * hd + 128],
                                                                         rhs=wmT[:, hd, :], start=True, stop=False),
                         reads=[b_zv[c], b_wmT], writes=[bb])
                    P.op("pe", lambda e, pb=pb, cj=cj, hd=hd: e.matmul(pb[:, 128 * cj:128 * cj + 128], lhsT=ones_bf[0:1, :],
                                                                    rhs=bsrow[0:1, 128 * hd:128 * hd + 128], start=False, stop=True),
                         reads=[b_ones, b_bsrow], writes=[bb])
                P.op("dve", lambda e, pb=pb, hd=hd, cg=cg: e.tensor_tensor(out=mixed[:, 8 + hd, 512 * cg:512 * cg + 512], in0=pb,
                                                                        in1=zu[:, hd, 512 * cg:512 * cg + 512], op=ALU.mult),
                     reads=[bb, b_zu[hd]], writes=[b_mixed[8 + hd]])

        if tap("mixB", mixed[:, 8:16, :], [128, 8, 1024], BF16, b_mixed):
            return nc
        class V:
            def __init__(self, ap, off): self.ap, self.off = ap, off
            def __getitem__(self, idx): return self.ap[idx[0], idx[1] + self.off, idx[2]]
        P.barrier()
        rmsnorm(V(mixed, 0), b_mixed[0:8], g_oss, V(mixed, 0), b_mixed[0:8], 8, rstd, b_rstd, 1024.0)
        rmsnorm(V(mixed, 8), b_mixed[8:16], g_osg, V(mixed, 8), b_mixed[8:16], 8, rstd2, b_rstd2, 1024.0)

        if tap("mixN", mixed, [128, 16, 1024], BF16, b_mixed):
            return nc
        P.barrier()
        x1T = A(0, 16 * 1024, F32).rearrange("p (k t) -> p k t", k=16); b_x1 = [Buf("x1_%d" % k) for k in range(16)]
        w_out_v = w_out_d.rearrange("(k p) c -> p k c", p=128)
        for n in range(16):
            P.op("sp", lambda e, n=n: e.dma_start(out=x1T[:, n, :], in_=xT_d[n * 128:(n + 1) * 128, :]), writes=[b_x1[n]], dma=True)
        for n in range(16):
            slab, b_slab = load_slab(w_out_v[:, :, 128 * n:128 * n + 128])
            for h in range(2):
                pb, bb = next_bank()
                for k in range(16):
                    P.op("pe", lambda e, pb=pb, k=k, h=h, slab=slab: e.matmul(pb, lhsT=slab[:, k, :], rhs=mixed[:, k, 512 * h:512 * h + 512],
                                                                           start=(k == 0), stop=(k == 15)), reads=[b_slab, b_mixed[k]], writes=[bb])
                P.op("dve", lambda e, pb=pb, n=n, h=h: e.tensor_tensor(out=x1T[:, n, 512 * h:512 * h + 512], in0=pb,
                                                                     in1=x1T[:, n, 512 * h:512 * h + 512], op=ALU.add),
                     reads=[bb, b_x1[n]], writes=[b_x1[n]])

        if tap("x1", x1T, [128, 16, 1024], F32, b_x1):
            return nc
        hn2, b_hn2 = hn, b_hn
        rmsnorm(x1T, b_x1, g_mlp, hn2, b_hn2, 16, rstd, b_rstd, float(D))
        P.barrier()
        Hh = A(128, 64 * 512).rearrange("p (m t) -> p m t", m=64); b_H = [Buf("H%d" % m) for m in range(64)]
        usl = [A(64 + 4 * i, 2048).rearrange("p (k c) -> p k c", k=16) for i in range(2)]
        b_usl = [Buf("usl%d" % i) for i in range(2)]
        dsl = [A(72 + 8 * i, 4096).rearrange("p (k c) -> p k c", k=32) for i in range(2)]
        b_dsl = [Buf("dsl%d" % i) for i in range(2)]
        w_up_v = w_up_d.rearrange("(k p) c -> p k c", p=128)
        w_dn_v = w_dn_d.rearrange("(k p) c -> p k c", p=128)
        ui = [0]; di = [0]
        Hf = Hh.rearrange("p m t -> p (m t)").rearrange("p (m t) -> p m t", m=32)
        relu2 = [A(94, 512), A(95, 512)]; b_relu2 = [Buf("relu0"), Buf("relu1")]
        for hh in range(2):
            for mm in range(32):
                m = 32 * hh + mm
                i = ui[0] % 2; ui[0] += 1
                P.op("pool", lambda e, i=i, m=m: e.dma_start(out=usl[i], in_=w_up_v[:, :, 128 * m:128 * m + 128]), writes=[b_usl[i]], dma=True)
                for th_ in range(2):
                    ts = slice(512 * th_, 512 * th_ + 512)
                    pb, bb = next_bank()
                    for k in range(16):
                        P.op("pe", lambda e, pb=pb, k=k, i=i, ts=ts: e.matmul(pb, lhsT=usl[i][:, k, :], rhs=hn2[:, k, ts], start=(k == 0), stop=(k == 15)),
                             reads=[b_usl[i], b_hn2[k]], writes=[bb])
                    rt, b_rt = relu2[th_], b_relu2[th_]
                    P.op("act", lambda e, pb=pb, rt=rt: e.activation(out=rt, in_=pb, func=AF.Relu), reads=[bb], writes=[b_rt])
                    P.op("dve", lambda e, mm=mm, ts=ts, rt=rt: e.tensor_tensor(out=Hf[:, mm, ts], in0=rt, in1=rt, op=ALU.mult), reads=[b_rt], writes=[b_H[mm]])
            for n in range(16):
                i = di[0] % 2; di[0] += 1
                P.op("pool", lambda e, i=i, n=n, hh=hh: e.dma_start(out=dsl[i], in_=w_dn_v[:, 32 * hh:32 * hh + 32, 128 * n:128 * n + 128]),
                     writes=[b_dsl[i]], dma=True)
                for th_ in range(2):
                    ts = slice(512 * th_, 512 * th_ + 512)
                    pb, bb = next_bank()
                    for kk in range(32):
                        P.op("pe", lambda e, pb=pb, kk=kk, i=i, ts=ts: e.matmul(pb, lhsT=dsl[i][:, kk, :], rhs=Hf[:, kk, ts], start=(kk == 0), stop=(kk == 31)),
                             reads=[b_dsl[i], b_H[kk]], writes=[bb])
                    P.op("dve", lambda e, pb=pb, n=n, ts=ts: e.tensor_tensor(out=x1T[:, n, ts], in0=pb, in1=x1T[:, n, ts], op=ALU.add),
                         reads=[bb, b_x1[n]], writes=[b_x1[n]])

        if tap("x2", x1T, [128, 16, 1024], F32, b_x1):
            return nc
        P.barrier()
        rmsnorm(x1T, b_x1, g_fin, x1T, b_x1, 16, rstd, b_rstd, float(D))
        b_out = Buf("out")
        for n in range(16):
            P.op("sp", lambda e, n=n: e.dma_start(out=out_d[n * 128:(n + 1) * 128, :], in_=x1T[:, n, :]), reads=[b_x1[n]], dma=True)
        P.final_wait("sp", b_x1)
        with nc.Block() as block:
            P.emit(block)
    return nc


def _consts():
    c = np.zeros((128, 640), np.float32)
    c[:, 0:128] = np.eye(128, dtype=np.float32)
    s = np.arange(128)
    c[:, 128:256] = (s[:, None] <= s[None, :]).astype(np.float32)
    c[:, 256] = s.astype(np.float32)
    for gl in range(8):
        c[16 * gl:16 * gl + 16, 320 + gl] = 1.0
    return c


def _col(v):
    return np.ascontiguousarray(np.asarray(v, np.float32).reshape(-1, 128).T)


_NC_CACHE = {}


def _get_nc(mode):
    if mode not in _NC_CACHE:
        _NC_CACHE[mode] = build(mode)
    return _NC_CACHE[mode]


def make_in_maps(inp, mode, sall=None):
    f = lambda a: np.ascontiguousarray(np.asarray(a, np.float32))
    x = f(inp["x"])[0]
    a_re, a_im = f(inp["ssm_a_re"])[0], f(inp["ssm_a_im"])[0]
    ldt = f(inp["ssm_log_dt"])[0]
    Bre, Bim = f(inp["ssm_b_re"])[0], f(inp["ssm_b_im"])[0]
    Cre, Cim = f(inp["ssm_c_re"])[0], f(inp["ssm_c_im"])[0]
    ldt_rep = np.repeat(ldt[:, None], 64, axis=1)

    def blay(a_gp):
        t = a_gp.reshape(8, 8, 64)
        t = np.repeat(t[:, :, None, :], 16, axis=2)
        return np.ascontiguousarray(t.transpose(1, 2, 0, 3).reshape(128, 512))

    def blayB(b_gph):
        t = b_gph.reshape(8, 8, 64, 16)
        return np.ascontiguousarray(t.transpose(1, 3, 0, 2).reshape(128, 512))

    def clay(c_ghp):
        t = c_ghp.reshape(32, 2, 16, 64)
        return np.ascontiguousarray(t.transpose(1, 3, 0, 2).reshape(128, 512))

    def tlayp(a_gp):
        t = a_gp.reshape(32, 2, 64)
        t = np.repeat(t[:, :, :, None], 16, axis=3)
        return np.ascontiguousarray(t.transpose(1, 2, 0, 3).reshape(128, 512))

    def tlayB(b_gph):
        t = b_gph.reshape(32, 2, 64, 16)
        return np.ascontiguousarray(t.transpose(1, 2, 0, 3).reshape(128, 512))

    pT = np.ascontiguousarray(np.stack([tlayp(a_re), tlayp(a_im), tlayp(ldt_rep)], axis=1))
    BTl = np.ascontiguousarray(np.stack([tlayB(Bre), tlayB(Bim)], axis=1))
    pB = np.ascontiguousarray(np.stack([blay(a_re), blay(a_im), blay(ldt_rep)], axis=1))
    BT = np.ascontiguousarray(np.stack([blayB(Bre), blayB(Bim)], axis=1))
    pE = np.ascontiguousarray(np.stack([a_re.reshape(-1), a_im.reshape(-1), ldt_rep.reshape(-1)], axis=0))
    CT = np.ascontiguousarray(np.stack([clay(Cre), clay(Cim)], axis=1))
    cols = np.zeros((128, 96), np.float32)
    cols[:, 0:16] = _col(inp["norm_mix_g"][0]); cols[:, 16:32] = _col(inp["norm_mlp_g"][0]); cols[:, 32:48] = _col(inp["norm_final_g"])
    cols[:, 48:56] = _col(inp["out_norm_ssm_g"][0]); cols[:, 56:64] = _col(inp["out_norm_sgu_g"][0])
    cols[:, 64:72] = _col(inp["ssm_glu_b"][0]); cols[:, 72:80] = _col(f(inp["ssm_d"])[0].reshape(-1))
    common = {"w_in": f(inp["w_in"])[0], "cst": _consts(), "pB": pB, "BT": BT, "pE": pE, "CT": CT, "cols": cols, "pT": pT, "BTl": BTl}
    xTs = [np.ascontiguousarray(x[c * T:(c + 1) * T].T) for c in range(NC)]
    if mode == "A":
        return [dict(common, xT=xTs[c]) for c in range(NC)]
    rows = np.ascontiguousarray(np.stack([f(inp["sgu_ln_g"])[0], f(inp["sgu_ln_b"])[0]], axis=0))
    sguwT = np.ascontiguousarray(f(inp["sgu_w"])[0].transpose(2, 0, 1))
    sgub = np.ascontiguousarray(f(inp["sgu_b"])[0].reshape(1, 1024))
    commonB = dict(common, w_out=f(inp["w_out"])[0], w_up=f(inp["w_up"])[0], w_down=f(inp["w_down"])[0],
                   glu_w=f(inp["ssm_glu_w"])[0], rows=rows, sguwT=sguwT, sgub=sgub)
    zero = np.zeros((D, T), np.float32)
    in_maps = []
    for c in range(NC):
        xTp = np.ascontiguousarray(np.stack([xTs[c - 7 + j] if c - 7 + j >= 0 else zero for j in range(8)], axis=0))
        in_maps.append(dict(commonB, xT=xTs[c], xTp=xTp))
    return in_maps


def kernel(**inp):
    resB = run_bass_kernel_spmd(_get_nc("B"), make_in_maps(inp, "B"), core_ids=list(range(NC)))
    y = np.concatenate([resB.results[c]["yT"].T for c in range(NC)], axis=0)
    return np.ascontiguousarray(y.reshape(1, NC * T, D).astype(np.float32))
```

```python
import math
from contextlib import ExitStack
import numpy as np
import concourse.bass as bass
import concourse.mybir as mybir
from concourse.bass_utils import run_bass_kernel_spmd

F32 = mybir.dt.float32
BF16 = mybir.dt.bfloat16
I32 = mybir.dt.int32
AF = mybir.ActivationFunctionType
ALU = mybir.AluOpType
AX = mybir.AxisListType

NC = 8
T = 1024
D = 2048
KD = 16
DFF = 8192
EPS = 1e-6
TWO_PI = 2.0 * math.pi


class Tok:
    __slots__ = ("key", "sem", "val", "eng")

    def __init__(self, key, sem, val, eng):
        self.key, self.sem, self.val, self.eng = key, sem, val, eng


class Buf:
    def __init__(self, name="", excl=False):
        self.name = name
        self.excl = excl
        self.w = {}
        self.r = {}
        self.dsem = None


class _Rec:
    def __getattr__(self, name):
        def f(*a, **k):
            self.call = (name, a, k)
            return self
        return f


class Prog:
    ENGS = ["pe", "act", "dve", "pool", "sp"]

    def __init__(self, nc, stack):
        self.nc, self.stack = nc, stack
        self.ops = {e: [] for e in self.ENGS}
        self.esem = {e: stack.enter_context(nc.semaphore("es_" + e)) for e in self.ENGS}
        self.ecnt = {e: 0 for e in self.ENGS}
        self.waited = {}
        self.dcnt = {}
        self.dsems = {}
        self.nd = 0

    def barrier(self):
        toks = [Tok(f, self.esem[f], self.ecnt[f], f) for f in self.ENGS if self.ecnt[f] > 0]
        toks += [Tok(k, self.dsems[k], v, "dma") for k, v in self.dcnt.items()]
        for e in self.ENGS:
            for t in toks:
                if t.key != e:
                    self._wait(e, t)

    def _wait(self, eng, tok):
        k = (eng, tok.key)
        if self.waited.get(k, 0) >= tok.val:
            return
        self.waited[k] = tok.val
        sem, val = tok.sem, tok.val
        self.ops[eng].append(lambda e: e.wait_ge(sem, val))

    def alias(self, new, olds):
        for o in olds:
            for d in (o.w, o.r):
                for k, t in d.items():
                    if k not in new.w or new.w[k].val < t.val:
                        new.w[k] = t

    def op(self, eng, fn, reads=(), writes=(), dma=False):
        rec = _Rec()
        fn(rec)
        name_, a_, k_ = rec.call
        fn = lambda e: getattr(e, name_)(*a_, **k_)
        deps = []
        for b in reads:
            deps += list(b.w.values())
            if b.excl:
                deps += [t for t in b.r.values() if t.eng != eng or dma]
        for b in writes:
            deps += [t for t in list(b.w.values()) + list(b.r.values()) if t.eng != eng or dma]
        for t in deps:
            if eng == "pe" and t.eng == "pe" and not dma:
                continue
            self._wait(eng, t)
        if dma:
            tgt = writes[0] if writes else reads[0]
            if tgt.dsem is None:
                self.nd += 1
                tgt.dsem = ("d%d" % self.nd, self.stack.enter_context(self.nc.semaphore("ds%d" % self.nd)))
            key, sem = tgt.dsem
            self.dsems[key] = sem
            self.dcnt[key] = self.dcnt.get(key, 0) + 16
            tok = Tok(key, sem, self.dcnt[key], "dma")
            self.ops[eng].append(lambda e: fn(e).then_inc(sem, 16))
        else:
            self.ecnt[eng] += 1
            sem = self.esem[eng]
            tok = Tok(eng, sem, self.ecnt[eng], eng)
            self.ops[eng].append(lambda e: fn(e).then_inc(sem, 1))
        for b in reads:
            b.r[tok.key] = tok
        for b in writes:
            b.w[tok.key] = tok
            b.r = {}
        return tok

    def final_wait(self, eng, bufs):
        for b in bufs:
            for t in list(b.w.values()) + list(b.r.values()):
                self._wait(eng, t)

    def emit(self, block):
        m = {"pe": block.tensor, "act": block.scalar, "dve": block.vector, "pool": block.gpsimd, "sp": block.sync}
        for en in self.ENGS:
            ops = self.ops[en]

            def body(e, ops=ops):
                for f in ops:
                    f(e)

            m[en](body)


def build(mode, stage=None):
    nc = bass.Bass("TRN2", target_bir_lowering=False)
    full = True
    dbg_done = [False]

    def din(name, shape, dt=F32):
        return nc.dram_tensor(name, list(shape), dt, kind="ExternalInput").ap()

    xT_d = din("xT", [D, T])
    w_in_d = din("w_in", [D, 3072])
    cst_d = din("cst", [128, 640])
    pB_d = din("pB", [128, 3, 512])
    BT_d = din("BT", [128, 2, 512])
    pE_d = din("pE", [3, 4096])
    CT_d = din("CT", [128, 2, 512])
    cols_d = din("cols", [128, 96])
    if full:
        w_out_d = din("w_out", [D, D])
        w_up_d = din("w_up", [D, DFF])
        w_dn_d = din("w_down", [DFF, D])
        glu_d = din("glu_w", [1024, 1024])
        rows_d = din("rows", [2, 1024])
        sguw_d = din("sguwT", [128, 8, 128])
        sgub_d = din("sgub", [1, 1024])
        xTp_d = din("xTp", [8, D, T])
        pT_d = din("pT", [128, 3, 512])
        BTl_d = din("BTl", [128, 2, 512])
        out_d = nc.dram_tensor("yT", [D, T], F32, kind="ExternalOutput").ap()
    else:
        out_d = nc.dram_tensor("send", [128, 64], F32, kind="ExternalOutput").ap()

    stack = ExitStack()
    with stack:
        P = Prog(nc, stack)

        def tap(name, ap, shape, dt, bufs):
            if stage != name:
                return False
            dd = nc.dram_tensor("dbg", list(shape), dt, kind="ExternalOutput").ap()
            P.barrier()
            tok = P.op("sp", lambda e: e.dma_start(out=dd, in_=ap), reads=list(bufs), dma=True)
            P._wait("sp", tok)
            with nc.Block() as block:
                P.emit(block)
            return True

        ARENA_KB = 192
        arena = stack.enter_context(nc.sbuf_tensor("arena", [128, ARENA_KB * 512], BF16))

        def A(off_kb, nelem, dt=BF16):
            esz = 2 if dt == BF16 else 4
            a = arena[:, int(off_kb * 512): int(off_kb * 512) + nelem * esz // 2]
            return a if dt == BF16 else a.bitcast(dt)

        def sb(name, shape, dt):
            return stack.enter_context(nc.sbuf_tensor("t_" + name, list(shape), dt))

        PA = stack.enter_context(nc.psum_tensor("PA", [128, 2048], F32))
        PB = stack.enter_context(nc.psum_tensor("PB", [128, 2048], F32))
        bPA = [Buf("PA%d" % i, True) for i in range(4)]
        bPB = [Buf("PB%d" % i, True) for i in range(4)]
        banks = [(PA[:, 512 * i:512 * i + 512], bPA[i]) for i in range(4)] + \
                [(PB[:, 512 * i:512 * i + 512], bPB[i]) for i in range(4)]
        bank_rr = [0]

        def next_bank():
            b = banks[bank_rr[0] % 8]
            bank_rr[0] += 1
            return b

        cst = sb("cst", [128, 640], F32); b_cst = Buf("cst")
        ident = cst[:, 0:128]
        tri_f = cst[:, 128:256]
        iota_c = cst[:, 256:257]
        maskc = cst[:, 320:328]
        colsv = sb("colsv", [128, 96], F32); b_cols = Buf("cols")
        g_mix = colsv[:, 0:16]; g_mlp = colsv[:, 16:32]; g_fin = colsv[:, 32:48]
        g_oss = colsv[:, 48:56]; g_osg = colsv[:, 56:64]; glu_b = colsv[:, 64:72]; Dcol = colsv[:, 72:80]
        ones_bf = sb("ones_bf", [128, 128], BF16); b_ones = Buf("ones")
        tri_bf = sb("tri_bf", [128, 128], BF16); b_tri = Buf("tri")
        negiota = sb("negiota", [128, 1], F32); b_ni = Buf("ni")
        b_rstd = Buf("rstd"); b_rstd2 = Buf("rstd2"); b_sq = Buf("sq")
        a1 = sb("a1", [128, 2, 32], F32)
        a127 = sb("a127", [128, 2, 32], F32)
        a128 = sb("a128", [128, 2, 32], F32)
        b_a = Buf("apow")
        b_hl = Buf("hl"); b_hp = Buf("hp"); b_sm = Buf("sm")
        yl = sb("yl", [128, 2, 32], F32); b_yl = Buf("yl")
        H0 = sb("H0", [128, 2, 32], F32); b_H0 = Buf("H0")

        Em = A(0, 8192); EpT = A(16, 8192); Bbd = A(32, 8192); Cbd = A(48, 8192)
        Wb = A(96, 8192); XT = A(112, 8192)
        rstd = A(88, 1024, F32); sq = A(92, 1024); rstd2 = A(64, 1024, F32)
        hl = A(184, 9 * 64, F32).rearrange("p (c r a) -> p c r a", c=9, r=2)
        hp = A(186.25, 8 * 64, F32).rearrange("p (c r a) -> p c r a", c=8, r=2)
        sm = sb("sm", [128, 6, 2, 32], F32)
        hrun = sb("hrun", [128, 2, 2, 32], F32)
        b_Em, b_EpT, b_Bbd, b_Cbd, b_Wb, b_XT = [Buf(n) for n in ("Em", "EpT", "Bbd", "Cbd", "Wb", "XT")]
        hn = A(96, 16 * 1024).rearrange("p (k t) -> p k t", k=16); b_hn = [Buf("hn%d" % k) for k in range(16)]
        zs = A(128, 8192).rearrange("p (k t) -> p k t", k=8); b_zs = [Buf("zs%d" % k) for k in range(8)]
        zu = A(144, 8192).rearrange("p (k t) -> p k t", k=8); b_zu = [Buf("zu%d" % k) for k in range(8)]
        zv = A(160, 8192).rearrange("p (c d) -> p c d", c=8); b_zv = [Buf("zv%d" % k) for k in range(8)]
        ys = zs; b_ys = b_zs
        wsl = [A(o_, 2048).rearrange("p (k c) -> p k c", k=16) for o_ in (176, 180, 76)]
        b_wsl = [Buf("wsl%d" % i) for i in range(3)]
        xT0 = A(128, 16 * 1024, F32).rearrange("p (k t) -> p k t", k=16); b_x0 = [Buf("x0_%d" % k) for k in range(16)]

        P.op("sp", lambda e: e.dma_start(out=cst[:], in_=cst_d[:, :]), writes=[b_cst], dma=True)
        P.op("sp", lambda e: e.dma_start(out=colsv[:], in_=cols_d[:, :]), writes=[b_cols], dma=True)
        P.op("dve", lambda e: e.memset(ones_bf[:], 1.0), writes=[b_ones])
        P.op("dve", lambda e: e.tensor_copy(out=tri_bf[:], in_=tri_f), reads=[b_cst], writes=[b_tri])
        P.op("dve", lambda e: e.tensor_scalar(out=negiota[:], in0=iota_c, scalar1=-1.0, scalar2=None, op0=ALU.mult),
             reads=[b_cst], writes=[b_ni])

        def rmsnorm(src, b_src, gcols, dst, b_dst, nk, rs, b_rs, width):
            for k in range(nk):
                P.op("act", lambda e, k=k: e.activation(out=sq, in_=src[:, k, :], func=AF.Square),
                     reads=[b_src[k]], writes=[b_sq])
                for h in range(2):
                    P.op("pe", lambda e, k=k, h=h: e.matmul(PA[:, 512 * h:512 * h + 512], lhsT=ones_bf[:],
                                                          rhs=sq[:, 512 * h:512 * h + 512], start=(k == 0), stop=(k == nk - 1)),
                         reads=[b_sq, b_ones], writes=[bPA[h]])
            P.op("act", lambda e: e.activation(out=rs, in_=PA[:, 0:1024], func=AF.Sqrt, scale=1.0 / width, bias=eps_c[:]),
                 reads=[bPA[0], bPA[1], b_eps], writes=[b_rs])
            P.op("dve", lambda e: e.reciprocal(out=rs, in_=rs), reads=[b_rs], writes=[b_rs])
            for k in range(nk):
                P.op("dve", lambda e, k=k: e.scalar_tensor_tensor(out=dst[:, k, :], in0=src[:, k, :], scalar=gcols[:, k:k + 1],
                                                                  in1=rs, op0=ALU.mult, op1=ALU.mult),
                     reads=[b_src[k], b_rs, b_cols], writes=[b_dst[k]])

        eps_c = sb("eps_c", [128, 1], F32); b_eps = Buf("eps")
        P.op("dve", lambda e: e.memset(eps_c[:], EPS), writes=[b_eps])
        tg = [A(o_, 4096, F32) for o_ in (128, 144, 160, 176, 64, 80)]
        b_tg = [Buf("tg%d" % i) for i in range(6)]
        lr, th, ang, kf, sn, cs = tg
        b_lr, b_th, b_ang, b_kf, b_sn, b_cs = b_tg
        P.op("sp", lambda e: e.dma_start(out=lr, in_=pE_d[0:1, :].partition_broadcast(128).rearrange("p o f -> p (o f)")),
             writes=[b_lr], dma=True)
        P.op("sp", lambda e: e.dma_start(out=th, in_=pE_d[1:2, :].partition_broadcast(128).rearrange("p o f -> p (o f)")),
             writes=[b_th], dma=True)
        P.op("sp", lambda e: e.dma_start(out=ang, in_=pE_d[2:3, :].partition_broadcast(128).rearrange("p o f -> p (o f)")),
             writes=[b_ang], dma=True)
        P.op("act", lambda e: e.activation(out=ang, in_=ang, func=AF.Exp), reads=[b_ang], writes=[b_ang])
        if tap("tg_load", ang, [128, 4096], F32, [b_ang]):
            return nc
        P.op("dve", lambda e: e.tensor_tensor(out=lr, in0=lr, in1=ang, op=ALU.mult), reads=[b_lr, b_ang], writes=[b_lr])
        P.op("dve", lambda e: e.tensor_tensor(out=th, in0=th, in1=ang, op=ALU.mult), reads=[b_th, b_ang], writes=[b_th])
        P.op("dve", lambda e: e.tensor_scalar(out=ang, in0=th, scalar1=iota_c, scalar2=None, op0=ALU.mult),
             reads=[b_th, b_cst], writes=[b_ang])
        kint = sn.bitcast(I32)
        P.op("dve", lambda e: e.tensor_scalar(out=kint, in0=ang, scalar1=1.0 / TWO_PI, scalar2=None, op0=ALU.mult),
             reads=[b_ang], writes=[b_sn])
        P.op("dve", lambda e: e.tensor_copy(out=kf, in_=kint), reads=[b_sn], writes=[b_kf])
        P.op("dve", lambda e: e.scalar_tensor_tensor(out=ang, in0=kf, scalar=-TWO_PI, in1=ang, op0=ALU.mult, op1=ALU.add),
             reads=[b_kf, b_ang], writes=[b_ang])
        P.op("dve", lambda e: e.tensor_scalar(out=ang, in0=ang, scalar1=math.pi, scalar2=-math.pi, op0=ALU.min, op1=ALU.max),
             reads=[b_ang], writes=[b_ang])
        if tap("tg_ang", ang, [128, 4096], F32, [b_ang]):
            return nc
        P.op("act", lambda e: e.activation(out=sn, in_=ang, func=AF.Sin), reads=[b_ang], writes=[b_sn])
        P.op("act", lambda e: e.activation(out=cs, in_=ang, func=AF.Sin, scale=0.5), reads=[b_ang], writes=[b_cs])
        P.op("dve", lambda e: e.tensor_tensor(out=cs, in0=cs, in1=cs, op=ALU.mult), reads=[b_cs], writes=[b_cs])
        P.op("dve", lambda e: e.tensor_scalar(out=cs, in0=cs, scalar1=-2.0, scalar2=1.0, op0=ALU.mult, op1=ALU.add),
             reads=[b_cs], writes=[b_cs])
        if tap("tg_cs", cs, [128, 4096], F32, [b_cs]):
            return nc
        P.op("act", lambda e: e.activation(out=kf, in_=lr, func=AF.Exp, scale=iota_c), reads=[b_lr, b_cst], writes=[b_kf])
        if tap("tg_mag", kf, [128, 4096], F32, [b_kf]):
            return nc
        P.op("act", lambda e: e.activation(out=ang, in_=lr, func=AF.Exp, scale=negiota[:]), reads=[b_lr, b_ni, b_sn, b_cs],
             writes=[b_ang])
        Em4 = Em.rearrange("p (a r q) -> p a r q", a=32, r=2)
        v3 = lambda ap: ap.rearrange("p (a q) -> p a q", a=32)
        P.op("dve", lambda e: e.tensor_tensor(out=Em4[:, :, 0, :], in0=v3(ang), in1=v3(cs), op=ALU.mult),
             reads=[b_ang, b_cs], writes=[b_Em])
        P.op("dve", lambda e: e.scalar_tensor_tensor(out=Em4[:, :, 1, :], in0=v3(ang), scalar=-1.0, in1=v3(sn),
                                                      op0=ALU.mult, op1=ALU.mult), reads=[b_ang, b_sn], writes=[b_Em])
        P.op("dve", lambda e: e.tensor_tensor(out=th, in0=kf, in1=cs, op=ALU.mult), reads=[b_kf, b_cs, b_Em], writes=[b_th])
        P.op("dve", lambda e: e.tensor_tensor(out=ang, in0=kf, in1=sn, op=ALU.mult), reads=[b_kf, b_sn, b_Em], writes=[b_ang])
        if tap("tg_ep", ang, [128, 4096], F32, [b_ang]):
            return nc
        EpT4 = EpT.rearrange("p (a r t) -> p a r t", a=32, r=2)
        for ri, (src, bsrc) in enumerate(((th, b_th), (ang, b_ang))):
            for grp in range(2):
                for pj in range(16):
                    pair = grp * 16 + pj
                    P.op("pe", lambda e, pair=pair, pj=pj, src=src: e.transpose(
                        out=PA[:, pj * 128:(pj + 1) * 128], in_=src[:, pair * 128:(pair + 1) * 128], identity=ident),
                        reads=[bsrc, b_cst], writes=bPA)
                pav = PA.rearrange("p (a t) -> p a t", a=16)
                sl = slice(grp * 16, grp * 16 + 16)
                P.op("act", lambda e, sl=sl, ri=ri, pav=pav: e.activation(out=EpT4[:, sl, ri, :], in_=pav, func=AF.Copy),
                     reads=bPA, writes=[b_EpT])
                if stage == "EpT_noA":
                    continue
                P.op("dve", lambda e, sl=sl, ri=ri, pav=pav: e.tensor_copy(out=a1[:, ri, sl], in_=pav[:, :, 1]),
                     reads=bPA, writes=[b_a])
                P.op("dve", lambda e, sl=sl, ri=ri, pav=pav: e.tensor_copy(out=a127[:, ri, sl], in_=pav[:, :, 127]),
                     reads=bPA, writes=[b_a])
        if tap("EpT_noA", EpT, [128, 8192], BF16, [b_EpT]):
            return nc
        if tap("a1tap", a127[:, :, :], [128, 2, 32], F32, [b_a]):
            return nc
        a1r, a1i = a1[:, 0, :], a1[:, 1, :]
        a127r, a127i = a127[:, 0, :], a127[:, 1, :]
        a128r, a128i = a128[:, 0, :], a128[:, 1, :]

        def cmul(eng, outr, outi, ar, ai, br, bi, t1, t2, reads, writes):
            bt = Buf("cm")
            P.op(eng, lambda e: e.tensor_tensor(out=t1, in0=ai, in1=bi, op=ALU.mult), reads=reads, writes=[bt])
            P.op(eng, lambda e: e.tensor_tensor(out=t2, in0=ai, in1=br, op=ALU.mult), reads=reads, writes=[bt])
            P.op(eng, lambda e: e.tensor_tensor(out=outr, in0=ar, in1=br, op=ALU.mult), reads=reads + [bt], writes=writes)
            P.op(eng, lambda e: e.tensor_tensor(out=outi, in0=ar, in1=bi, op=ALU.mult), reads=reads + [bt], writes=writes)
            P.op(eng, lambda e: e.tensor_tensor(out=outr, in0=outr, in1=t1, op=ALU.subtract), reads=writes + [bt], writes=writes)
            P.op(eng, lambda e: e.tensor_tensor(out=outi, in0=outi, in1=t2, op=ALU.add), reads=writes + [bt], writes=writes)

        cmul("dve", a128r, a128i, a127r, a127i, a1r, a1i, sm[:, 0, 0, :], sm[:, 0, 1, :], [b_a], [b_a, b_sm])
        if tap("Em", Em, [128, 8192], BF16, [b_Em]):
            return nc
        if tap("EpT", EpT, [128, 8192], BF16, [b_EpT]):
            return nc
        if tap("a128", a128[:, :, :], [128, 2, 32], F32, [b_a]):
            return nc

        def bbar_calc(p_dram, B_dram):
            P.barrier()
            pB = A(128, 3 * 512, F32).rearrange("p (a f) -> p a f", a=3)
            BT = A(136, 2 * 512, F32).rearrange("p (a f) -> p a f", a=2)
            qq = A(144, 8 * 512, F32).rearrange("p (a f) -> p a f", a=8)
            b_pB = Buf("pB")
            P.op("sp", lambda e: e.dma_start(out=pB, in_=p_dram[:, :, :]), writes=[b_pB], dma=True)
            P.op("sp", lambda e: e.dma_start(out=BT, in_=B_dram[:, :, :]), writes=[b_pB], dma=True)
            are, aim, ldt = pB[:, 0, :], pB[:, 1, :], pB[:, 2, :]
            dtq, lrq, thq, magq, snq, csq, t1q, t2q = [qq[:, i, :] for i in range(8)]
            R, Wr = [b_pB], [b_pB]
            o = lambda fn: P.op("dve", fn, reads=R, writes=Wr)
            oa = lambda fn: P.op("act", fn, reads=R, writes=Wr)
            oa(lambda e: e.activation(out=dtq, in_=ldt, func=AF.Exp))
            o(lambda e: e.tensor_tensor(out=lrq, in0=are, in1=dtq, op=ALU.mult))
            o(lambda e: e.tensor_tensor(out=thq, in0=aim, in1=dtq, op=ALU.mult))
            oa(lambda e: e.activation(out=magq, in_=lrq, func=AF.Exp))
            kq = t1q.bitcast(I32)
            o(lambda e: e.tensor_scalar(out=kq, in0=thq, scalar1=1.0 / TWO_PI, scalar2=None, op0=ALU.mult))
            o(lambda e: e.tensor_copy(out=t2q, in_=kq))
            o(lambda e: e.scalar_tensor_tensor(out=thq, in0=t2q, scalar=-TWO_PI, in1=thq, op0=ALU.mult, op1=ALU.add))
            o(lambda e: e.tensor_scalar(out=thq, in0=thq, scalar1=math.pi, scalar2=-math.pi, op0=ALU.min, op1=ALU.max))
            oa(lambda e: e.activation(out=snq, in_=thq, func=AF.Sin))
            oa(lambda e: e.activation(out=csq, in_=thq, func=AF.Sin, scale=0.5))
            o(lambda e: e.tensor_tensor(out=csq, in0=csq, in1=csq, op=ALU.mult))
            o(lambda e: e.tensor_scalar(out=csq, in0=csq, scalar1=-2.0, scalar2=1.0, op0=ALU.mult, op1=ALU.add))
            o(lambda e: e.tensor_tensor(out=csq, in0=csq, in1=magq, op=ALU.mult))
            o(lambda e: e.tensor_scalar(out=csq, in0=csq, scalar1=-1.0, scalar2=None, op0=ALU.add))
            o(lambda e: e.tensor_tensor(out=snq, in0=snq, in1=magq, op=ALU.mult))
            o(lambda e: e.tensor_tensor(out=t1q, in0=are, in1=are, op=ALU.mult))
            o(lambda e: e.tensor_tensor(out=t2q, in0=aim, in1=aim, op=ALU.mult))
            o(lambda e: e.tensor_tensor(out=magq, in0=t1q, in1=t2q, op=ALU.add))
            o(lambda e: e.reciprocal(out=magq, in_=magq))
            o(lambda e: e.tensor_tensor(out=t1q, in0=csq, in1=are, op=ALU.mult))
            o(lambda e: e.tensor_tensor(out=t2q, in0=snq, in1=aim, op=ALU.mult))
            o(lambda e: e.tensor_tensor(out=lrq, in0=t1q, in1=t2q, op=ALU.add))
            o(lambda e: e.tensor_tensor(out=lrq, in0=lrq, in1=magq, op=ALU.mult))
            o(lambda e: e.tensor_tensor(out=t1q, in0=snq, in1=are, op=ALU.mult))
            o(lambda e: e.tensor_tensor(out=t2q, in0=csq, in1=aim, op=ALU.mult))
            o(lambda e: e.tensor_tensor(out=thq, in0=t1q, in1=t2q, op=ALU.subtract))
            o(lambda e: e.tensor_tensor(out=thq, in0=thq, in1=magq, op=ALU.mult))
            Br, Bi = BT[:, 0, :], BT[:, 1, :]
            o(lambda e: e.tensor_tensor(out=t1q, in0=lrq, in1=Br, op=ALU.mult))
            o(lambda e: e.tensor_tensor(out=t2q, in0=thq, in1=Bi, op=ALU.mult))
            o(lambda e: e.tensor_tensor(out=csq, in0=t1q, in1=t2q, op=ALU.subtract))
            o(lambda e: e.tensor_tensor(out=t1q, in0=lrq, in1=Bi, op=ALU.mult))
            o(lambda e: e.tensor_tensor(out=t2q, in0=thq, in1=Br, op=ALU.mult))
            o(lambda e: e.tensor_tensor(out=snq, in0=t1q, in1=t2q, op=ALU.add))

            return csq, snq, b_pB

        def gen_BC():
            csq, snq, b_pB = bbar_calc(pB_d, BT_d)
            Bbd6 = Bbd.rearrange("p (b a r g q) -> p b a r g q", b=8, a=4, r=2, g=2)
            for gl in range(8):
                pl, g2 = gl // 2, gl % 2
                for ri, src in enumerate((csq, snq)):
                    P.op("dve", lambda e, pl=pl, g2=g2, ri=ri, src=src, gl=gl: e.tensor_scalar(
                        out=Bbd6[:, :, pl, ri, g2, :], in0=src.rearrange("p (b q) -> p b q", b=8),
                        scalar1=maskc[:, gl:gl + 1], scalar2=None, op0=ALU.mult), reads=[b_pB, b_cst], writes=[b_Bbd])

            CT = A(160, 2 * 512, F32).rearrange("p (a f) -> p a f", a=2)
            b_CT = Buf("CT"); P.barrier()
            P.op("sp", lambda e: e.dma_start(out=CT, in_=CT_d[:, :, :]), writes=[b_CT], dma=True)
            P.op("pool", lambda e: e.memset(Cbd, 0.0), writes=[b_Cbd])
            Cbd5 = Cbd.rearrange("p (b a r c) -> p b a r c", b=8, a=4, r=2)
            CT5 = CT.rearrange("p r (b a h) -> p r b a h", b=8, a=4)
            for q4 in range(4):
                for ri in range(2):
                    for hf in range(2):
                        ps = slice(64 * hf, 64 * hf + 64)
                        c0 = 32 * q4 + 16 * hf
                        P.op("dve", lambda e, ps=ps, c0=c0, q4=q4, ri=ri: e.tensor_scalar(
                            out=Cbd5[ps, :, q4, ri, c0:c0 + 16], in0=CT5[ps, ri, :, q4, :],
                            scalar1=(1.0 if ri == 0 else -1.0), scalar2=None, op0=ALU.mult),
                            reads=[b_CT], writes=[b_Cbd])

            P.barrier()
        apw = sb("apw", [128, 4, 2, 32], F32); b_apw = Buf("apw")
        P.op("dve", lambda e: e.tensor_copy(out=apw[:, 0, :, :], in_=a128[:, :, :]), reads=[b_a], writes=[b_apw])
        for i in range(3):
            cmul("dve", apw[:, i + 1, 0, :], apw[:, i + 1, 1, :], apw[:, i, 0, :], apw[:, i, 1, :], apw[:, i, 0, :], apw[:, i, 1, :],
                 sm[:, 0, 0, :], sm[:, 0, 1, :], [b_apw], [b_apw, b_sm])
        a1024r, a1024i = apw[:, 3, 0, :], apw[:, 3, 1, :]
        P.op("dve", lambda e: e.memset(H0[:], 0.0), writes=[b_H0])

        csqT, snqT, b_pT = bbar_calc(pT_d, BTl_d)
        BbmR = A(64, 2048).rearrange("p (a r c) -> p a r c", a=32, r=2)
        BbmI = A(68, 2048).rearrange("p (a r c) -> p a r c", a=32, r=2)
        b_Bbm = Buf("Bbm")
        P.op("pool", lambda e: e.memset(A(64, 4096), 0.0), writes=[b_Bbm])
        for hf in range(2):
            ps_ = slice(64 * hf, 64 * hf + 64)
            cs_ = slice(16 * hf, 16 * hf + 16)
            br_ = csqT.rearrange("p (a h) -> p a h", a=32)
            bi_ = snqT.rearrange("p (a h) -> p a h", a=32)
            P.op("dve", lambda e: e.tensor_copy(out=BbmR[ps_, :, 0, cs_], in_=br_[ps_]), reads=[b_pT], writes=[b_Bbm])
            P.op("dve", lambda e: e.tensor_scalar(out=BbmR[ps_, :, 1, cs_], in0=bi_[ps_], scalar1=-1.0, scalar2=None, op0=ALU.mult),
                 reads=[b_pT], writes=[b_Bbm])
            P.op("dve", lambda e: e.tensor_copy(out=BbmI[ps_, :, 0, cs_], in_=bi_[ps_]), reads=[b_pT], writes=[b_Bbm])
            P.op("dve", lambda e: e.tensor_copy(out=BbmI[ps_, :, 1, cs_], in_=br_[ps_]), reads=[b_pT], writes=[b_Bbm])
        P.barrier()
        Wssm = A(32, 16 * 1024).rearrange("p (k c) -> p k c", k=16); b_Wssm = Buf("Wssm")
        w_in_v = w_in_d.rearrange("(k p) c -> p k c", p=128)
        for m in range(8):
            P.op("pool", lambda e, m=m: e.dma_start(out=Wssm[:, :, 128 * m:128 * m + 128], in_=w_in_v[:, :, 128 * m:128 * m + 128]),
                 writes=[b_Wssm], dma=True)
        for k in range(16):
            P.op("dve", lambda e, k=k: e.tensor_scalar(out=Wssm[:, k, :], in0=Wssm[:, k, :], scalar1=g_mix[:, k:k + 1], scalar2=None, op0=ALU.mult),
                 reads=[b_Wssm, b_cols], writes=[b_Wssm])
        AW = A(92, 512, F32).rearrange("p (c r a) -> p c r a", c=8, r=2); b_AW = Buf("AW")
        P.op("dve", lambda e: e.tensor_copy(out=AW[:, 7, :, :], in_=a127[:, :, :]), reads=[b_a], writes=[b_AW])
        for c in range(6, -1, -1):
            cmul("dve", AW[:, c, 0, :], AW[:, c, 1, :], a128r, a128i, AW[:, c + 1, 0, :], AW[:, c + 1, 1, :], sm[:, 0, 0, :], sm[:, 0, 1, :],
                 [b_a, b_AW], [b_AW, b_sm])
        xb = [A(128 + 32 * i, 16 * 1024).rearrange("p (k t) -> p k t", k=16) for i in range(2)]
        b_xb = [[Buf("xb%d_%d" % (i, k)) for k in range(16)] for i in range(2)]
        sqc = [A(96 + 4 * i, 2048).rearrange("p (k t) -> p k t", k=16) for i in range(2)]; b_sqc = [Buf("sqc0"), Buf("sqc1")]
        ztqs = [A(104 + 8 * i, 4096).rearrange("p (c f) -> p c f", c=4) for i in range(2)]
        b_ztqs = [[Buf("ztq%d_%d" % (i, c)) for c in range(4)] for i in range(2)]
        Vb = [A(120, 1024), A(122, 1024)]; b_Vb = [Buf("Vb0"), Buf("Vb1")]
        tVs = [A(124, 1024), A(126, 1024)]; b_tVs = [Buf("tV0"), Buf("tV1")]
        cmt = A(88, 4 * 256, F32).rearrange("p (j c a) -> p j c a", j=4, c=8); b_cmt = Buf("cmt")
        ylall2 = sb("ylall", [128, 2, 8, 2, 32], F32); b_ylall2 = [Buf("ylall0"), Buf("ylall1")]
        rc2 = sb("rc", [128, 2, 8], F32); b_rc2 = [Buf("rc0"), Buf("rc1")]
        Em3 = Em.rearrange("p (j q) -> p j q", j=64)
        ones_col = ones_bf[:, 0:1]
        xh = A(128, 16 * 512, F32).rearrange("p (k t) -> p k t", k=16)
        PBq = [PB[:, 1024 * i:1024 * i + 1024].rearrange("p (a r c f) -> p a r c f", a=4, r=2, c=4) for i in range(2)]
        rtmp = sb("rtmp", [128, 2, 8, 4], F32); b_rtmp = [Buf("rtmp0"), Buf("rtmp1")]
        tgc = [0]

        def load_prev(sh):
            i = sh % 2
            for k in range(16):
                P.op("pool", lambda e, k=k: e.dma_start(out=xb[i][:, k, :], in_=xTp_d[sh][k * 128:(k + 1) * 128, :]),
                     writes=[b_xb[i][k]], dma=True)

        def vprime_tg(gh, tg8):
            sh, half = divmod(gh, 2)
            ztq, b_ztq = ztqs[gh % 2], b_ztqs[gh % 2]
            ylall, b_ylall = ylall2[:, sh % 2], b_ylall2[sh % 2]
            st = tgc[0] % 2; tgc[0] += 1
            bks = [bPB[2 * st], bPB[2 * st + 1]]
            for pl in range(4):
                pr = 4 * tg8 + pl
                for ri in range(2):
                    j = 2 * pr + ri
                    P.op("pe", lambda e: e.matmul(PBq[st][:, pl, ri, :, :], lhsT=Em3[:, j, :], rhs=ztq[:, :, 32 * pr:32 * pr + 32],
                                                  start=True, stop=True), reads=[b_Em] + b_ztq, writes=[bks[pl // 2]])
            vb, bvb = Vb[st], b_Vb[st]
            P.op("act", lambda e: e.activation(out=vb, in_=PB[:, 1024 * st:1024 * st + 1024], func=AF.Copy), reads=bks, writes=[bvb])
            vb4 = vb.rearrange("p (q c f) -> p q c f", q=8, c=4)
            for ri, Bm in enumerate((BbmR, BbmI)):
                tv, btv = tVs[ri], b_tVs[ri]
                tv4 = tv.rearrange("p (q c f) -> p q c f", q=8, c=4)
                bm = Bm[:, 4 * tg8:4 * tg8 + 4, :, :].rearrange("p a r f -> p (a r) f").unsqueeze(2).broadcast_to([128, 8, 4, 32])
                P.op("dve", lambda e: e.tensor_tensor(out=tv4, in0=vb4, in1=bm, op=ALU.mult), reads=[bvb, b_Bbm], writes=[btv])
                P.op("dve", lambda e: e.tensor_reduce(out=rtmp[:, ri, :, :], in_=tv4, axis=AX.X, op=ALU.add),
                     reads=[btv], writes=[b_rtmp[ri]])
                r5 = rtmp[:, ri, :, :].rearrange("p (a r) c -> p a r c", r=2)
                P.op("dve", lambda e: e.tensor_tensor(out=ylall[:, 4 * half:4 * half + 4, ri, 4 * tg8:4 * tg8 + 4].rearrange("p c a -> p a c"),
                                                      in0=r5[:, :, 0, :], in1=r5[:, :, 1, :], op=ALU.add),
                     reads=[b_rtmp[ri]], writes=[b_ylall])

        def shard_chain(sh):
            own = (sh == 7)
            ylall, b_ylall = ylall2[:, sh % 2], b_ylall2[sh % 2]
            if own:
                if tap("ylall", ylall, [128, 8, 2, 32], F32, [b_ylall]):
                    return True
                P.barrier()
                P.op("dve", lambda e: e.memset(hl[:, 0, :, :], 0.0), writes=[b_hl])
                for c in range(8):
                    cmul("dve", sm[:, 2, 0, :], sm[:, 2, 1, :], a127r, a127i, ylall[:, c, 0, :], ylall[:, c, 1, :], sm[:, 0, 0, :], sm[:, 0, 1, :],
                         [b_a, b_ylall], [b_sm])
                    cmul("dve", sm[:, 3, 0, :], sm[:, 3, 1, :], a128r, a128i, hl[:, c, 0, :], hl[:, c, 1, :], sm[:, 1, 0, :], sm[:, 1, 1, :],
                         [b_a, b_hl], [b_sm])
                    P.op("dve", lambda e, c=c: e.tensor_tensor(out=hl[:, c + 1, :, :], in0=sm[:, 2, :, :], in1=sm[:, 3, :, :], op=ALU.add),
                         reads=[b_sm], writes=[b_hl])
            else:
                cmul("dve", cmt[:, 0, :, :], cmt[:, 1, :, :], AW[:, :, 0, :], AW[:, :, 1, :], ylall[:, :, 0, :], ylall[:, :, 1, :],
                     cmt[:, 2, :, :], cmt[:, 3, :, :], [b_AW, b_ylall], [b_cmt])
                P.op("dve", lambda e: e.tensor_reduce(out=sm[:, 4, 0, :], in_=cmt[:, 0, :, :].rearrange("p c a -> p a c"), axis=AX.X, op=ALU.add),
                     reads=[b_cmt], writes=[b_sm])
                P.op("dve", lambda e: e.tensor_reduce(out=sm[:, 4, 1, :], in_=cmt[:, 1, :, :].rearrange("p c a -> p a c"), axis=AX.X, op=ALU.add),
                     reads=[b_cmt], writes=[b_sm])
                cmul("dve", sm[:, 2, 0, :], sm[:, 2, 1, :], a1024r, a1024i, H0[:, 0, :], H0[:, 1, :], sm[:, 0, 0, :], sm[:, 0, 1, :],
                     [b_apw, b_H0], [b_sm])
                P.op("dve", lambda e: e.tensor_tensor(out=H0[:], in0=sm[:, 2, :, :], in1=sm[:, 4, :, :], op=ALU.add),
                     reads=[b_sm], writes=[b_H0])
            return False

        load_prev(0)
        for sh in range(8):
            own = (sh == 7)
            xs, b_xs = xb[sh % 2], b_xb[sh % 2]
            rc, b_rc = rc2[:, sh % 2, :], b_rc2[sh % 2]
            if own:
                for k in range(16):
                    P.op("sp", lambda e, k=k: e.dma_start(out=xh[:, k, :], in_=xT_d[k * 128:(k + 1) * 128, 0:512]),
                         writes=[b_x0[k], b_xb[0][k]], dma=True)
            for c in range(8):
                sq_, bsq_ = sqc[c % 2], b_sqc[c % 2]
                P.op("act", lambda e: e.activation(out=sq_, in_=xs[:, :, 128 * c:128 * c + 128], func=AF.Square), reads=b_xs, writes=[bsq_])
                for k in range(16):
                    P.op("pe", lambda e, k=k: e.matmul(PA[:, 1024 + c:1024 + c + 1], lhsT=sq_[:, k, :], rhs=ones_col,
                                                      start=(k == 0), stop=(k == 15)), reads=[bsq_, b_ones], writes=[bPA[2]])
            P.op("act", lambda e: e.activation(out=rc, in_=PA[:, 1024:1032], func=AF.Sqrt, scale=1.0 / D, bias=eps_c[:]),
                 reads=[bPA[2], b_eps], writes=[b_rc])
            P.op("dve", lambda e: e.reciprocal(out=rc, in_=rc), reads=[b_rc], writes=[b_rc])
            if sh + 1 < 8:
                load_prev(sh + 1)
            for half in range(2):
                gh = 2 * sh + half
                ztq, b_ztq = ztqs[gh % 2], b_ztqs[gh % 2]
                for cc in range(4):
                    c = 4 * half + cc
                    for h in range(2):
                        for k in range(16):
                            P.op("pe", lambda e, k=k, h=h: e.matmul(
                                PA[:, 512 * h:512 * h + 512], lhsT=xs[:, k, 128 * c:128 * c + 128], rhs=Wssm[:, k, 512 * h:512 * h + 512],
                                start=(k == 0), stop=(k == 15)), reads=[b_xs[k], b_Wssm], writes=[bPA[h]])
                    P.op("act", lambda e: e.activation(out=ztq[:, cc, :], in_=PA[:, 0:1024], func=AF.Identity, scale=rc[:, c:c + 1]),
                         reads=[bPA[0], bPA[1], b_rc], writes=[b_ztq[cc]])
                    if gh >= 1:
                        vprime_tg(gh - 1, 2 * cc)
                        vprime_tg(gh - 1, 2 * cc + 1)
                if gh >= 1 and (gh - 1) % 2 == 1:
                    if shard_chain((gh - 1) // 2):
                        return nc
        for tg8 in range(8):
            vprime_tg(15, tg8)
        if shard_chain(7):
            return nc
        P.barrier()
        for hf in range(2):
            tsl = slice(512 * hf, 512 * hf + 512)
            if hf == 1:
                for k in range(16):
                    P.op("sp", lambda e, k=k, tsl=tsl: e.dma_start(out=xh[:, k, :], in_=xT_d[k * 128:(k + 1) * 128, tsl]), writes=[b_x0[k]], dma=True)
            for k in range(16):
                P.op("act", lambda e, k=k: e.activation(out=sq[:, 0:512], in_=xh[:, k, :], func=AF.Square), reads=[b_x0[k]], writes=[b_sq])
                P.op("pe", lambda e, k=k: e.matmul(PA[:, 0:512], lhsT=ones_bf[:], rhs=sq[:, 0:512], start=(k == 0), stop=(k == 15)),
                     reads=[b_sq, b_ones], writes=[bPA[0]])
            P.op("act", lambda e: e.activation(out=rstd[:, 0:512], in_=PA[:, 0:512], func=AF.Sqrt, scale=1.0 / D, bias=eps_c[:]),
                 reads=[bPA[0], b_eps], writes=[b_rstd])
            P.op("dve", lambda e: e.reciprocal(out=rstd[:, 0:512], in_=rstd[:, 0:512]), reads=[b_rstd], writes=[b_rstd])
            for k in range(16):
                P.op("dve", lambda e, k=k, tsl=tsl: e.scalar_tensor_tensor(out=hn[:, k, tsl], in0=xh[:, k, :], scalar=g_mix[:, k:k + 1],
                                                                          in1=rstd[:, 0:512], op0=ALU.mult, op1=ALU.mult),
                     reads=[b_x0[k], b_rstd, b_cols], writes=[b_hn[k]])
        if tap("hn", hn, [128, 16, 1024], BF16, b_hn):
            return nc
        if tap("hl1", hl[:, 1, :, :], [128, 2, 32], F32, [b_hl]):
            return nc
        if tap("hl8", hl[:, 8, :, :], [128, 2, 32], F32, [b_hl]):
            return nc
        P.barrier()
        gen_BC()
        P.barrier()
        slot_i = [0]

        def load_slab(dview, nk=16):
            i = slot_i[0] % 3
            slot_i[0] += 1
            P.op("pool", lambda e: e.dma_start(out=wsl[i][:, 0:nk, :], in_=dview), writes=[b_wsl[i]], dma=True)
            return wsl[i], b_wsl[i]

        nblk = 24

        for m in range(nblk):
            slab, b_slab = load_slab(w_in_v[:, :, 128 * m:128 * m + 128])
            if m < 16:
                for h in range(2):
                    pb, bb = next_bank()
                    for k in range(16):
                        P.op("pe", lambda e, pb=pb, k=k, h=h, slab=slab: e.matmul(
                            pb, lhsT=slab[:, k, :], rhs=hn[:, k, 512 * h:512 * h + 512], start=(k == 0), stop=(k == 15)),
                            reads=[b_slab, b_hn[k]], writes=[bb])
                    if m < 8:
                        P.op("act", lambda e, pb=pb, m=m, h=h: e.activation(out=zs[:, m, 512 * h:512 * h + 512], in_=pb, func=AF.Copy),
                             reads=[bb], writes=[b_zs[m]])
                    else:
                        P.op("act", lambda e, pb=pb, m=m, h=h: e.activation(out=zu[:, m - 8, 512 * h:512 * h + 512], in_=pb,
                                                                         func=AF.Gelu_apprx_tanh), reads=[bb], writes=[b_zu[m - 8]])
            else:
                mv = m - 16
                for cg in range(2):
                    pb, bb = next_bank()
                    for cj in range(4):
                        c = cg * 4 + cj
                        for k in range(16):
                            P.op("pe", lambda e, pb=pb, k=k, c=c, cj=cj, slab=slab: e.matmul(
                                pb[:, 128 * cj:128 * cj + 128], lhsT=hn[:, k, 128 * c:128 * c + 128], rhs=slab[:, k, :],
                                start=(k == 0), stop=(k == 15)), reads=[b_slab, b_hn[k]], writes=[bb])
                    P.op("act", lambda e, pb=pb, cg=cg, mv=mv: e.activation(
                        out=zv[:, 4 * cg:4 * cg + 4, 128 * mv:128 * mv + 128], in_=pb.rearrange("p (c d) -> p c d", c=4),
                        func=AF.Gelu_apprx_tanh), reads=[bb], writes=b_zv[4 * cg:4 * cg + 4])

        if tap("zs", zs, [128, 8, 1024], BF16, b_zs):
            return nc
        P.barrier()
        tmp4 = [A(64 + 4 * i, 1024, F32) for i in range(4)]
        b_tmp4 = [Buf("tmp4_%d" % i) for i in range(4)]
        Em5 = Em.rearrange("p (g a r q) -> p g a r q", g=4, a=8, r=2)
        Wb5 = Wb.rearrange("p (g a r q) -> p g a r q", g=4, a=8, r=2)
        EpT5 = EpT.rearrange("p (g a r t) -> p g a r t", g=4, a=8, r=2)
        XT5 = XT.rearrange("p (g a r t) -> p g a r t", g=4, a=8, r=2)
        XT3 = XT.rearrange("p (j t) -> p j t", j=64)
        Wb3 = Wb.rearrange("p (j q) -> p j q", j=64)
        PAv = PA.rearrange("p (a r q) -> p a r q", a=8, r=2)
        PBv = PB.rearrange("p (a r q) -> p a r q", a=8, r=2)
        v8 = lambda ap: ap.rearrange("p (a q) -> p a q", a=8)

        def step12(c):
            for g in range(4):
                PX, bPX, PXv = (PA, bPA, PAv) if g % 2 == 0 else (PB, bPB, PBv)
                for q in range(4):
                    blk, half = 2 * g + q // 2, q % 2
                    P.op("pe", lambda e, PX=PX, q=q, blk=blk, half=half: e.matmul(
                        PX[:, 512 * q:512 * q + 512], lhsT=zs[:, blk, 128 * c:128 * c + 128],
                        rhs=Bbd[:, blk * 1024 + half * 512: blk * 1024 + half * 512 + 512], start=True, stop=True),
                        reads=[b_zs[blk], b_Bbd], writes=[bPX[q]])
                Br_, Bi_ = PXv[:, :, 0, :], PXv[:, :, 1, :]
                Er_, Ei_ = Em5[:, g, :, 0, :], Em5[:, g, :, 1, :]
                t = [v8(x) for x in tmp4]
                P.op("dve", lambda e: e.tensor_tensor(out=t[0], in0=Br_, in1=Er_, op=ALU.mult), reads=bPX + [b_Em], writes=[b_tmp4[0]])
                P.op("dve", lambda e: e.tensor_tensor(out=t[1], in0=Bi_, in1=Ei_, op=ALU.mult), reads=bPX + [b_Em], writes=[b_tmp4[1]])
                P.op("dve", lambda e: e.tensor_tensor(out=t[2], in0=Bi_, in1=Er_, op=ALU.mult), reads=bPX + [b_Em], writes=[b_tmp4[2]])
                P.op("dve", lambda e: e.tensor_tensor(out=t[3], in0=Br_, in1=Ei_, op=ALU.mult), reads=bPX + [b_Em], writes=[b_tmp4[3]])
                P.op("pool", lambda e, g=g: e.tensor_tensor(out=Wb5[:, g, :, 0, :], in0=t[0], in1=t[1], op=ALU.subtract),
                     reads=[b_tmp4[0], b_tmp4[1]], writes=[b_Wb])
                P.op("pool", lambda e, g=g: e.tensor_tensor(out=Wb5[:, g, :, 1, :], in0=t[2], in1=t[3], op=ALU.add),
                     reads=[b_tmp4[2], b_tmp4[3]], writes=[b_Wb])

        if tap("H0", H0[:], [128, 2, 32], F32, [b_H0]):
            return nc
        hc = sb("hc", [128, 2, 32], F32); b_hc = Buf("hc")
        P.op("dve", lambda e: e.tensor_copy(out=hc[:], in_=H0[:]), reads=[b_H0], writes=[b_hc])
        for c in range(8):
            P.op("dve", lambda e, c=c: e.tensor_tensor(out=sm[:, 4, :, :], in0=hl[:, c, :, :], in1=hc[:], op=ALU.add),
                 reads=[b_hl, b_hc], writes=[b_sm])
            cmul("dve", hp[:, c, 0, :], hp[:, c, 1, :], a1r, a1i, sm[:, 4, 0, :], sm[:, 4, 1, :], sm[:, 0, 0, :], sm[:, 0, 1, :],
                 [b_a, b_sm], [b_hp, b_sm])
            if c < 7:
                cmul("dve", sm[:, 5, 0, :], sm[:, 5, 1, :], a128r, a128i, hc[:, 0, :], hc[:, 1, :], sm[:, 1, 0, :], sm[:, 1, 1, :],
                     [b_a, b_hc], [b_sm])
                P.op("dve", lambda e: e.tensor_copy(out=hc[:], in_=sm[:, 5, :, :]), reads=[b_sm], writes=[b_hc])

        ypre = sb("ypre", [128, 2, 128], F32); b_ypre = [Buf("ypre0"), Buf("ypre1")]
        W2 = [A(96 + 4 * i, 2048).rearrange("p (k a q) -> p k a q", k=4, a=4) for i in range(2)]; b_W2 = [Buf("W2_0"), Buf("W2_1")]
        X2 = [A(104 + 4 * i, 2048).rearrange("p (k a q) -> p k a q", k=4, a=4) for i in range(2)]; b_X2 = [Buf("X2_0"), Buf("X2_1")]
        Zs = [A(112 + 2 * i, 1024).rearrange("p (a r t) -> p a r t", a=4, r=2) for i in range(2)]; b_Zs = [Buf("Zs0"), Buf("Zs1")]
        Em6 = Em.rearrange("p (b a r q) -> p b a r q", b=8, a=4, r=2)
        EpT6 = EpT.rearrange("p (b a r t) -> p b a r t", b=8, a=4, r=2)
        def pb_views(i):
            st = i % 2
            PX, bPX = (PA, bPA) if st == 0 else (PB, bPB)
            return st, PX, bPX

        def stage0(i):
            c, blk = divmod(i, 8)
            st, PX, bPX = pb_views(i)
            PS1 = PX[:, 0:1024].rearrange("p (a r q) -> p a r q", a=4, r=2)
            for h in range(2):
                P.op("pe", lambda e: e.matmul(PX[:, 512 * h:512 * h + 512], lhsT=zs[:, blk, 128 * c:128 * c + 128],
                                              rhs=Bbd[:, blk * 1024 + h * 512: blk * 1024 + h * 512 + 512], start=True, stop=True),
                     reads=[b_zs[blk], b_Bbd], writes=[bPX[h]])
            Br_, Bi_ = PS1[:, :, 0, :], PS1[:, :, 1, :]
            Er_, Ei_ = Em6[:, blk, :, 0, :], Em6[:, blk, :, 1, :]
            w2, bw2 = W2[st], b_W2[st]
            rd = [bPX[0], bPX[1], b_Em]
            P.op("dve", lambda e: e.tensor_tensor(out=w2[:, 0], in0=Br_, in1=Er_, op=ALU.mult), reads=rd, writes=[bw2])
            P.op("dve", lambda e: e.scalar_tensor_tensor(out=w2[:, 1], in0=Bi_, scalar=-1.0, in1=Ei_, op0=ALU.mult, op1=ALU.mult), reads=rd, writes=[bw2])
            P.op("dve", lambda e: e.tensor_tensor(out=w2[:, 2], in0=Bi_, in1=Er_, op=ALU.mult), reads=rd, writes=[bw2])
            P.op("dve", lambda e: e.tensor_tensor(out=w2[:, 3], in0=Br_, in1=Ei_, op=ALU.mult), reads=rd, writes=[bw2])

        def stage1(i):
            c, blk = divmod(i, 8)
            st, PX, bPX = pb_views(i)
            w2, bw2 = W2[st], b_W2[st]
            for a in range(4):
                for ri in range(2):
                    jj = 2 * a + ri
                    for q2 in range(2):
                        P.op("pe", lambda e: e.matmul(PX[:, 1024 + 128 * jj:1024 + 128 * jj + 128], lhsT=w2[:, 2 * ri + q2, a, :], rhs=tri_bf[:],
                                                      start=(q2 == 0), stop=(q2 == 1)), reads=[bw2, b_tri], writes=[bPX[2 + jj // 4]])
            zsb, bz = Zs[st], b_Zs[st]
            for a in range(4):
                for ri in range(2):
                    jj = 2 * a + ri
                    pr = 4 * blk + a
                    P.op("act", lambda e: e.activation(out=zsb[:, a, ri, :], in_=PX[:, 1024 + 128 * jj:1024 + 128 * jj + 128], func=AF.Identity,
                                                       bias=hp[:, c, ri, pr:pr + 1]), reads=[bPX[2 + jj // 4], b_hp], writes=[bz])
            Zr, Zi = zsb[:, :, 0, :], zsb[:, :, 1, :]
            Fr_, Fi_ = EpT6[:, blk, :, 0, :], EpT6[:, blk, :, 1, :]
            x2, bx2 = X2[st], b_X2[st]
            rd2 = [bz, b_EpT]
            P.op("dve", lambda e: e.tensor_tensor(out=x2[:, 0], in0=Zr, in1=Fr_, op=ALU.mult), reads=rd2, writes=[bx2])
            P.op("dve", lambda e: e.scalar_tensor_tensor(out=x2[:, 1], in0=Zi, scalar=-1.0, in1=Fi_, op0=ALU.mult, op1=ALU.mult), reads=rd2, writes=[bx2])
            P.op("dve", lambda e: e.tensor_tensor(out=x2[:, 2], in0=Zi, in1=Fr_, op=ALU.mult), reads=rd2, writes=[bx2])
            P.op("dve", lambda e: e.tensor_tensor(out=x2[:, 3], in0=Zr, in1=Fi_, op=ALU.mult), reads=rd2, writes=[bx2])

        def stage2(i):
            c, blk = divmod(i, 8)
            st, PX, bPX = pb_views(i)
            x2, bx2 = X2[st], b_X2[st]
            PS3 = PX[:, 1024:1152]
            n_mm = 0
            for a in range(4):
                for k4 in range(4):
                    j = (4 * blk + a) * 2 + (k4 // 2)
                    P.op("pe", lambda e: e.matmul(PS3, lhsT=Cbd[:, j * 128:(j + 1) * 128], rhs=x2[:, k4, a, :], start=(n_mm == 0), stop=(n_mm == 15)),
                         reads=[b_Cbd, bx2], writes=[bPX[2]])
                    n_mm += 1
            P.op("dve", lambda e: e.scalar_tensor_tensor(out=ypre[:, st, :], in0=zs[:, blk, 128 * c:128 * c + 128], scalar=Dcol[:, blk:blk + 1], in1=PS3,
                                                         op0=ALU.mult, op1=ALU.add), reads=[b_zs[blk], bPX[2], b_cols], writes=[b_ypre[st]])
            P.op("act", lambda e: e.activation(out=ys[:, blk, 128 * c:128 * c + 128], in_=ypre[:, st, :], func=AF.Gelu_apprx_tanh),
                 reads=[b_ypre[st]], writes=[b_ys[blk]])

        NI = 64
        for step in range(NI + 2):
            if step < NI:
                stage0(step)
            if 0 <= step - 1 < NI:
                stage1(step - 1)
            if 0 <= step - 2 < NI:
                stage2(step - 2)

        if tap("ys", ys, [128, 8, 1024], BF16, b_ys):
            return nc
        mixed = hn; b_mixed = b_hn
        glu_v = glu_d.rearrange("(k p) c -> p k c", p=128)
        P.barrier()
        gate = A(94, 512); b_gate = Buf("gate")
        for m in range(8):
            slab, b_slab = load_slab(glu_v[:, :, 128 * m:128 * m + 128], 8)
            for h in range(2):
                pb, bb = next_bank()
                for k in range(8):
                    P.op("pe", lambda e, pb=pb, k=k, h=h, slab=slab: e.matmul(pb, lhsT=slab[:, k, :], rhs=ys[:, k, 512 * h:512 * h + 512],
                                                                           start=(k == 0), stop=(k == 7)), reads=[b_slab, b_ys[k]], writes=[bb])
                P.op("act", lambda e, pb=pb, m=m: e.activation(out=gate, in_=pb, func=AF.Sigmoid, bias=glu_b[:, m:m + 1]),
                     reads=[bb, b_cols], writes=[b_gate])
                P.op("dve", lambda e, m=m, h=h: e.tensor_tensor(out=mixed[:, m, 512 * h:512 * h + 512], in0=ys[:, m, 512 * h:512 * h + 512],
                                                              in1=gate, op=ALU.mult), reads=[b_gate, b_ys[m]], writes=[b_mixed[m]])

        if tap("mixA", mixed[:, 0:8, :], [128, 8, 1024], BF16, b_mixed):
            return nc
        lnrows = A(84, 2048).rearrange("p (a f) -> p a f", a=2); b_lnrows = Buf("lnrows")
        for a_ in range(2):
            P.op("pool", lambda e, a_=a_: e.dma_start(out=lnrows[:, a_, :], in_=rows_d[a_:a_ + 1, :].partition_broadcast(128).rearrange("p o f -> p (o f)")),
                 writes=[b_lnrows], dma=True)
        wmT = A(68, 1024).rearrange("p (a t) -> p a t", a=8); b_wmT = Buf("wmT")
        wtmp = A(64, 1024, F32).rearrange("p (a t) -> p a t", a=8)
        b_wtmp = Buf("wtmp")
        P.op("sp", lambda e: e.dma_start(out=wtmp, in_=sguw_d[:, :, :]), writes=[b_wtmp], dma=True)
        P.op("dve", lambda e: e.tensor_tensor(out=wmT, in0=wtmp, in1=tri_f.unsqueeze(1).broadcast_to([128, 8, 128]), op=ALU.mult),
             reads=[b_wtmp, b_cst], writes=[b_wmT])
        bsrow = A(70, 1024)[0:1, :]; b_bsrow = Buf("bsrow")
        P.op("pool", lambda e: e.dma_start(out=bsrow, in_=sgub_d[:, :]), writes=[b_bsrow], dma=True)
        stats = sb("stats", [128, 2, 6], F32); b_stats = Buf("stats")
        mv_ = sb("mv", [128, 2], F32); b_mv = Buf("mv")
        for c in range(8):
            for h in range(2):
                P.op("dve", lambda e, c=c, h=h: e.bn_stats(out=stats[:, h, :], in_=zv[:, c, 512 * h:512 * h + 512]),
                     reads=[b_zv[c]], writes=[b_stats])
            P.op("dve", lambda e: e.bn_aggr(out=mv_[:], in_=stats[:].rearrange("p a s -> p (a s)")), reads=[b_stats], writes=[b_mv])
            P.op("act", lambda e: e.activation(out=mv_[:, 1:2], in_=mv_[:, 1:2], func=AF.Sqrt, bias=eps_c[:]), reads=[b_mv, b_eps], writes=[b_mv])
            P.op("dve", lambda e: e.reciprocal(out=mv_[:, 1:2], in_=mv_[:, 1:2]), reads=[b_mv], writes=[b_mv])
            P.op("dve", lambda e, c=c: e.tensor_scalar(out=zv[:, c, :], in0=zv[:, c, :], scalar1=mv_[:, 0:1], scalar2=mv_[:, 1:2],
                                                      op0=ALU.subtract, op1=ALU.mult), reads=[b_mv, b_zv[c]], writes=[b_zv[c]])
            P.op("pool", lambda e, c=c: e.tensor_tensor(out=zv[:, c, :], in0=zv[:, c, :], in1=lnrows[:, 0, :], op=ALU.mult),
                 reads=[b_lnrows, b_zv[c]], writes=[b_zv[c]])
            P.op("pool", lambda e, c=c: e.tensor_tensor(out=zv[:, c, :], in0=zv[:, c, :], in1=lnrows[:, 1, :], op=ALU.add),
                 reads=[b_lnrows, b_zv[c]], writes=[b_zv[c]])
        for hd in range(8):
            for cg in range(2):
                pb, bb = next_bank()
                for cj in range(4):
                    c = cg * 4 + cj
                    P.op("pe", lambda e, pb=pb, cj=cj, c=c, hd=hd: e.matmul(pb[:, 128 * cj:128 * cj + 128], lhsT=zv[:, c, 128 * hd:128 * hd + 128],
                                                                         rhs=wmT[:, hd, :], start=True, stop=False),
                         reads=[b_zv[c], b_wmT], writes=[bb])
                    P.op("pe", lambda e, pb=pb, cj=cj, hd=hd: e.matmul(pb[:, 128 * cj:128 * cj + 128], lhsT=ones_bf[0:1, :],
                                                                    rhs=bsrow[0:1, 128 * hd:128 * hd + 128], start=False, stop=True),
                         reads=[b_ones, b_bsrow], writes=[bb])
                P.op("dve", lambda e, pb=pb, hd=hd, cg=cg: e.tensor_tensor(out=mixed[:, 8 + hd, 512 * cg:512 * cg + 512], in0=pb,
                                                                        in1=zu[:, hd, 512 * cg:512 * cg + 512], op=ALU.mult),
                     reads=[bb, b_zu[hd]], writes=[b_mixed[8 + hd]])

        if tap("mixB", mixed[:, 8:16, :], [128, 8, 1024], BF16, b_mixed):
            return nc
        class V:
            def __init__(self, ap, off): self.ap, self.off = ap, off
            def __getitem__(self, idx): return self.ap[idx[0], idx[1] + self.off, idx[2]]
        P.barrier()
        rmsnorm(V(mixed, 0), b_mixed[0:8], g_oss, V(mixed, 0), b_mixed[0:8], 8, rstd, b_rstd, 1024.0)
        rmsnorm(V(mixed, 8), b_mixed[8:16], g_osg, V(mixed, 8), b_mixed[8:16], 8, rstd2, b_rstd2, 1024.0)

        if tap("mixN", mixed, [128, 16, 1024], BF16, b_mixed):
            return nc
        P.barrier()
        x1T = A(0, 16 * 1024, F32).rearrange("p (k t) -> p k t", k=16); b_x1 = [Buf("x1_%d" % k) for k in range(16)]
        w_out_v = w_out_d.rearrange("(k p) c -> p k c", p=128)
        for n in range(16):
            P.op("sp", lambda e, n=n: e.dma_start(out=x1T[:, n, :], in_=xT_d[n * 128:(n + 1) * 128, :]), writes=[b_x1[n]], dma=True)
        for n in range(16):
            slab, b_slab = load_slab(w_out_v[:, :, 128 * n:128 * n + 128])
            for h in range(2):
                pb, bb = next_bank()
                for k in range(16):
                    P.op("pe", lambda e, pb=pb, k=k, h=h, slab=slab: e.matmul(pb, lhsT=slab[:, k, :], rhs=mixed[:, k, 512 * h:512 * h + 512],
                                                                           start=(k == 0), stop=(k == 15)), reads=[b_slab, b_mixed[k]], writes=[bb])
                P.op("dve", lambda e, pb=pb, n=n, h=h: e.tensor_tensor(out=x1T[:, n, 512 * h:512 * h + 512], in0=pb,
                                                                     in1=x1T[:, n, 512 * h:512 * h + 512], op=ALU.add),
                     reads=[bb, b_x1[n]], writes=[b_x1[n]])

        if tap("x1", x1T, [128, 16, 1024], F32, b_x1):
            return nc
        hn2, b_hn2 = hn, b_hn
        rmsnorm(x1T, b_x1, g_mlp, hn2, b_hn2, 16, rstd, b_rstd, float(D))
        P.barrier()
        Hh = A(128, 64 * 512).rearrange("p (m t) -> p m t", m=64); b_H = [Buf("H%d" % m) for m in range(64)]
        usl = [A(64 + 4 * i, 2048).rearrange("p (k c) -> p k c", k=16) for i in range(2)]
        b_usl = [Buf("usl%d" % i) for i in range(2)]
        dsl = [A(72 + 8 * i, 4096).rearrange("p (k c) -> p k c", k=32) for i in range(2)]
        b_dsl = [Buf("dsl%d" % i) for i in range(2)]
        w_up_v = w_up_d.rearrange("(k p) c -> p k c", p=128)
        w_dn_v = w_dn_d.rearrange("(k p) c -> p k c", p=128)
        ui = [0]; di = [0]
        Hf = Hh.rearrange("p m t -> p (m t)").rearrange("p (m t) -> p m t", m=32)
        relu2 = [A(94, 512), A(95, 512)]; b_relu2 = [Buf("relu0"), Buf("relu1")]
        for hh in range(2):
            for mm in range(32):
                m = 32 * hh + mm
                i = ui[0] % 2; ui[0] += 1
                P.op("pool", lambda e, i=i, m=m: e.dma_start(out=usl[i], in_=w_up_v[:, :, 128 * m:128 * m + 128]), writes=[b_usl[i]], dma=True)
                for th_ in range(2):
                    ts = slice(512 * th_, 512 * th_ + 512)
                    pb, bb = next_bank()
                    for k in range(16):
                        P.op("pe", lambda e, pb=pb, k=k, i=i, ts=ts: e.matmul(pb, lhsT=usl[i][:, k, :], rhs=hn2[:, k, ts], start=(k == 0), stop=(k == 15)),
                             reads=[b_usl[i], b_hn2[k]], writes=[bb])
                    rt, b_rt = relu2[th_], b_relu2[th_]
                    P.op("act", lambda e, pb=pb, rt=rt: e.activation(out=rt, in_=pb, func=AF.Relu), reads=[bb], writes=[b_rt])
                    P.op("dve", lambda e, mm=mm, ts=ts, rt=rt: e.tensor_tensor(out=Hf[:, mm, ts], in0=rt, in1=rt, op=ALU.mult), reads=[b_rt], writes=[b_H[mm]])
            for n in range(16):
                i = di[0] % 2; di[0] += 1
                P.op("pool", lambda e, i=i, n=n, hh=hh: e.dma_start(out=dsl[i], in_=w_dn_v[:, 32 * hh:32 * hh + 32, 128 * n:128 * n + 128]),
                     writes=[b_dsl[i]], dma=True)
                for th_ in range(2):
                    ts = slice(512 * th_, 512 * th_ + 512)
                    pb, bb = next_bank()
                    for kk in range(32):
                        P.op("pe", lambda e, pb=pb, kk=kk, i=i, ts=ts: e.matmul(pb, lhsT=dsl[i][:, kk, :], rhs=Hf[:, kk, ts], start=(kk == 0), stop=(kk == 31)),
                             reads=[b_dsl[i], b_H[kk]], writes=[bb])
                    P.op("dve", lambda e, pb=pb, n=n, ts=ts: e.tensor_tensor(out=x1T[:, n, ts], in0=pb, in1=x1T[:, n, ts], op=ALU.add),
                         reads=[bb, b_x1[n]], writes=[b_x1[n]])

        if tap("x2", x1T, [128, 16, 1024], F32, b_x1):
            return nc
        P.barrier()
        rmsnorm(x1T, b_x1, g_fin, x1T, b_x1, 16, rstd, b_rstd, float(D))
        b_out = Buf("out")
        for n in range(16):
            P.op("sp", lambda e, n=n: e.dma_start(out=out_d[n * 128:(n + 1) * 128, :], in_=x1T[:, n, :]), reads=[b_x1[n]], dma=True)
        P.final_wait("sp", b_x1)
        with nc.Block() as block:
            P.emit(block)
    return nc


def _consts():
    c = np.zeros((128, 640), np.float32)
    c[:, 0:128] = np.eye(128, dtype=np.float32)
    s = np.arange(128)
    c[:, 128:256] = (s[:, None] <= s[None, :]).astype(np.float32)
    c[:, 256] = s.astype(np.float32)
    for gl in range(8):
        c[16 * gl:16 * gl + 16, 320 + gl] = 1.0
    return c


def _col(v):
    return np.ascontiguousarray(np.asarray(v, np.float32).reshape(-1, 128).T)


_NC_CACHE = {}


def _get_nc(mode):
    if mode not in _NC_CACHE:
        _NC_CACHE[mode] = build(mode)
    return _NC_CACHE[mode]


def make_in_maps(inp, mode, sall=None):
    f = lambda a: np.ascontiguousarray(np.asarray(a, np.float32))
    x = f(inp["x"])[0]
    a_re, a_im = f(inp["ssm_a_re"])[0], f(inp["ssm_a_im"])[0]
    ldt = f(inp["ssm_log_dt"])[0]
    Bre, Bim = f(inp["ssm_b_re"])[0], f(inp["ssm_b_im"])[0]
    Cre, Cim = f(inp["ssm_c_re"])[0], f(inp["ssm_c_im"])[0]
    ldt_rep = np.repeat(ldt[:, None], 64, axis=1)

    def blay(a_gp):
        t = a_gp.reshape(8, 8, 64)
        t = np.repeat(t[:, :, None, :], 16, axis=2)
        return np.ascontiguousarray(t.transpose(1, 2, 0, 3).reshape(128, 512))

    def blayB(b_gph):
        t = b_gph.reshape(8, 8, 64, 16)
        return np.ascontiguousarray(t.transpose(1, 3, 0, 2).reshape(128, 512))

    def clay(c_ghp):
        t = c_ghp.reshape(32, 2, 16, 64)
        return np.ascontiguousarray(t.transpose(1, 3, 0, 2).reshape(128, 512))

    def tlayp(a_gp):
        t = a_gp.reshape(32, 2, 64)
        t = np.repeat(t[:, :, :, None], 16, axis=3)
        return np.ascontiguousarray(t.transpose(1, 2, 0, 3).reshape(128, 512))

    def tlayB(b_gph):
        t = b_gph.reshape(32, 2, 64, 16)
        return np.ascontiguousarray(t.transpose(1, 2, 0, 3).reshape(128, 512))

    pT = np.ascontiguousarray(np.stack([tlayp(a_re), tlayp(a_im), tlayp(ldt_rep)], axis=1))
    BTl = np.ascontiguousarray(np.stack([tlayB(Bre), tlayB(Bim)], axis=1))
    pB = np.ascontiguousarray(np.stack([blay(a_re), blay(a_im), blay(ldt_rep)], axis=1))
    BT = np.ascontiguousarray(np.stack([blayB(Bre), blayB(Bim)], axis=1))
    pE = np.ascontiguousarray(np.stack([a_re.reshape(-1), a_im.reshape(-1), ldt_rep.reshape(-1)], axis=0))
    CT = np.ascontiguousarray(np.stack([clay(Cre), clay(Cim)], axis=1))
    cols = np.zeros((128, 96), np.float32)
    cols[:, 0:16] = _col(inp["norm_mix_g"][0]); cols[:, 16:32] = _col(inp["norm_mlp_g"][0]); cols[:, 32:48] = _col(inp["norm_final_g"])
    cols[:, 48:56] = _col(inp["out_norm_ssm_g"][0]); cols[:, 56:64] = _col(inp["out_norm_sgu_g"][0])
    cols[:, 64:72] = _col(inp["ssm_glu_b"][0]); cols[:, 72:80] = _col(f(inp["ssm_d"])[0].reshape(-1))
    common = {"w_in": f(inp["w_in"])[0], "cst": _consts(), "pB": pB, "BT": BT, "pE": pE, "CT": CT, "cols": cols, "pT": pT, "BTl": BTl}
    xTs = [np.ascontiguousarray(x[c * T:(c + 1) * T].T) for c in range(NC)]
    if mode == "A":
        return [dict(common, xT=xTs[c]) for c in range(NC)]
    rows = np.ascontiguousarray(np.stack([f(inp["sgu_ln_g"])[0], f(inp["sgu_ln_b"])[0]], axis=0))
    sguwT = np.ascontiguousarray(f(inp["sgu_w"])[0].transpose(2, 0, 1))
    sgub = np.ascontiguousarray(f(inp["sgu_b"])[0].reshape(1, 1024))
    commonB = dict(common, w_out=f(inp["w_out"])[0], w_up=f(inp["w_up"])[0], w_down=f(inp["w_down"])[0],
                   glu_w=f(inp["ssm_glu_w"])[0], rows=rows, sguwT=sguwT, sgub=sgub)
    zero = np.zeros((D, T), np.float32)
    in_maps = []
    for c in range(NC):
        xTp = np.ascontiguousarray(np.stack([xTs[c - 7 + j] if c - 7 + j >= 0 else zero for j in range(8)], axis=0))
        in_maps.append(dict(commonB, xT=xTs[c], xTp=xTp))
    return in_maps


def kernel(**inp):
    resB = run_bass_kernel_spmd(_get_nc("B"), make_in_maps(inp, "B"), core_ids=list(range(NC)))
    y = np.concatenate([resB.results[c]["yT"].T for c in range(NC)], axis=0)
    return np.ascontiguousarray(y.reshape(1, NC * T, D).astype(np.float32))
```

```python
import math
from contextlib import ExitStack
import numpy as np
import concourse.bass as bass
import concourse.mybir as mybir
from concourse.bass_utils import run_bass_kernel_spmd

F32 = mybir.dt.float32
BF16 = mybir.dt.bfloat16
I32 = mybir.dt.int32
AF = mybir.ActivationFunctionType
ALU = mybir.AluOpType
AX = mybir.AxisListType

NC = 8
T = 1024
D = 2048
KD = 16
DFF = 8192
EPS = 1e-6
TWO_PI = 2.0 * math.pi


class Tok:
    __slots__ = ("key", "sem", "val", "eng")

    def __init__(self, key, sem, val, eng):
        self.key, self.sem, self.val, self.eng = key, sem, val, eng


class Buf:
    def __init__(self, name="", excl=False):
        self.name = name
        self.excl = excl
        self.w = {}
        self.r = {}
        self.dsem = None


class _Rec:
    def __getattr__(self, name):
        def f(*a, **k):
            self.call = (name, a, k)
            return self
        return f


class Prog:
    ENGS = ["pe", "act", "dve", "pool", "sp"]

    def __init__(self, nc, stack):
        self.nc, self.stack = nc, stack
        self.ops = {e: [] for e in self.ENGS}
        self.esem = {e: stack.enter_context(nc.semaphore("es_" + e)) for e in self.ENGS}
        self.ecnt = {e: 0 for e in self.ENGS}
        self.waited = {}
        self.dcnt = {}
        self.dsems = {}
        self.nd = 0

    def barrier(self):
        toks = [Tok(f, self.esem[f], self.ecnt[f], f) for f in self.ENGS if self.ecnt[f] > 0]
        toks += [Tok(k, self.dsems[k], v, "dma") for k, v in self.dcnt.items()]
        for e in self.ENGS:
            for t in toks:
                if t.key != e:
                    self._wait(e, t)

    def _wait(self, eng, tok):
        k = (eng, tok.key)
        if self.waited.get(k, 0) >= tok.val:
            return
        self.waited[k] = tok.val
        sem, val = tok.sem, tok.val
        self.ops[eng].append(lambda e: e.wait_ge(sem, val))

    def alias(self, new, olds):
        for o in olds:
            for d in (o.w, o.r):
                for k, t in d.items():
                    if k not in new.w or new.w[k].val < t.val:
                        new.w[k] = t

    def op(self, eng, fn, reads=(), writes=(), dma=False):
        rec = _Rec()
        fn(rec)
        name_, a_, k_ = rec.call
        fn = lambda e: getattr(e, name_)(*a_, **k_)
        deps = []
        for b in reads:
            deps += list(b.w.values())
            if b.excl:
                deps += [t for t in b.r.values() if t.eng != eng or dma]
        for b in writes:
            deps += [t for t in list(b.w.values()) + list(b.r.values()) if t.eng != eng or dma]
        for t in deps:
            if eng == "pe" and t.eng == "pe" and not dma:
                continue
            self._wait(eng, t)
        if dma:
            tgt = writes[0] if writes else reads[0]
            if tgt.dsem is None:
                self.nd += 1
                tgt.dsem = ("d%d" % self.nd, self.stack.enter_context(self.nc.semaphore("ds%d" % self.nd)))
            key, sem = tgt.dsem
            self.dsems[key] = sem
            self.dcnt[key] = self.dcnt.get(key, 0) + 16
            tok = Tok(key, sem, self.dcnt[key], "dma")
            self.ops[eng].append(lambda e: fn(e).then_inc(sem, 16))
        else:
            self.ecnt[eng] += 1
            sem = self.esem[eng]
            tok = Tok(eng, sem, self.ecnt[eng], eng)
            self.ops[eng].append(lambda e: fn(e).then_inc(sem, 1))
        for b in reads:
            b.r[tok.key] = tok
        for b in writes:
            b.w[tok.key] = tok
            b.r = {}
        return tok

    def final_wait(self, eng, bufs):
        for b in bufs:
            for t in list(b.w.values()) + list(b.r.values()):
                self._wait(eng, t)

    def emit(self, block):
        m = {"pe": block.tensor, "act": block.scalar, "dve": block.vector, "pool": block.gpsimd, "sp": block.sync}
        for en in self.ENGS:
            ops = self.ops[en]

            def body(e, ops=ops):
                for f in ops:
                    f(e)

            m[en](body)


def build(mode, stage=None):
    nc = bass.Bass("TRN2", target_bir_lowering=False)
    full = True
    dbg_done = [False]

    def din(name, shape, dt=F32):
        return nc.dram_tensor(name, list(shape), dt, kind="ExternalInput").ap()

    xT_d = din("xT", [D, T])
    w_in_d = din("w_in", [D, 3072])
    cst_d = din("cst", [128, 640])
    pB_d = din("pB", [128, 3, 512])
    BT_d = din("BT", [128, 2, 512])
    pE_d = din("pE", [3, 4096])
    CT_d = din("CT", [128, 2, 512])
    cols_d = din("cols", [128, 96])
    if full:
        w_out_d = din("w_out", [D, D])
        w_up_d = din("w_up", [D, DFF])
        w_dn_d = din("w_down", [DFF, D])
        glu_d = din("glu_w", [1024, 1024])
        rows_d = din("rows", [2, 1024])
        sguw_d = din("sguwT", [128, 8, 128])
        sgub_d = din("sgub", [1, 1024])
        xTp_d = din("xTp", [8, D, T])
        pT_d = din("pT", [128, 3, 512])
        BTl_d = din("BTl", [128, 2, 512])
        out_d = nc.dram_tensor("yT", [D, T], F32, kind="ExternalOutput").ap()
    else:
        out_d = nc.dram_tensor("send", [128, 64], F32, kind="ExternalOutput").ap()

    stack = ExitStack()
    with stack:
        P = Prog(nc, stack)

        def tap(name, ap, shape, dt, bufs):
            if stage != name:
                return False
            dd = nc.dram_tensor("dbg", list(shape), dt, kind="ExternalOutput").ap()
            P.barrier()
            tok = P.op("sp", lambda e: e.dma_start(out=dd, in_=ap), reads=list(bufs), dma=True)
            P._wait("sp", tok)
            with nc.Block() as block:
                P.emit(block)
            return True

        ARENA_KB = 192
        arena = stack.enter_context(nc.sbuf_tensor("arena", [128, ARENA_KB * 512], BF16))

        def A(off_kb, nelem, dt=BF16):
            esz = 2 if dt == BF16 else 4
            a = arena[:, int(off_kb * 512): int(off_kb * 512) + nelem * esz // 2]
            return a if dt == BF16 else a.bitcast(dt)

        def sb(name, shape, dt):
            return stack.enter_context(nc.sbuf_tensor("t_" + name, list(shape), dt))

        PA = stack.enter_context(nc.psum_tensor("PA", [128, 2048], F32))
        PB = stack.enter_context(nc.psum_tensor("PB", [128, 2048], F32))
        bPA = [Buf("PA%d" % i, True) for i in range(4)]
        bPB = [Buf("PB%d" % i, True) for i in range(4)]
        banks = [(PA[:, 512 * i:512 * i + 512], bPA[i]) for i in range(4)] + \
                [(PB[:, 512 * i:512 * i + 512], bPB[i]) for i in range(4)]
        bank_rr = [0]

        def next_bank():
            b = banks[bank_rr[0] % 8]
            bank_rr[0] += 1
            return b

        cst = sb("cst", [128, 640], F32); b_cst = Buf("cst")
        ident = cst[:, 0:128]
        tri_f = cst[:, 128:256]
        iota_c = cst[:, 256:257]
        maskc = cst[:, 320:328]
        colsv = sb("colsv", [128, 96], F32); b_cols = Buf("cols")
        g_mix = colsv[:, 0:16]; g_mlp = colsv[:, 16:32]; g_fin = colsv[:, 32:48]
        g_oss = colsv[:, 48:56]; g_osg = colsv[:, 56:64]; glu_b = colsv[:, 64:72]; Dcol = colsv[:, 72:80]
        ones_bf = sb("ones_bf", [128, 128], BF16); b_ones = Buf("ones")
        tri_bf = sb("tri_bf", [128, 128], BF16); b_tri = Buf("tri")
        negiota = sb("negiota", [128, 1], F32); b_ni = Buf("ni")
        b_rstd = Buf("rstd"); b_rstd2 = Buf("rstd2"); b_sq = Buf("sq")
        a1 = sb("a1", [128, 2, 32], F32)
        a127 = sb("a127", [128, 2, 32], F32)
        a128 = sb("a128", [128, 2, 32], F32)
        b_a = Buf("apow")
        b_hl = Buf("hl"); b_hp = Buf("hp"); b_sm = Buf("sm")
        yl = sb("yl", [128, 2, 32], F32); b_yl = Buf("yl")
        H0 = sb("H0", [128, 2, 32], F32); b_H0 = Buf("H0")

        Em = A(0, 8192); EpT = A(16, 8192); Bbd = A(32, 8192); Cbd = A(48, 8192)
        Wb = A(96, 8192); XT = A(112, 8192)
        rstd = A(88, 1024, F32); sq = A(92, 1024); rstd2 = A(64, 1024, F32)
        hl = A(184, 9 * 64, F32).rearrange("p (c r a) -> p c r a", c=9, r=2)
        hp = A(186.25, 8 * 64, F32).rearrange("p (c r a) -> p c r a", c=8, r=2)
        sm = sb("sm", [128, 6, 2, 32], F32)
        hrun = sb("hrun", [128, 2, 2, 32], F32)
        b_Em, b_EpT, b_Bbd, b_Cbd, b_Wb, b_XT = [Buf(n) for n in ("Em", "EpT", "Bbd", "Cbd", "Wb", "XT")]
        hn = A(96, 16 * 1024).rearrange("p (k t) -> p k t", k=16); b_hn = [Buf("hn%d" % k) for k in range(16)]
        zs = A(128, 8192).rearrange("p (k t) -> p k t", k=8); b_zs = [Buf("zs%d" % k) for k in range(8)]
        zu = A(144, 8192).rearrange("p (k t) -> p k t", k=8); b_zu = [Buf("zu%d" % k) for k in range(8)]
        zv = A(160, 8192).rearrange("p (c d) -> p c d", c=8); b_zv = [Buf("zv%d" % k) for k in range(8)]
        ys = zs; b_ys = b_zs
        wsl = [A(o_, 2048).rearrange("p (k c) -> p k c", k=16) for o_ in (176, 180, 76)]
        b_wsl = [Buf("wsl%d" % i) for i in range(3)]
        xT0 = A(128, 16 * 1024, F32).rearrange("p (k t) -> p k t", k=16); b_x0 = [Buf("x0_%d" % k) for k in range(16)]

        P.op("sp", lambda e: e.dma_start(out=cst[:], in_=cst_d[:, :]), writes=[b_cst], dma=True)
        P.op("sp", lambda e: e.dma_start(out=colsv[:], in_=cols_d[:, :]), writes=[b_cols], dma=True)
        P.op("dve", lambda e: e.memset(ones_bf[:], 1.0), writes=[b_ones])
        P.op("dve", lambda e: e.tensor_copy(out=tri_bf[:], in_=tri_f), reads=[b_cst], writes=[b_tri])
        P.op("dve", lambda e: e.tensor_scalar(out=negiota[:], in0=iota_c, scalar1=-1.0, scalar2=None, op0=ALU.mult),
             reads=[b_cst], writes=[b_ni])

        def rmsnorm(src, b_src, gcols, dst, b_dst, nk, rs, b_rs, width):
            for k in range(nk):
                P.op("act", lambda e, k=k: e.activation(out=sq, in_=src[:, k, :], func=AF.Square),
                     reads=[b_src[k]], writes=[b_sq])
                for h in range(2):
                    P.op("pe", lambda e, k=k, h=h: e.matmul(PA[:, 512 * h:512 * h + 512], lhsT=ones_bf[:],
                                                          rhs=sq[:, 512 * h:512 * h + 512], start=(k == 0), stop=(k == nk - 1)),
                         reads=[b_sq, b_ones], writes=[bPA[h]])
            P.op("act", lambda e: e.activation(out=rs, in_=PA[:, 0:1024], func=AF.Sqrt, scale=1.0 / width, bias=eps_c[:]),
                 reads=[bPA[0], bPA[1], b_eps], writes=[b_rs])
            P.op("dve", lambda e: e.reciprocal(out=rs, in_=rs), reads=[b_rs], writes=[b_rs])
            for k in range(nk):
                P.op("dve", lambda e, k=k: e.scalar_tensor_tensor(out=dst[:, k, :], in0=src[:, k, :], scalar=gcols[:, k:k + 1],
                                                                  in1=rs, op0=ALU.mult, op1=ALU.mult),
                     reads=[b_src[k], b_rs, b_cols], writes=[b_dst[k]])

        eps_c = sb("eps_c", [128, 1], F32); b_eps = Buf("eps")
        P.op("dve", lambda e: e.memset(eps_c[:], EPS), writes=[b_eps])
        tg = [A(o_, 4096, F32) for o_ in (128, 144, 160, 176, 64, 80)]
        b_tg = [Buf("tg%d" % i) for i in range(6)]
        lr, th, ang, kf, sn, cs = tg
        b_lr, b_th, b_ang, b_kf, b_sn, b_cs = b_tg
        P.op("sp", lambda e: e.dma_start(out=lr, in_=pE_d[0:1, :].partition_broadcast(128).rearrange("p o f -> p (o f)")),
             writes=[b_lr], dma=True)
        P.op("sp", lambda e: e.dma_start(out=th, in_=pE_d[1:2, :].partition_broadcast(128).rearrange("p o f -> p (o f)")),
             writes=[b_th], dma=True)
        P.op("sp", lambda e: e.dma_start(out=ang, in_=pE_d[2:3, :].partition_broadcast(128).rearrange("p o f -> p (o f)")),
             writes=[b_ang], dma=True)
        P.op("act", lambda e: e.activation(out=ang, in_=ang, func=AF.Exp), reads=[b_ang], writes=[b_ang])
        if tap("tg_load", ang, [128, 4096], F32, [b_ang]):
            return nc
        P.op("dve", lambda e: e.tensor_tensor(out=lr, in0=lr, in1=ang, op=ALU.mult), reads=[b_lr, b_ang], writes=[b_lr])
        P.op("dve", lambda e: e.tensor_tensor(out=th, in0=th, in1=ang, op=ALU.mult), reads=[b_th, b_ang], writes=[b_th])
        P.op("dve", lambda e: e.tensor_scalar(out=ang, in0=th, scalar1=iota_c, scalar2=None, op0=ALU.mult),
             reads=[b_th, b_cst], writes=[b_ang])
        kint = sn.bitcast(I32)
        P.op("dve", lambda e: e.tensor_scalar(out=kint, in0=ang, scalar1=1.0 / TWO_PI, scalar2=None, op0=ALU.mult),
             reads=[b_ang], writes=[b_sn])
        P.op("dve", lambda e: e.tensor_copy(out=kf, in_=kint), reads=[b_sn], writes=[b_kf])
        P.op("dve", lambda e: e.scalar_tensor_tensor(out=ang, in0=kf, scalar=-TWO_PI, in1=ang, op0=ALU.mult, op1=ALU.add),
             reads=[b_kf, b_ang], writes=[b_ang])
        P.op("dve", lambda e: e.tensor_scalar(out=ang, in0=ang, scalar1=math.pi, scalar2=-math.pi, op0=ALU.min, op1=ALU.max),
             reads=[b_ang], writes=[b_ang])
        if tap("tg_ang", ang, [128, 4096], F32, [b_ang]):
            return nc
        P.op("act", lambda e: e.activation(out=sn, in_=ang, func=AF.Sin), reads=[b_ang], writes=[b_sn])
        P.op("act", lambda e: e.activation(out=cs, in_=ang, func=AF.Sin, scale=0.5), reads=[b_ang], writes=[b_cs])
        P.op("dve", lambda e: e.tensor_tensor(out=cs, in0=cs, in1=cs, op=ALU.mult), reads=[b_cs], writes=[b_cs])
        P.op("dve", lambda e: e.tensor_scalar(out=cs, in0=cs, scalar1=-2.0, scalar2=1.0, op0=ALU.mult, op1=ALU.add),
             reads=[b_cs], writes=[b_cs])
        if tap("tg_cs", cs, [128, 4096], F32, [b_cs]):
            return nc
        P.op("act", lambda e: e.activation(out=kf, in_=lr, func=AF.Exp, scale=iota_c), reads=[b_lr, b_cst], writes=[b_kf])
        if tap("tg_mag", kf, [128, 4096], F32, [b_kf]):
            return nc
        P.op("act", lambda e: e.activation(out=ang, in_=lr, func=AF.Exp, scale=negiota[:]), reads=[b_lr, b_ni, b_sn, b_cs],
             writes=[b_ang])
        Em4 = Em.rearrange("p (a r q) -> p a r q", a=32, r=2)
        v3 = lambda ap: ap.rearrange("p (a q) -> p a q", a=32)
        P.op("dve", lambda e: e.tensor_tensor(out=Em4[:, :, 0, :], in0=v3(ang), in1=v3(cs), op=ALU.mult),
             reads=[b_ang, b_cs], writes=[b_Em])
        P.op("dve", lambda e: e.scalar_tensor_tensor(out=Em4[:, :, 1, :], in0=v3(ang), scalar=-1.0, in1=v3(sn),
                                                      op0=ALU.mult, op1=ALU.mult), reads=[b_ang, b_sn], writes=[b_Em])
        P.op("dve", lambda e: e.tensor_tensor(out=th, in0=kf, in1=cs, op=ALU.mult), reads=[b_kf, b_cs, b_Em], writes=[b_th])
        P.op("dve", lambda e: e.tensor_tensor(out=ang, in0=kf, in1=sn, op=ALU.mult), reads=[b_kf, b_sn, b_Em], writes=[b_ang])
        if tap("tg_ep", ang, [128, 4096], F32, [b_ang]):
            return nc
        EpT4 = EpT.rearrange("p (a r t) -> p a r t", a=32, r=2)
        for ri, (src, bsrc) in enumerate(((th, b_th), (ang, b_ang))):
            for grp in range(2):
                for pj in range(16):
                    pair = grp * 16 + pj
                    P.op("pe", lambda e, pair=pair, pj=pj, src=src: e.transpose(
                        out=PA[:, pj * 128:(pj + 1) * 128], in_=src[:, pair * 128:(pair + 1) * 128], identity=ident),
                        reads=[bsrc, b_cst], writes=bPA)
                pav = PA.rearrange("p (a t) -> p a t", a=16)
                sl = slice(grp * 16, grp * 16 + 16)
                P.op("act", lambda e, sl=sl, ri=ri, pav=pav: e.activation(out=EpT4[:, sl, ri, :], in_=pav, func=AF.Copy),
                     reads=bPA, writes=[b_EpT])
                if stage == "EpT_noA":
                    continue
                P.op("dve", lambda e, sl=sl, ri=ri, pav=pav: e.tensor_copy(out=a1[:, ri, sl], in_=pav[:, :, 1]),
                     reads=bPA, writes=[b_a])
                P.op("dve", lambda e, sl=sl, ri=ri, pav=pav: e.tensor_copy(out=a127[:, ri, sl], in_=pav[:, :, 127]),
                     reads=bPA, writes=[b_a])
        if tap("EpT_noA", EpT, [128, 8192], BF16, [b_EpT]):
            return nc
        if tap("a1tap", a127[:, :, :], [128, 2, 32], F32, [b_a]):
            return nc
        a1r, a1i = a1[:, 0, :], a1[:, 1, :]
        a127r, a127i = a127[:, 0, :], a127[:, 1, :]
        a128r, a128i = a128[:, 0, :], a128[:, 1, :]

        def cmul(eng, outr, outi, ar, ai, br, bi, t1, t2, reads, writes):
            bt = Buf("cm")
            P.op(eng, lambda e: e.tensor_tensor(out=t1, in0=ai, in1=bi, op=ALU.mult), reads=reads, writes=[bt])
            P.op(eng, lambda e: e.tensor_tensor(out=t2, in0=ai, in1=br, op=ALU.mult), reads=reads, writes=[bt])
            P.op(eng, lambda e: e.tensor_tensor(out=outr, in0=ar, in1=br, op=ALU.mult), reads=reads + [bt], writes=writes)
            P.op(eng, lambda e: e.tensor_tensor(out=outi, in0=ar, in1=bi, op=ALU.mult), reads=reads + [bt], writes=writes)
            P.op(eng, lambda e: e.tensor_tensor(out=outr, in0=outr, in1=t1, op=ALU.subtract), reads=writes + [bt], writes=writes)
            P.op(eng, lambda e: e.tensor_tensor(out=outi, in0=outi, in1=t2, op=ALU.add), reads=writes + [bt], writes=writes)

        cmul("dve", a128r, a128i, a127r, a127i, a1r, a1i, sm[:, 0, 0, :], sm[:, 0, 1, :], [b_a], [b_a, b_sm])
        if tap("Em", Em, [128, 8192], BF16, [b_Em]):
            return nc
        if tap("EpT", EpT, [128, 8192], BF16, [b_EpT]):
            return nc
        if tap("a128", a128[:, :, :], [128, 2, 32], F32, [b_a]):
            return nc

        def bbar_calc(p_dram, B_dram):
            P.barrier()
            pB = A(128, 3 * 512, F32).rearrange("p (a f) -> p a f", a=3)
            BT = A(136, 2 * 512, F32).rearrange("p (a f) -> p a f", a=2)
            qq = A(144, 8 * 512, F32).rearrange("p (a f) -> p a f", a=8)
            b_pB = Buf("pB")
            P.op("sp", lambda e: e.dma_start(out=pB, in_=p_dram[:, :, :]), writes=[b_pB], dma=True)
            P.op("sp", lambda e: e.dma_start(out=BT, in_=B_dram[:, :, :]), writes=[b_pB], dma=True)
            are, aim, ldt = pB[:, 0, :], pB[:, 1, :], pB[:, 2, :]
            dtq, lrq, thq, magq, snq, csq, t1q, t2q = [qq[:, i, :] for i in range(8)]
            R, Wr = [b_pB], [b_pB]
            o = lambda fn: P.op("dve", fn, reads=R, writes=Wr)
            oa = lambda fn: P.op("act", fn, reads=R, writes=Wr)
            oa(lambda e: e.activation(out=dtq, in_=ldt, func=AF.Exp))
            o(lambda e: e.tensor_tensor(out=lrq, in0=are, in1=dtq, op=ALU.mult))
            o(lambda e: e.tensor_tensor(out=thq, in0=aim, in1=dtq, op=ALU.mult))
            oa(lambda e: e.activation(out=magq, in_=lrq, func=AF.Exp))
            kq = t1q.bitcast(I32)
            o(lambda e: e.tensor_scalar(out=kq, in0=thq, scalar1=1.0 / TWO_PI, scalar2=None, op0=ALU.mult))
            o(lambda e: e.tensor_copy(out=t2q, in_=kq))
            o(lambda e: e.scalar_tensor_tensor(out=thq, in0=t2q, scalar=-TWO_PI, in1=thq, op0=ALU.mult, op1=ALU.add))
            o(lambda e: e.tensor_scalar(out=thq, in0=thq, scalar1=math.pi, scalar2=-math.pi, op0=ALU.min, op1=ALU.max))
            oa(lambda e: e.activation(out=snq, in_=thq, func=AF.Sin))
            oa(lambda e: e.activation(out=csq, in_=thq, func=AF.Sin, scale=0.5))
            o(lambda e: e.tensor_tensor(out=csq, in0=csq, in1=csq, op=ALU.mult))
            o(lambda e: e.tensor_scalar(out=csq, in0=csq, scalar1=-2.0, scalar2=1.0, op0=ALU.mult, op1=ALU.add))
            o(lambda e: e.tensor_tensor(out=csq, in0=csq, in1=magq, op=ALU.mult))
            o(lambda e: e.tensor_scalar(out=csq, in0=csq, scalar1=-1.0, scalar2=None, op0=ALU.add))
            o(lambda e: e.tensor_tensor(out=snq, in0=snq, in1=magq, op=ALU.mult))
            o(lambda e: e.tensor_tensor(out=t1q, in0=are, in1=are, op=ALU.mult))
            o(lambda e: e.tensor_tensor(out=t2q, in0=aim, in1=aim, op=ALU.mult))
            o(lambda e: e.tensor_tensor(out=magq, in0=t1q, in1=t2q, op=ALU.add))
            o(lambda e: e.reciprocal(out=magq, in_=magq))
            o(lambda e: e.tensor_tensor(out=t1q, in0=csq, in1=are, op=ALU.mult))
            o(lambda e: e.tensor_tensor(out=t2q, in0=snq, in1=aim, op=ALU.mult))
            o(lambda e: e.tensor_tensor(out=lrq, in0=t1q, in1=t2q, op=ALU.add))
            o(lambda e: e.tensor_tensor(out=lrq, in0=lrq, in1=magq, op=ALU.mult))
            o(lambda e: e.tensor_tensor(out=t1q, in0=snq, in1=are, op=ALU.mult))
            o(lambda e: e.tensor_tensor(out=t2q, in0=csq, in1=aim, op=ALU.mult))
            o(lambda e: e.tensor_tensor(out=thq, in0=t1q, in1=t2q, op=ALU.subtract))
            o(lambda e: e.tensor_tensor(out=thq, in0=thq, in1=magq, op=ALU.mult))
            Br, Bi = BT[:, 0, :], BT[:, 1, :]
            o(lambda e: e.tensor_tensor(out=t1q, in0=lrq, in1=Br, op=ALU.mult))
            o(lambda e: e.tensor_tensor(out=t2q, in0=thq, in1=Bi, op=ALU.mult))
            o(lambda e: e.tensor_tensor(out=csq, in0=t1q, in1=t2q, op=ALU.subtract))
            o(lambda e: e.tensor_tensor(out=t1q, in0=lrq, in1=Bi, op=ALU.mult))
            o(lambda e: e.tensor_tensor(out=t2q, in0=thq, in1=Br, op=ALU.mult))
            o(lambda e: e.tensor_tensor(out=snq, in0=t1q, in1=t2q, op=ALU.add))

            return csq, snq, b_pB

        def gen_BC():
            csq, snq, b_pB = bbar_calc(pB_d, BT_d)
            Bbd6 = Bbd.rearrange("p (b a r g q) -> p b a r g q", b=8, a=4, r=2, g=2)
            for gl in range(8):
                pl, g2 = gl // 2, gl % 2
                for ri, src in enumerate((csq, snq)):
                    P.op("dve", lambda e, pl=pl, g2=g2, ri=ri, src=src, gl=gl: e.tensor_scalar(
                        out=Bbd6[:, :, pl, ri, g2, :], in0=src.rearrange("p (b q) -> p b q", b=8),
                        scalar1=maskc[:, gl:gl + 1], scalar2=None, op0=ALU.mult), reads=[b_pB, b_cst], writes=[b_Bbd])

            CT = A(160, 2 * 512, F32).rearrange("p (a f) -> p a f", a=2)
            b_CT = Buf("CT"); P.barrier()
            P.op("sp", lambda e: e.dma_start(out=CT, in_=CT_d[:, :, :]), writes=[b_CT], dma=True)
            P.op("pool", lambda e: e.memset(Cbd, 0.0), writes=[b_Cbd])
            Cbd5 = Cbd.rearrange("p (b a r c) -> p b a r c", b=8, a=4, r=2)
            CT5 = CT.rearrange("p r (b a h) -> p r b a h", b=8, a=4)
            for q4 in range(4):
                for ri in range(2):
                    for hf in range(2):
                        ps = slice(64 * hf, 64 * hf + 64)
                        c0 = 32 * q4 + 16 * hf
                        P.op("dve", lambda e, ps=ps, c0=c0, q4=q4, ri=ri: e.tensor_scalar(
                            out=Cbd5[ps, :, q4, ri, c0:c0 + 16], in0=CT5[ps, ri, :, q4, :],
                            scalar1=(1.0 if ri == 0 else -1.0), scalar2=None, op0=ALU.mult),
                            reads=[b_CT], writes=[b_Cbd])

            P.barrier()
        apw = sb("apw", [128, 4, 2, 32], F32); b_apw = Buf("apw")
        P.op("dve", lambda e: e.tensor_copy(out=apw[:, 0, :, :], in_=a128[:, :, :]), reads=[b_a], writes=[b_apw])
        for i in range(3):
            cmul("dve", apw[:, i + 1, 0, :], apw[:, i + 1, 1, :], apw[:, i, 0, :], apw[:, i, 1, :], apw[:, i, 0, :], apw[:, i, 1, :],
                 sm[:, 0, 0, :], sm[:, 0, 1, :], [b_apw], [b_apw, b_sm])
        a1024r, a1024i = apw[:, 3, 0, :], apw[:, 3, 1, :]
        P.op("dve", lambda e: e.memset(H0[:], 0.0), writes=[b_H0])

        csqT, snqT, b_pT = bbar_calc(pT_d, BTl_d)
        BbmR = A(64, 2048).rearrange("p (a r c) -> p a r c", a=32, r=2)
        BbmI = A(68, 2048).rearrange("p (a r c) -> p a r c", a=32, r=2)
        b_Bbm = Buf("Bbm")
        P.op("pool", lambda e: e.memset(A(64, 4096), 0.0), writes=[b_Bbm])
        for hf in range(2):
            ps_ = slice(64 * hf, 64 * hf + 64)
            cs_ = slice(16 * hf, 16 * hf + 16)
            br_ = csqT.rearrange("p (a h) -> p a h", a=32)
            bi_ = snqT.rearrange("p (a h) -> p a h", a=32)
            P.op("dve", lambda e: e.tensor_copy(out=BbmR[ps_, :, 0, cs_], in_=br_[ps_]), reads=[b_pT], writes=[b_Bbm])
            P.op("dve", lambda e: e.tensor_scalar(out=BbmR[ps_, :, 1, cs_], in0=bi_[ps_], scalar1=-1.0, scalar2=None, op0=ALU.mult),
                 reads=[b_pT], writes=[b_Bbm])
            P.op("dve", lambda e: e.tensor_copy(out=BbmI[ps_, :, 0, cs_], in_=bi_[ps_]), reads=[b_pT], writes=[b_Bbm])
            P.op("dve", lambda e: e.tensor_copy(out=BbmI[ps_, :, 1, cs_], in_=br_[ps_]), reads=[b_pT], writes=[b_Bbm])
        P.barrier()
        Wssm = A(32, 16 * 1024).rearrange("p (k c) -> p k c", k=16); b_Wssm = Buf("Wssm")
        w_in_v = w_in_d.rearrange("(k p) c -> p k c", p=128)
        for m in range(8):
            P.op("pool", lambda e, m=m: e.dma_start(out=Wssm[:, :, 128 * m:128 * m + 128], in_=w_in_v[:, :, 128 * m:128 * m + 128]),
                 writes=[b_Wssm], dma=True)
        for k in range(16):
            P.op("dve", lambda e, k=k: e.tensor_scalar(out=Wssm[:, k, :], in0=Wssm[:, k, :], scalar1=g_mix[:, k:k + 1], scalar2=None, op0=ALU.mult),
                 reads=[b_Wssm, b_cols], writes=[b_Wssm])
        AW = A(86, 512, F32).rearrange("p (c r a) -> p c r a", c=8, r=2); b_AW = Buf("AW")
        P.op("dve", lambda e: e.tensor_copy(out=AW[:, 7, :, :], in_=a127[:, :, :]), reads=[b_a], writes=[b_AW])
        for c in range(6, -1, -1):
            cmul("dve", AW[:, c, 0, :], AW[:, c, 1, :], a128r, a128i, AW[:, c + 1, 0, :], AW[:, c + 1, 1, :], sm[:, 0, 0, :], sm[:, 0, 1, :],
                 [b_a, b_AW], [b_AW, b_sm])
        xb = [A(128 + 32 * i, 16 * 1024).rearrange("p (k t) -> p k t", k=16) for i in range(2)]
        b_xb = [[Buf("xb%d_%d" % (i, k)) for k in range(16)] for i in range(2)]
        sqb = A(96, 16 * 1024).rearrange("p (k t) -> p k t", k=16); b_sqb = [Buf("sqb%d" % k) for k in range(16)]
        zt = [A(80 + 2 * i, 1024) for i in range(2)]; b_zt = [Buf("zt0"), Buf("zt1")]
        tVs = [A(84, 1024), A(72, 1024)]; b_tVs = [Buf("tV0"), Buf("tV1")]
        Vb = [A(92, 1024), A(94, 1024)]; b_Vb = [Buf("Vb0"), Buf("Vb1")]
        cmt = A(88, 4 * 256, F32).rearrange("p (j c a) -> p j c a", j=4, c=8); b_cmt = Buf("cmt")
        ylall = sb("ylall", [128, 8, 2, 32], F32); b_ylall = Buf("ylall")
        rc = sb("rc", [128, 8], F32); b_rc = Buf("rc")
        Em3 = Em.rearrange("p (j q) -> p j q", j=64)
        PBv4 = PB.rearrange("p (a r c) -> p a r c", a=32, r=2)
        ones_col = ones_bf[:, 0:1]

        def load_prev(sh):
            i = sh % 2
            for k in range(16):
                P.op("pool", lambda e, k=k: e.dma_start(out=xb[i][:, k, :], in_=xTp_d[sh][k * 128:(k + 1) * 128, :]),
                     writes=[b_xb[i][k]], dma=True)

        def vprime(sh, c):
            zi = c % 2
            for j in range(64):
                pr = j // 2
                P.op("pe", lambda e, j=j, pr=pr, zi=zi: e.matmul(PB[:, 32 * j:32 * j + 32], lhsT=Em3[:, j, :], rhs=zt[zi][:, 32 * pr:32 * pr + 32],
                                                               start=True, stop=True), reads=[b_Em, b_zt[zi]], writes=[bPB[j // 16]])
            for hq in range(2):
                rb = [bPB[2 * hq], bPB[2 * hq + 1]]
                asl = slice(16 * hq, 16 * hq + 16)
                vb, bvb = Vb[hq], b_Vb[hq]
                P.op("act", lambda e: e.activation(out=vb, in_=PB[:, 1024 * hq:1024 * hq + 1024], func=AF.Copy), reads=rb, writes=[bvb])
                vb4 = vb.rearrange("p (a r c) -> p a r c", a=16, r=2)
                for ri, Bm in enumerate((BbmR, BbmI)):
                    tv, btv = tVs[ri], b_tVs[ri]
                    tv4 = tv.rearrange("p (a r c) -> p a r c", a=16, r=2)
                    P.op("dve", lambda e: e.tensor_tensor(out=tv4, in0=vb4, in1=Bm[:, asl, :, :], op=ALU.mult),
                         reads=[bvb, b_Bbm], writes=[btv])
                    P.op("dve", lambda e: e.tensor_reduce(out=ylall[:, c, ri, asl], in_=tv4, axis=AX.XY, op=ALU.add),
                         reads=[btv], writes=[b_ylall])

        xh = A(128, 16 * 512, F32).rearrange("p (k t) -> p k t", k=16)
        load_prev(0)
        for sh in range(8):
            own = (sh == 7)
            xs, b_xs = xb[sh % 2], b_xb[sh % 2]
            if own:
                for k in range(16):
                    P.op("sp", lambda e, k=k: e.dma_start(out=xh[:, k, :], in_=xT_d[k * 128:(k + 1) * 128, 0:512]),
                         writes=[b_x0[k], b_xb[0][k]], dma=True)
            for k in range(16):
                P.op("act", lambda e, k=k: e.activation(out=sqb[:, k, :], in_=xs[:, k, :], func=AF.Square), reads=[b_xs[k]], writes=[b_sqb[k]])
            if sh + 1 < 8:
                load_prev(sh + 1)
            for c in range(8):
                for k in range(16):
                    P.op("pe", lambda e, k=k, c=c: e.matmul(PA[:, 1024 + c:1024 + c + 1], lhsT=sqb[:, k, 128 * c:128 * c + 128], rhs=ones_col,
                                                          start=(k == 0), stop=(k == 15)), reads=[b_sqb[k], b_ones], writes=[bPA[2]])
            P.op("act", lambda e: e.activation(out=rc[:], in_=PA[:, 1024:1032], func=AF.Sqrt, scale=1.0 / D, bias=eps_c[:]),
                 reads=[bPA[2], b_eps], writes=[b_rc])
            P.op("dve", lambda e: e.reciprocal(out=rc[:], in_=rc[:]), reads=[b_rc], writes=[b_rc])
            for c in range(8):
                zi = c % 2
                for h in range(2):
                    for k in range(16):
                        P.op("pe", lambda e, k=k, h=h, c=c: e.matmul(
                            PA[:, 512 * h:512 * h + 512], lhsT=xs[:, k, 128 * c:128 * c + 128], rhs=Wssm[:, k, 512 * h:512 * h + 512],
                            start=(k == 0), stop=(k == 15)), reads=[b_xs[k], b_Wssm], writes=[bPA[h]])
                P.op("act", lambda e, zi=zi, c=c: e.activation(out=zt[zi], in_=PA[:, 0:1024], func=AF.Identity, scale=rc[:, c:c + 1]),
                     reads=[bPA[0], bPA[1], b_rc], writes=[b_zt[zi]])
                if own and c == 0 and tap("zt0", zt[0], [128, 1024], BF16, [b_zt[0]]):
                    return nc
                if own and c == 0 and tap("rc", rc[:], [128, 8], F32, [b_rc]):
                    return nc
                if c > 0:
                    vprime(sh, c - 1)
            vprime(sh, 7)
            if own:
                P.barrier()
                P.op("dve", lambda e: e.memset(hl[:, 0, :, :], 0.0), writes=[b_hl])
                for c in range(8):
                    cmul("dve", sm[:, 2, 0, :], sm[:, 2, 1, :], a127r, a127i, ylall[:, c, 0, :], ylall[:, c, 1, :], sm[:, 0, 0, :], sm[:, 0, 1, :],
                         [b_a, b_ylall], [b_sm])
                    cmul("dve", sm[:, 3, 0, :], sm[:, 3, 1, :], a128r, a128i, hl[:, c, 0, :], hl[:, c, 1, :], sm[:, 1, 0, :], sm[:, 1, 1, :],
                         [b_a, b_hl], [b_sm])
                    P.op("dve", lambda e, c=c: e.tensor_tensor(out=hl[:, c + 1, :, :], in0=sm[:, 2, :, :], in1=sm[:, 3, :, :], op=ALU.add),
                         reads=[b_sm], writes=[b_hl])
            else:
                cmul("dve", cmt[:, 0, :, :], cmt[:, 1, :, :], AW[:, :, 0, :], AW[:, :, 1, :], ylall[:, :, 0, :], ylall[:, :, 1, :],
                     cmt[:, 2, :, :], cmt[:, 3, :, :], [b_AW, b_ylall], [b_cmt])
                P.op("dve", lambda e: e.tensor_reduce(out=sm[:, 4, 0, :], in_=cmt[:, 0, :, :].rearrange("p c a -> p a c"), axis=AX.X, op=ALU.add),
                     reads=[b_cmt], writes=[b_sm])
                P.op("dve", lambda e: e.tensor_reduce(out=sm[:, 4, 1, :], in_=cmt[:, 1, :, :].rearrange("p c a -> p a c"), axis=AX.X, op=ALU.add),
                     reads=[b_cmt], writes=[b_sm])
                cmul("dve", sm[:, 2, 0, :], sm[:, 2, 1, :], a1024r, a1024i, H0[:, 0, :], H0[:, 1, :], sm[:, 0, 0, :], sm[:, 0, 1, :],
                     [b_apw, b_H0], [b_sm])
                P.op("dve", lambda e: e.tensor_tensor(out=H0[:], in0=sm[:, 2, :, :], in1=sm[:, 4, :, :], op=ALU.add),
                     reads=[b_sm], writes=[b_H0])
        P.barrier()
        for hf in range(2):
            tsl = slice(512 * hf, 512 * hf + 512)
            if hf == 1:
                for k in range(16):
                    P.op("sp", lambda e, k=k, tsl=tsl: e.dma_start(out=xh[:, k, :], in_=xT_d[k * 128:(k + 1) * 128, tsl]), writes=[b_x0[k]], dma=True)
            for k in range(16):
                P.op("act", lambda e, k=k: e.activation(out=sq[:, 0:512], in_=xh[:, k, :], func=AF.Square), reads=[b_x0[k]], writes=[b_sq])
                P.op("pe", lambda e, k=k: e.matmul(PA[:, 0:512], lhsT=ones_bf[:], rhs=sq[:, 0:512], start=(k == 0), stop=(k == 15)),
                     reads=[b_sq, b_ones], writes=[bPA[0]])
            P.op("act", lambda e: e.activation(out=rstd[:, 0:512], in_=PA[:, 0:512], func=AF.Sqrt, scale=1.0 / D, bias=eps_c[:]),
                 reads=[bPA[0], b_eps], writes=[b_rstd])
            P.op("dve", lambda e: e.reciprocal(out=rstd[:, 0:512], in_=rstd[:, 0:512]), reads=[b_rstd], writes=[b_rstd])
            for k in range(16):
                P.op("dve", lambda e, k=k, tsl=tsl: e.scalar_tensor_tensor(out=hn[:, k, tsl], in0=xh[:, k, :], scalar=g_mix[:, k:k + 1],
                                                                          in1=rstd[:, 0:512], op0=ALU.mult, op1=ALU.mult),
                     reads=[b_x0[k], b_rstd, b_cols], writes=[b_hn[k]])
        if tap("hn", hn, [128, 16, 1024], BF16, b_hn):
            return nc
        if tap("hl8", hl[:, 8, :, :], [128, 2, 32], F32, [b_hl]):
            return nc
        P.barrier()
        gen_BC()
        P.barrier()
        slot_i = [0]

        def load_slab(dview, nk=16):
            i = slot_i[0] % 3
            slot_i[0] += 1
            P.op("pool", lambda e: e.dma_start(out=wsl[i][:, 0:nk, :], in_=dview), writes=[b_wsl[i]], dma=True)
            return wsl[i], b_wsl[i]

        nblk = 24

        for m in range(nblk):
            slab, b_slab = load_slab(w_in_v[:, :, 128 * m:128 * m + 128])
            if m < 16:
                for h in range(2):
                    pb, bb = next_bank()
                    for k in range(16):
                        P.op("pe", lambda e, pb=pb, k=k, h=h, slab=slab: e.matmul(
                            pb, lhsT=slab[:, k, :], rhs=hn[:, k, 512 * h:512 * h + 512], start=(k == 0), stop=(k == 15)),
                            reads=[b_slab, b_hn[k]], writes=[bb])
                    if m < 8:
                        P.op("act", lambda e, pb=pb, m=m, h=h: e.activation(out=zs[:, m, 512 * h:512 * h + 512], in_=pb, func=AF.Copy),
                             reads=[bb], writes=[b_zs[m]])
                    else:
                        P.op("act", lambda e, pb=pb, m=m, h=h: e.activation(out=zu[:, m - 8, 512 * h:512 * h + 512], in_=pb,
                                                                         func=AF.Gelu_apprx_tanh), reads=[bb], writes=[b_zu[m - 8]])
            else:
                mv = m - 16
                for cg in range(2):
                    pb, bb = next_bank()
                    for cj in range(4):
                        c = cg * 4 + cj
                        for k in range(16):
                            P.op("pe", lambda e, pb=pb, k=k, c=c, cj=cj, slab=slab: e.matmul(
                                pb[:, 128 * cj:128 * cj + 128], lhsT=hn[:, k, 128 * c:128 * c + 128], rhs=slab[:, k, :],
                                start=(k == 0), stop=(k == 15)), reads=[b_slab, b_hn[k]], writes=[bb])
                    P.op("act", lambda e, pb=pb, cg=cg, mv=mv: e.activation(
                        out=zv[:, 4 * cg:4 * cg + 4, 128 * mv:128 * mv + 128], in_=pb.rearrange("p (c d) -> p c d", c=4),
                        func=AF.Gelu_apprx_tanh), reads=[bb], writes=b_zv[4 * cg:4 * cg + 4])

        if tap("zs", zs, [128, 8, 1024], BF16, b_zs):
            return nc
        P.barrier()
        tmp4 = [A(64 + 4 * i, 1024, F32) for i in range(4)]
        b_tmp4 = [Buf("tmp4_%d" % i) for i in range(4)]
        Em5 = Em.rearrange("p (g a r q) -> p g a r q", g=4, a=8, r=2)
        Wb5 = Wb.rearrange("p (g a r q) -> p g a r q", g=4, a=8, r=2)
        EpT5 = EpT.rearrange("p (g a r t) -> p g a r t", g=4, a=8, r=2)
        XT5 = XT.rearrange("p (g a r t) -> p g a r t", g=4, a=8, r=2)
        XT3 = XT.rearrange("p (j t) -> p j t", j=64)
        Wb3 = Wb.rearrange("p (j q) -> p j q", j=64)
        PAv = PA.rearrange("p (a r q) -> p a r q", a=8, r=2)
        PBv = PB.rearrange("p (a r q) -> p a r q", a=8, r=2)
        v8 = lambda ap: ap.rearrange("p (a q) -> p a q", a=8)

        def step12(c):
            for g in range(4):
                PX, bPX, PXv = (PA, bPA, PAv) if g % 2 == 0 else (PB, bPB, PBv)
                for q in range(4):
                    blk, half = 2 * g + q // 2, q % 2
                    P.op("pe", lambda e, PX=PX, q=q, blk=blk, half=half: e.matmul(
                        PX[:, 512 * q:512 * q + 512], lhsT=zs[:, blk, 128 * c:128 * c + 128],
                        rhs=Bbd[:, blk * 1024 + half * 512: blk * 1024 + half * 512 + 512], start=True, stop=True),
                        reads=[b_zs[blk], b_Bbd], writes=[bPX[q]])
                Br_, Bi_ = PXv[:, :, 0, :], PXv[:, :, 1, :]
                Er_, Ei_ = Em5[:, g, :, 0, :], Em5[:, g, :, 1, :]
                t = [v8(x) for x in tmp4]
                P.op("dve", lambda e: e.tensor_tensor(out=t[0], in0=Br_, in1=Er_, op=ALU.mult), reads=bPX + [b_Em], writes=[b_tmp4[0]])
                P.op("dve", lambda e: e.tensor_tensor(out=t[1], in0=Bi_, in1=Ei_, op=ALU.mult), reads=bPX + [b_Em], writes=[b_tmp4[1]])
                P.op("dve", lambda e: e.tensor_tensor(out=t[2], in0=Bi_, in1=Er_, op=ALU.mult), reads=bPX + [b_Em], writes=[b_tmp4[2]])
                P.op("dve", lambda e: e.tensor_tensor(out=t[3], in0=Br_, in1=Ei_, op=ALU.mult), reads=bPX + [b_Em], writes=[b_tmp4[3]])
                P.op("pool", lambda e, g=g: e.tensor_tensor(out=Wb5[:, g, :, 0, :], in0=t[0], in1=t[1], op=ALU.subtract),
                     reads=[b_tmp4[0], b_tmp4[1]], writes=[b_Wb])
                P.op("pool", lambda e, g=g: e.tensor_tensor(out=Wb5[:, g, :, 1, :], in0=t[2], in1=t[3], op=ALU.add),
                     reads=[b_tmp4[2], b_tmp4[3]], writes=[b_Wb])

        if tap("H0", H0[:], [128, 2, 32], F32, [b_H0]):
            return nc
        hc = sb("hc", [128, 2, 32], F32); b_hc = Buf("hc")
        P.op("dve", lambda e: e.tensor_copy(out=hc[:], in_=H0[:]), reads=[b_H0], writes=[b_hc])
        for c in range(8):
            P.op("dve", lambda e, c=c: e.tensor_tensor(out=sm[:, 4, :, :], in0=hl[:, c, :, :], in1=hc[:], op=ALU.add),
                 reads=[b_hl, b_hc], writes=[b_sm])
            cmul("dve", hp[:, c, 0, :], hp[:, c, 1, :], a1r, a1i, sm[:, 4, 0, :], sm[:, 4, 1, :], sm[:, 0, 0, :], sm[:, 0, 1, :],
                 [b_a, b_sm], [b_hp, b_sm])
            if c < 7:
                cmul("dve", sm[:, 5, 0, :], sm[:, 5, 1, :], a128r, a128i, hc[:, 0, :], hc[:, 1, :], sm[:, 1, 0, :], sm[:, 1, 1, :],
                     [b_a, b_hc], [b_sm])
                P.op("dve", lambda e: e.tensor_copy(out=hc[:], in_=sm[:, 5, :, :]), reads=[b_sm], writes=[b_hc])

        ypre = sb("ypre", [128, 2, 128], F32); b_ypre = [Buf("ypre0"), Buf("ypre1")]
        W2 = [A(96 + 4 * i, 2048).rearrange("p (k a q) -> p k a q", k=4, a=4) for i in range(2)]; b_W2 = [Buf("W2_0"), Buf("W2_1")]
        X2 = [A(104 + 4 * i, 2048).rearrange("p (k a q) -> p k a q", k=4, a=4) for i in range(2)]; b_X2 = [Buf("X2_0"), Buf("X2_1")]
        Zs = [A(112 + 2 * i, 1024).rearrange("p (a r t) -> p a r t", a=4, r=2) for i in range(2)]; b_Zs = [Buf("Zs0"), Buf("Zs1")]
        Bsb = [A(116 + 2 * i, 1024) for i in range(2)]; b_Bsb = [Buf("Bsb0"), Buf("Bsb1")]
        Em6 = Em.rearrange("p (b a r q) -> p b a r q", b=8, a=4, r=2)
        EpT6 = EpT.rearrange("p (b a r t) -> p b a r t", b=8, a=4, r=2)
        def pb_views(i):
            st = i % 2
            PX, bPX = (PA, bPA) if st == 0 else (PB, bPB)
            return st, PX, bPX

        def stage0(i):
            c, blk = divmod(i, 8)
            st, PX, bPX = pb_views(i)
            PS1 = PX[:, 0:1024].rearrange("p (a r q) -> p a r q", a=4, r=2)
            for h in range(2):
                P.op("pe", lambda e: e.matmul(PX[:, 512 * h:512 * h + 512], lhsT=zs[:, blk, 128 * c:128 * c + 128],
                                              rhs=Bbd[:, blk * 1024 + h * 512: blk * 1024 + h * 512 + 512], start=True, stop=True),
                     reads=[b_zs[blk], b_Bbd], writes=[bPX[h]])
            bsb, bbsb = Bsb[st], b_Bsb[st]
            P.op("act", lambda e: e.activation(out=bsb, in_=PX[:, 0:1024], func=AF.Copy), reads=[bPX[0], bPX[1]], writes=[bbsb])
            bsb4 = bsb.rearrange("p (a r q) -> p a r q", a=4, r=2)
            Br_, Bi_ = bsb4[:, :, 0, :], bsb4[:, :, 1, :]
            Er_, Ei_ = Em6[:, blk, :, 0, :], Em6[:, blk, :, 1, :]
            w2, bw2 = W2[st], b_W2[st]
            rd = [bbsb, b_Em]
            P.op("dve", lambda e: e.tensor_tensor(out=w2[:, 0], in0=Br_, in1=Er_, op=ALU.mult), reads=rd, writes=[bw2])
            P.op("dve", lambda e: e.scalar_tensor_tensor(out=w2[:, 1], in0=Bi_, scalar=-1.0, in1=Ei_, op0=ALU.mult, op1=ALU.mult), reads=rd, writes=[bw2])
            P.op("dve", lambda e: e.tensor_tensor(out=w2[:, 2], in0=Bi_, in1=Er_, op=ALU.mult), reads=rd, writes=[bw2])
            P.op("dve", lambda e: e.tensor_tensor(out=w2[:, 3], in0=Br_, in1=Ei_, op=ALU.mult), reads=rd, writes=[bw2])

        def stage1(i):
            c, blk = divmod(i, 8)
            st, PX, bPX = pb_views(i)
            w2, bw2 = W2[st], b_W2[st]
            for a in range(4):
                for ri in range(2):
                    jj = 2 * a + ri
                    for q2 in range(2):
                        P.op("pe", lambda e: e.matmul(PX[:, 1024 + 128 * jj:1024 + 128 * jj + 128], lhsT=w2[:, 2 * ri + q2, a, :], rhs=tri_bf[:],
                                                      start=(q2 == 0), stop=(q2 == 1)), reads=[bw2, b_tri], writes=[bPX[2 + jj // 4]])
            zsb, bz = Zs[st], b_Zs[st]
            for a in range(4):
                for ri in range(2):
                    jj = 2 * a + ri
                    pr = 4 * blk + a
                    P.op("act", lambda e: e.activation(out=zsb[:, a, ri, :], in_=PX[:, 1024 + 128 * jj:1024 + 128 * jj + 128], func=AF.Identity,
                                                       bias=hp[:, c, ri, pr:pr + 1]), reads=[bPX[2 + jj // 4], b_hp], writes=[bz])
            Zr, Zi = zsb[:, :, 0, :], zsb[:, :, 1, :]
            Fr_, Fi_ = EpT6[:, blk, :, 0, :], EpT6[:, blk, :, 1, :]
            x2, bx2 = X2[st], b_X2[st]
            rd2 = [bz, b_EpT]
            P.op("dve", lambda e: e.tensor_tensor(out=x2[:, 0], in0=Zr, in1=Fr_, op=ALU.mult), reads=rd2, writes=[bx2])
            P.op("dve", lambda e: e.scalar_tensor_tensor(out=x2[:, 1], in0=Zi, scalar=-1.0, in1=Fi_, op0=ALU.mult, op1=ALU.mult), reads=rd2, writes=[bx2])
            P.op("dve", lambda e: e.tensor_tensor(out=x2[:, 2], in0=Zi, in1=Fr_, op=ALU.mult), reads=rd2, writes=[bx2])
            P.op("dve", lambda e: e.tensor_tensor(out=x2[:, 3], in0=Zr, in1=Fi_, op=ALU.mult), reads=rd2, writes=[bx2])

        def stage2(i):
            c, blk = divmod(i, 8)
            st, PX, bPX = pb_views(i)
            x2, bx2 = X2[st], b_X2[st]
            PS3 = PX[:, 1024:1152]
            n_mm = 0
            for a in range(4):
                for k4 in range(4):
                    j = (4 * blk + a) * 2 + (k4 // 2)
                    P.op("pe", lambda e: e.matmul(PS3, lhsT=Cbd[:, j * 128:(j + 1) * 128], rhs=x2[:, k4, a, :], start=(n_mm == 0), stop=(n_mm == 15)),
                         reads=[b_Cbd, bx2], writes=[bPX[2]])
                    n_mm += 1
            P.op("dve", lambda e: e.scalar_tensor_tensor(out=ypre[:, st, :], in0=zs[:, blk, 128 * c:128 * c + 128], scalar=Dcol[:, blk:blk + 1], in1=PS3,
                                                         op0=ALU.mult, op1=ALU.add), reads=[b_zs[blk], bPX[2], b_cols], writes=[b_ypre[st]])
            P.op("act", lambda e: e.activation(out=ys[:, blk, 128 * c:128 * c + 128], in_=ypre[:, st, :], func=AF.Gelu_apprx_tanh),
                 reads=[b_ypre[st]], writes=[b_ys[blk]])

        NI = 64
        for step in range(NI + 2):
            if step < NI:
                stage0(step)
            if 0 <= step - 1 < NI:
                stage1(step - 1)
            if 0 <= step - 2 < NI:
                stage2(step - 2)

        if tap("ys", ys, [128, 8, 1024], BF16, b_ys):
            return nc
        mixed = hn; b_mixed = b_hn
        glu_v = glu_d.rearrange("(k p) c -> p k c", p=128)
        P.barrier()
        gate = A(94, 512); b_gate = Buf("gate")
        for m in range(8):
            slab, b_slab = load_slab(glu_v[:, :, 128 * m:128 * m + 128], 8)
            for h in range(2):
                pb, bb = next_bank()
                for k in range(8):
                    P.op("pe", lambda e, pb=pb, k=k, h=h, slab=slab: e.matmul(pb, lhsT=slab[:, k, :], rhs=ys[:, k, 512 * h:512 * h + 512],
                                                                           start=(k == 0), stop=(k == 7)), reads=[b_slab, b_ys[k]], writes=[bb])
                P.op("act", lambda e, pb=pb, m=m: e.activation(out=gate, in_=pb, func=AF.Sigmoid, bias=glu_b[:, m:m + 1]),
                     reads=[bb, b_cols], writes=[b_gate])
                P.op("dve", lambda e, m=m, h=h: e.tensor_tensor(out=mixed[:, m, 512 * h:512 * h + 512], in0=ys[:, m, 512 * h:512 * h + 512],
                                                              in1=gate, op=ALU.mult), reads=[b_gate, b_ys[m]], writes=[b_mixed[m]])

        if tap("mixA", mixed[:, 0:8, :], [128, 8, 1024], BF16, b_mixed):
            return nc
        lnrows = A(84, 2048).rearrange("p (a f) -> p a f", a=2); b_lnrows = Buf("lnrows")
        for a_ in range(2):
            P.op("pool", lambda e, a_=a_: e.dma_start(out=lnrows[:, a_, :], in_=rows_d[a_:a_ + 1, :].partition_broadcast(128).rearrange("p o f -> p (o f)")),
                 writes=[b_lnrows], dma=True)
        wmT = A(68, 1024).rearrange("p (a t) -> p a t", a=8); b_wmT = Buf("wmT")
        wtmp = A(64, 1024, F32).rearrange("p (a t) -> p a t", a=8)
        b_wtmp = Buf("wtmp")
        P.op("sp", lambda e: e.dma_start(out=wtmp, in_=sguw_d[:, :, :]), writes=[b_wtmp], dma=True)
        P.op("dve", lambda e: e.tensor_tensor(out=wmT, in0=wtmp, in1=tri_f.unsqueeze(1).broadcast_to([128, 8, 128]), op=ALU.mult),
             reads=[b_wtmp, b_cst], writes=[b_wmT])
        bsrow = A(70, 1024)[0:1, :]; b_bsrow = Buf("bsrow")
        P.op("pool", lambda e: e.dma_start(out=bsrow, in_=sgub_d[:, :]), writes=[b_bsrow], dma=True)
        stats = sb("stats", [128, 2, 6], F32); b_stats = Buf("stats")
        mv_ = sb("mv", [128, 2], F32); b_mv = Buf("mv")
        for c in range(8):
            for h in range(2):
                P.op("dve", lambda e, c=c, h=h: e.bn_stats(out=stats[:, h, :], in_=zv[:, c, 512 * h:512 * h + 512]),
                     reads=[b_zv[c]], writes=[b_stats])
            P.op("dve", lambda e: e.bn_aggr(out=mv_[:], in_=stats[:].rearrange("p a s -> p (a s)")), reads=[b_stats], writes=[b_mv])
            P.op("act", lambda e: e.activation(out=mv_[:, 1:2], in_=mv_[:, 1:2], func=AF.Sqrt, bias=eps_c[:]), reads=[b_mv, b_eps], writes=[b_mv])
            P.op("dve", lambda e: e.reciprocal(out=mv_[:, 1:2], in_=mv_[:, 1:2]), reads=[b_mv], writes=[b_mv])
            P.op("dve", lambda e, c=c: e.tensor_scalar(out=zv[:, c, :], in0=zv[:, c, :], scalar1=mv_[:, 0:1], scalar2=mv_[:, 1:2],
                                                      op0=ALU.subtract, op1=ALU.mult), reads=[b_mv, b_zv[c]], writes=[b_zv[c]])
            P.op("pool", lambda e, c=c: e.tensor_tensor(out=zv[:, c, :], in0=zv[:, c, :], in1=lnrows[:, 0, :], op=ALU.mult),
                 reads=[b_lnrows, b_zv[c]], writes=[b_zv[c]])
            P.op("pool", lambda e, c=c: e.tensor_tensor(out=zv[:, c, :], in0=zv[:, c, :], in1=lnrows[:, 1, :], op=ALU.add),
                 reads=[b_lnrows, b_zv[c]], writes=[b_zv[c]])
        for hd in range(8):
            for cg in range(2):
                pb, bb = next_bank()
                for cj in range(4):
                    c = cg * 4 + cj
                    P.op("pe", lambda e, pb=pb, cj=cj, c=c, hd=hd: e.matmul(pb[:, 128 * cj:128 * cj + 128], lhsT=zv[:, c, 128 * hd:128 * hd + 128],
                                                                         rhs=wmT[:, hd, :], start=True, stop=False),
                         reads=[b_zv[c], b_wmT], writes=[bb])
                    P.op("pe", lambda e, pb=pb, cj=cj, hd=hd: e.matmul(pb[:, 128 * cj:128 * cj + 128], lhsT=ones_bf[0:1, :],
                                                                    rhs=bsrow[0:1, 128 * hd:128 * hd + 128], start=False, stop=True),
                         reads=[b_ones, b_bsrow], writes=[bb])
                P.op("dve", lambda e, pb=pb, hd=hd, cg=cg: e.tensor_tensor(out=mixed[:, 8 + hd, 512 * cg:512 * cg + 512], in0=pb,
                                                                        in1=zu[:, hd, 512 * cg:512 * cg + 512], op=ALU.mult),
                     reads=[bb, b_zu[hd]], writes=[b_mixed[8 + hd]])

        if tap("mixB", mixed[:, 8:16, :], [128, 8, 1024], BF16, b_mixed):
            return nc
        class V:
            def __init__(self, ap, off): self.ap, self.off = ap, off
            def __getitem__(self, idx): return self.ap[idx[0], idx[1] + self.off, idx[2]]
        P.barrier()
        rmsnorm(V(mixed, 0), b_mixed[0:8], g_oss, V(mixed, 0), b_mixed[0:8], 8, rstd, b_rstd, 1024.0)
        rmsnorm(V(mixed, 8), b_mixed[8:16], g_osg, V(mixed, 8), b_mixed[8:16], 8, rstd2, b_rstd2, 1024.0)

        if tap("mixN", mixed, [128, 16, 1024], BF16, b_mixed):
            return nc
        P.barrier()
        x1T = A(0, 16 * 1024, F32).rearrange("p (k t) -> p k t", k=16); b_x1 = [Buf("x1_%d" % k) for k in range(16)]
        w_out_v = w_out_d.rearrange("(k p) c -> p k c", p=128)
        for n in range(16):
            P.op("sp", lambda e, n=n: e.dma_start(out=x1T[:, n, :], in_=xT_d[n * 128:(n + 1) * 128, :]), writes=[b_x1[n]], dma=True)
        for n in range(16):
            slab, b_slab = load_slab(w_out_v[:, :, 128 * n:128 * n + 128])
            for h in range(2):
                pb, bb = next_bank()
                for k in range(16):
                    P.op("pe", lambda e, pb=pb, k=k, h=h, slab=slab: e.matmul(pb, lhsT=slab[:, k, :], rhs=mixed[:, k, 512 * h:512 * h + 512],
                                                                           start=(k == 0), stop=(k == 15)), reads=[b_slab, b_mixed[k]], writes=[bb])
                P.op("dve", lambda e, pb=pb, n=n, h=h: e.tensor_tensor(out=x1T[:, n, 512 * h:512 * h + 512], in0=pb,
                                                                     in1=x1T[:, n, 512 * h:512 * h + 512], op=ALU.add),
                     reads=[bb, b_x1[n]], writes=[b_x1[n]])

        if tap("x1", x1T, [128, 16, 1024], F32, b_x1):
            return nc
        hn2, b_hn2 = hn, b_hn
        rmsnorm(x1T, b_x1, g_mlp, hn2, b_hn2, 16, rstd, b_rstd, float(D))
        P.barrier()
        Hh = A(128, 64 * 512).rearrange("p (m t) -> p m t", m=64); b_H = [Buf("H%d" % m) for m in range(64)]
        usl = [A(64 + 4 * i, 2048).rearrange("p (k c) -> p k c", k=16) for i in range(2)]
        b_usl = [Buf("usl%d" % i) for i in range(2)]
        dsl = [A(72 + 8 * i, 4096).rearrange("p (k c) -> p k c", k=32) for i in range(2)]
        b_dsl = [Buf("dsl%d" % i) for i in range(2)]
        w_up_v = w_up_d.rearrange("(k p) c -> p k c", p=128)
        w_dn_v = w_dn_d.rearrange("(k p) c -> p k c", p=128)
        ui = [0]; di = [0]
        Hf = Hh.rearrange("p m t -> p (m t)").rearrange("p (m t) -> p m t", m=32)
        relu2 = [A(94, 512), A(95, 512)]; b_relu2 = [Buf("relu0"), Buf("relu1")]
        for hh in range(2):
            for mm in range(32):
                m = 32 * hh + mm
                i = ui[0] % 2; ui[0] += 1
                P.op("pool", lambda e, i=i, m=m: e.dma_start(out=usl[i], in_=w_up_v[:, :, 128 * m:128 * m + 128]), writes=[b_usl[i]], dma=True)
                for th_ in range(2):
                    ts = slice(512 * th_, 512 * th_ + 512)
                    pb, bb = next_bank()
                    for k in range(16):
                        P.op("pe", lambda e, pb=pb, k=k, i=i, ts=ts: e.matmul(pb, lhsT=usl[i][:, k, :], rhs=hn2[:, k, ts], start=(k == 0), stop=(k == 15)),
                             reads=[b_usl[i], b_hn2[k]], writes=[bb])
                    rt, b_rt = relu2[th_], b_relu2[th_]
                    P.op("act", lambda e, pb=pb, rt=rt: e.activation(out=rt, in_=pb, func=AF.Relu), reads=[bb], writes=[b_rt])
                    P.op("dve", lambda e, mm=mm, ts=ts, rt=rt: e.tensor_tensor(out=Hf[:, mm, ts], in0=rt, in1=rt, op=ALU.mult), reads=[b_rt], writes=[b_H[mm]])
            for n in range(16):
                i = di[0] % 2; di[0] += 1
                P.op("pool", lambda e, i=i, n=n, hh=hh: e.dma_start(out=dsl[i], in_=w_dn_v[:, 32 * hh:32 * hh + 32, 128 * n:128 * n + 128]),
                     writes=[b_dsl[i]], dma=True)
                for th_ in range(2):
                    ts = slice(512 * th_, 512 * th_ + 512)
                    pb, bb = next_bank()
                    for kk in range(32):
                        P.op("pe", lambda e, pb=pb, kk=kk, i=i, ts=ts: e.matmul(pb, lhsT=dsl[i][:, kk, :], rhs=Hf[:, kk, ts], start=(kk == 0), stop=(kk == 31)),
                             reads=[b_dsl[i], b_H[kk]], writes=[bb])
                    P.op("dve", lambda e, pb=pb, n=n, ts=ts: e.tensor_tensor(out=x1T[:, n, ts], in0=pb, in1=x1T[:, n, ts], op=ALU.add),
                         reads=[bb, b_x1[n]], writes=[b_x1[n]])

        if tap("x2", x1T, [128, 16, 1024], F32, b_x1):
            return nc
        P.barrier()
        rmsnorm(x1T, b_x1, g_fin, x1T, b_x1, 16, rstd, b_rstd, float(D))
        b_out = Buf("out")
        for n in range(16):
            P.op("sp", lambda e, n=n: e.dma_start(out=out_d[n * 128:(n + 1) * 128, :], in_=x1T[:, n, :]), reads=[b_x1[n]], dma=True)
        P.final_wait("sp", b_x1)
        with nc.Block() as block:
            P.emit(block)
    return nc


def _consts():
    c = np.zeros((128, 640), np.float32)
    c[:, 0:128] = np.eye(128, dtype=np.float32)
    s = np.arange(128)
    c[:, 128:256] = (s[:, None] <= s[None, :]).astype(np.float32)
    c[:, 256] = s.astype(np.float32)
    for gl in range(8):
        c[16 * gl:16 * gl + 16, 320 + gl] = 1.0
    return c


def _col(v):
    return np.ascontiguousarray(np.asarray(v, np.float32).reshape(-1, 128).T)


_NC_CACHE = {}


def _get_nc(mode):
    if mode not in _NC_CACHE:
        _NC_CACHE[mode] = build(mode)
    return _NC_CACHE[mode]


def make_in_maps(inp, mode, sall=None):
    f = lambda a: np.ascontiguousarray(np.asarray(a, np.float32))
    x = f(inp["x"])[0]
    a_re, a_im = f(inp["ssm_a_re"])[0], f(inp["ssm_a_im"])[0]
    ldt = f(inp["ssm_log_dt"])[0]
    Bre, Bim = f(inp["ssm_b_re"])[0], f(inp["ssm_b_im"])[0]
    Cre, Cim = f(inp["ssm_c_re"])[0], f(inp["ssm_c_im"])[0]
    ldt_rep = np.repeat(ldt[:, None], 64, axis=1)

    def blay(a_gp):
        t = a_gp.reshape(8, 8, 64)
        t = np.repeat(t[:, :, None, :], 16, axis=2)
        return np.ascontiguousarray(t.transpose(1, 2, 0, 3).reshape(128, 512))

    def blayB(b_gph):
        t = b_gph.reshape(8, 8, 64, 16)
        return np.ascontiguousarray(t.transpose(1, 3, 0, 2).reshape(128, 512))

    def clay(c_ghp):
        t = c_ghp.reshape(32, 2, 16, 64)
        return np.ascontiguousarray(t.transpose(1, 3, 0, 2).reshape(128, 512))

    def tlayp(a_gp):
        t = a_gp.reshape(32, 2, 64)
        t = np.repeat(t[:, :, :, None], 16, axis=3)
        return np.ascontiguousarray(t.transpose(1, 2, 0, 3).reshape(128, 512))

    def tlayB(b_gph):
        t = b_gph.reshape(32, 2, 64, 16)
        return np.ascontiguousarray(t.transpose(1, 2, 0, 3).reshape(128, 512))

    pT = np.ascontiguousarray(np.stack([tlayp(a_re), tlayp(a_im), tlayp(ldt_rep)], axis=1))
    BTl = np.ascontiguousarray(np.stack([tlayB(Bre), tlayB(Bim)], axis=1))
    pB = np.ascontiguousarray(np.stack([blay(a_re), blay(a_im), blay(ldt_rep)], axis=1))
    BT = np.ascontiguousarray(np.stack([blayB(Bre), blayB(Bim)], axis=1))
    pE = np.ascontiguousarray(np.stack([a_re.reshape(-1), a_im.reshape(-1), ldt_rep.reshape(-1)], axis=0))
    CT = np.ascontiguousarray(np.stack([clay(Cre), clay(Cim)], axis=1))
    cols = np.zeros((128, 96), np.float32)
    cols[:, 0:16] = _col(inp["norm_mix_g"][0]); cols[:, 16:32] = _col(inp["norm_mlp_g"][0]); cols[:, 32:48] = _col(inp["norm_final_g"])
    cols[:, 48:56] = _col(inp["out_norm_ssm_g"][0]); cols[:, 56:64] = _col(inp["out_norm_sgu_g"][0])
    cols[:, 64:72] = _col(inp["ssm_glu_b"][0]); cols[:, 72:80] = _col(f(inp["ssm_d"])[0].reshape(-1))
    common = {"w_in": f(inp["w_in"])[0], "cst": _consts(), "pB": pB, "BT": BT, "pE": pE, "CT": CT, "cols": cols, "pT": pT, "BTl": BTl}
    xTs = [np.ascontiguousarray(x[c * T:(c + 1) * T].T) for c in range(NC)]
    if mode == "A":
        return [dict(common, xT=xTs[c]) for c in range(NC)]
    rows = np.ascontiguousarray(np.stack([f(inp["sgu_ln_g"])[0], f(inp["sgu_ln_b"])[0]], axis=0))
    sguwT = np.ascontiguousarray(f(inp["sgu_w"])[0].transpose(2, 0, 1))
    sgub = np.ascontiguousarray(f(inp["sgu_b"])[0].reshape(1, 1024))
    commonB = dict(common, w_out=f(inp["w_out"])[0], w_up=f(inp["w_up"])[0], w_down=f(inp["w_down"])[0],
                   glu_w=f(inp["ssm_glu_w"])[0], rows=rows, sguwT=sguwT, sgub=sgub)
    zero = np.zeros((D, T), np.float32)
    in_maps = []
    for c in range(NC):
        xTp = np.ascontiguousarray(np.stack([xTs[c - 7 + j] if c - 7 + j >= 0 else zero for j in range(8)], axis=0))
        in_maps.append(dict(commonB, xT=xTs[c], xTp=xTp))
    return in_maps


def kernel(**inp):
    resB = run_bass_kernel_spmd(_get_nc("B"), make_in_maps(inp, "B"), core_ids=list(range(NC)))
    y = np.concatenate([resB.results[c]["yT"].T for c in range(NC)], axis=0)
    return np.ascontiguousarray(y.reshape(1, NC * T, D).astype(np.float32))
```
